# Optimizing a Trainium2 kernel written in Bass

```python
import math
import jax, jax.numpy as jnp
from jax import lax
import numpy as np

D_MODEL = 2048
BATCH = 4
SEQ = 4096
DEPTH = 2

N_MIXERS = 2
N_RET_LAYERS = (DEPTH + 1) // 2
N_MOBA_LAYERS = DEPTH // 2
RET_HEADS = 8
RET_QK_DIM = D_MODEL // RET_HEADS
RET_V_WIDTH = 2 * D_MODEL
RET_V_DIM = RET_V_WIDTH // RET_HEADS
RET_CHUNK = 128
ROPE_BASE = 10000.0
ATT_HEADS = 16
ATT_HEAD_DIM = D_MODEL // ATT_HEADS
MOBA_BLOCK = 256
MOBA_TOPK = 3
MOBA_Q_BLOCK = 16
REL_BUCKETS = 32
REL_MAX_DIST = 128
D_FF = 256 * ((8 * D_MODEL // 3 + 255) // 256)
CONV_WIDTH = 3
RMS_EPS = 1e-6
GN_EPS = 1e-5
NEG_INF = -1e30

kernel_name = "hybrid_retention_moba_convffn"


def rms_norm(x, g):
    xf = x.astype(jnp.float32)
    y = xf * lax.rsqrt(jnp.mean(xf * xf, axis=-1, keepdims=True) + RMS_EPS)
    return (y * g.astype(jnp.float32)).astype(x.dtype)


def rotary(x, pos):
    d = x.shape[-1]
    half = d // 2
    inv = ROPE_BASE ** (-jnp.arange(half, dtype=jnp.float32) / half)
    ang = pos.astype(jnp.float32)[:, None] * inv[None, :]
    cos = jnp.cos(ang)[None, :, None, :]
    sin = jnp.sin(ang)[None, :, None, :]
    xf = x.astype(jnp.float32)
    x1, x2 = xf[..., :half], xf[..., half:]
    return jnp.concatenate([x1 * cos - x2 * sin, x2 * cos + x1 * sin], axis=-1).astype(x.dtype)


def retention_mixer(h, w_in, gn_gain, w_out):
    B, S, _ = h.shape
    H, dk, dv, C = RET_HEADS, RET_QK_DIM, RET_V_DIM, RET_CHUNK
    nc = S // C
    proj = h @ w_in
    q, k, v, g = jnp.split(proj, [D_MODEL, 2 * D_MODEL, 2 * D_MODEL + RET_V_WIDTH], axis=-1)
    pos = jnp.arange(S)
    q = rotary(q.reshape(B, S, H, dk), pos)
    k = rotary(k.reshape(B, S, H, dk), pos) * (dk ** -0.5)
    v = v.reshape(B, S, H, dv)
    dt = q.dtype
    qc = q.reshape(B, nc, C, H, dk).transpose(0, 3, 1, 2, 4)
    kc = k.reshape(B, nc, C, H, dk).transpose(0, 3, 1, 2, 4)
    vc = v.reshape(B, nc, C, H, dv).transpose(0, 3, 1, 2, 4)
    log_gamma = jnp.log1p(-jnp.power(2.0, -5.0 - jnp.arange(H, dtype=jnp.float32)))
    idx = jnp.arange(C, dtype=jnp.float32)
    diff = idx[:, None] - idx[None, :]
    decay_in = jnp.where(diff >= 0, jnp.exp(log_gamma[:, None, None] * jnp.maximum(diff, 0.0)), 0.0).astype(dt)
    xi = jnp.exp(log_gamma[:, None] * (idx + 1.0)).astype(dt)
    zeta = jnp.exp(log_gamma[:, None] * (C - 1.0 - idx)).astype(dt)
    chunk_decay = jnp.exp(log_gamma * C).astype(dt)
    inner = jnp.einsum('bhncd,bhnmd->bhncm', qc, kc) * decay_in[None, :, None]
    inner_out = jnp.einsum('bhncm,bhnme->bhnce', inner, vc)

    def step(state, inp):
        q_i, k_i, v_i = inp
        cross = jnp.einsum('bhcd,bhde->bhce', q_i, state) * xi[None, :, :, None]
        state = state * chunk_decay[None, :, None, None] + jnp.einsum(
            'bhcd,bhce->bhde', k_i, v_i * zeta[None, :, :, None])
        return state, cross

    state0 = jnp.zeros((B, H, dk, dv), dt)
    _, cross = lax.scan(step, state0, (qc.transpose(2, 0, 1, 3, 4),
                                       kc.transpose(2, 0, 1, 3, 4),
                                       vc.transpose(2, 0, 1, 3, 4)))
    o = inner_out + cross.transpose(1, 2, 0, 3, 4)
    o = o.transpose(0, 2, 3, 1, 4).reshape(B, S, H, dv)
    of = o.astype(jnp.float32)
    mu = jnp.mean(of, axis=-1, keepdims=True)
    var = jnp.mean(jnp.square(of - mu), axis=-1, keepdims=True)
    on = ((of - mu) * lax.rsqrt(var + GN_EPS)).reshape(B, S, RET_V_WIDTH) * gn_gain.astype(jnp.float32)
    y = (jax.nn.silu(g.astype(jnp.float32)) * on).astype(h.dtype)
    return y @ w_out


def t5_bucket(rel):
    n = jnp.maximum(rel, 0)
    max_exact = REL_BUCKETS // 2
    nf = jnp.maximum(n, max_exact).astype(jnp.float32)
    large = max_exact + (jnp.log(nf / max_exact) / math.log(REL_MAX_DIST / max_exact)
                         * (REL_BUCKETS - max_exact)).astype(jnp.int32)
    large = jnp.minimum(large, REL_BUCKETS - 1)
    return jnp.where(n < max_exact, n, large)


def moba_mixer(h, w_qkv, w_out, rel_bias):
    B, S, _ = h.shape
    H, Dh, Bk, QB = ATT_HEADS, ATT_HEAD_DIM, MOBA_BLOCK, MOBA_Q_BLOCK
    nb = -(-S // Bk)
    pad = nb * Bk - S
    q, k, v = jnp.split(h @ w_qkv, 3, axis=-1)
    q = q.reshape(B, S, H, Dh).transpose(0, 2, 1, 3)
    k = k.reshape(B, S, H, Dh).transpose(0, 2, 1, 3)
    v = v.reshape(B, S, H, Dh).transpose(0, 2, 1, 3)
    kb = jnp.pad(k, ((0, 0), (0, 0), (0, pad), (0, 0))).reshape(B, H, nb, Bk, Dh)
    vb = jnp.pad(v, ((0, 0), (0, 0), (0, pad), (0, 0))).reshape(B, H, nb, Bk, Dh)
    k_mean = jnp.mean(kb.astype(jnp.float32), axis=3)
    gate = jnp.einsum('bhsd,bhnd->bhsn', q.astype(jnp.float32), k_mean)
    q_blk = jnp.arange(S) // Bk
    past = jnp.arange(nb)[None, :] < q_blk[:, None]
    gate = jnp.where(past[None, None], gate, NEG_INF)
    topk = min(MOBA_TOPK, nb)
    _, sel = lax.top_k(gate, topk)
    sel_ok = sel < q_blk[None, None, :, None]
    tbl = rel_bias.astype(jnp.float32).T
    scale = Dh ** -0.5
    b_idx = jnp.arange(B)[:, None, None]
    h_idx = jnp.arange(H)[None, :, None]
    key_off = jnp.arange(Bk)

    def one_block(c):
        q0 = c * QB
        q_pos = q0 + jnp.arange(QB)
        qc = lax.dynamic_slice_in_dim(q, q0, QB, axis=2)
        sel_c = lax.dynamic_slice_in_dim(sel, q0, QB, axis=2)
        ok_c = lax.dynamic_slice_in_dim(sel_ok, q0, QB, axis=2)
        own = q0 // Bk
        k_own = lax.dynamic_index_in_dim(kb, own, axis=2, keepdims=False)
        v_own = lax.dynamic_index_in_dim(vb, own, axis=2, keepdims=False)
        rel_own = q_pos[:, None] - (own * Bk + key_off)[None, :]
        l_own = (jnp.einsum('bhqd,bhkd->bhqk', qc, k_own).astype(jnp.float32) * scale
                 + tbl[:, t5_bucket(rel_own)][None])
        l_own = jnp.where(rel_own[None, None] >= 0, l_own, NEG_INF)
        flat = sel_c.reshape(B, H, QB * topk)
        k_sel = kb[b_idx, h_idx, flat].reshape(B, H, QB, topk * Bk, Dh)
        v_sel = vb[b_idx, h_idx, flat].reshape(B, H, QB, topk * Bk, Dh)
        sel_pos = (sel_c[..., None] * Bk + key_off).reshape(B, H, QB, topk * Bk)
        rel_sel = q_pos[None, None, :, None] - sel_pos
        l_sel = (jnp.einsum('bhqd,bhqkd->bhqk', qc, k_sel).astype(jnp.float32) * scale
                 + tbl[h_idx[..., None], t5_bucket(rel_sel)])
        ok = jnp.repeat(ok_c, Bk, axis=-1, total_repeat_length=topk * Bk)
        l_sel = jnp.where(ok, l_sel, NEG_INF)
        p = jax.nn.softmax(jnp.concatenate([l_own, l_sel], axis=-1), axis=-1).astype(v.dtype)
        return (jnp.einsum('bhqk,bhkd->bhqd', p[..., :Bk], v_own)
                + jnp.einsum('bhqk,bhqkd->bhqd', p[..., Bk:], v_sel))

    o = lax.map(one_block, jnp.arange(S // QB))
    o = o.transpose(1, 0, 3, 2, 4).reshape(B, S, H * Dh)
    return o @ w_out


def conv_ffn(h, w_up, conv_w, conv_b, w_down):
    u = h @ w_up
    ch = u.shape[-1]
    u = lax.conv_general_dilated(
        u, conv_w.astype(u.dtype)[:, None, :], window_strides=(1,),
        padding=[(CONV_WIDTH - 1, 0)], dimension_numbers=('NWC', 'WIO', 'NWC'),
        feature_group_count=ch) + conv_b
    gate, val = jnp.split(u, 2, axis=-1)
    return (jax.nn.silu(gate) * val) @ w_down


def setup_inputs(seed: int = 0) -> dict:
    key = jax.random.key(seed)
    ks = jax.random.split(key, 16)
    f32 = jnp.float32

    def w(k, shape, fan_in):
        return jax.random.normal(k, shape, f32) * (fan_in ** -0.5)

    def gain(k, shape):
        return 1.0 + 0.02 * jax.random.normal(k, shape, f32)

    return {
        "x": jax.random.normal(ks[0], (BATCH, SEQ, D_MODEL), f32),
        "mix_norm": gain(ks[1], (DEPTH, D_MODEL)),
        "ret_w_in": w(ks[2], (N_RET_LAYERS, D_MODEL, 2 * D_MODEL + 2 * RET_V_WIDTH), D_MODEL),
        "ret_gn": gain(ks[3], (N_RET_LAYERS, RET_V_WIDTH)),
        "ret_w_out": w(ks[4], (N_RET_LAYERS, RET_V_WIDTH, D_MODEL), RET_V_WIDTH),
        "moba_w_qkv": w(ks[5], (N_MOBA_LAYERS, D_MODEL, 3 * D_MODEL), D_MODEL),
        "moba_w_out": w(ks[6], (N_MOBA_LAYERS, D_MODEL, D_MODEL), D_MODEL),
        "rel_bias": 0.5 * jax.random.normal(ks[7], (REL_BUCKETS, ATT_HEADS), f32),
        "ffn_norm": gain(ks[8], (DEPTH, D_MODEL)),
        "ffn_w_up": w(ks[9], (DEPTH, D_MODEL, 2 * D_FF), D_MODEL),
        "ffn_conv_w": w(ks[10], (DEPTH, CONV_WIDTH, 2 * D_FF), CONV_WIDTH),
        "ffn_conv_b": 0.02 * jax.random.normal(ks[11], (DEPTH, 2 * D_FF), f32),
        "ffn_w_down": w(ks[12], (DEPTH, D_FF, D_MODEL), D_FF),
        "final_norm": gain(ks[13], (D_MODEL,)),
    }


def reference(x, mix_norm, ret_w_in, ret_gn, ret_w_out, moba_w_qkv, moba_w_out, rel_bias,
              ffn_norm, ffn_w_up, ffn_conv_w, ffn_conv_b, ffn_w_down, final_norm):
    h = x
    for i in range(DEPTH):
        hn = rms_norm(h, mix_norm[i])
        j = i // N_MIXERS
        if i % N_MIXERS == 0:
            h = h + retention_mixer(hn, ret_w_in[j], ret_gn[j], ret_w_out[j])
        else:
            h = h + moba_mixer(hn, moba_w_qkv[j], moba_w_out[j], rel_bias)
        h = h + conv_ffn(rms_norm(h, ffn_norm[i]), ffn_w_up[i], ffn_conv_w[i], ffn_conv_b[i], ffn_w_down[i])
    return rms_norm(h, final_norm)
```

```python
import math
import os
import contextlib
import numpy as np
import ml_dtypes
import concourse.bass as bass
import concourse.mybir as mybir
from concourse.bass_utils import run_bass_kernel_spmd


ENGS = ("pe", "act", "dve", "pool", "sp")
N_DMA_SEMS = 12
STRICT = True


class _Op:
    __slots__ = ("eng", "fn", "deps", "signal", "sem", "val", "is_dma", "idx")

    def __init__(self, eng, fn, is_dma):
        self.eng = eng
        self.fn = fn
        self.deps = []
        self.signal = False
        self.sem = None
        self.val = 0
        self.is_dma = is_dma


class Sched:
    def __init__(self, nc):
        self.nc = nc
        self.ops = {e: [] for e in ENGS}
        self.last_w = {}
        self.readers = {}
        self.dma_rr = {e: 0 for e in ENGS}
        self.dma_last = {}

    def _add(self, eng, fn, reads, writes, is_dma):
        op = _Op(eng, fn, is_dma)
        deps = []
        for r in reads:
            w = self.last_w.get(r)
            if w is not None:
                deps.append(w)
            if r.startswith("bk") or r.startswith("xb"):
                for rd in self.readers.get(r, ()):
                    if rd.eng != eng:
                        deps.append(rd)
        for r in writes:
            w = self.last_w.get(r)
            if w is not None and (w.eng != eng or w.is_dma or is_dma or (STRICT and eng != "pe")):
                deps.append(w)
            for rd in self.readers.get(r, ()):
                if rd.eng != eng or rd.is_dma or is_dma or STRICT:
                    deps.append(rd)
        if is_dma:
            slot = self.dma_rr[eng] % N_DMA_SEMS
            self.dma_rr[eng] += 1
            prev = self.dma_last.get((eng, slot))
            if prev is not None:
                deps.append(prev)
            self.dma_last[(eng, slot)] = op
            op.sem = (eng, slot)
            op.signal = True
        seen = set()
        for d in deps:
            if id(d) in seen or d is op:
                continue
            seen.add(id(d))
            d.signal = True
            op.deps.append(d)
        for r in reads:
            self.readers.setdefault(r, []).append(op)
        for r in writes:
            self.last_w[r] = op
            self.readers[r] = []
        self.ops[eng].append(op)
        return op

    def op(self, eng, fn, reads=(), writes=()):
        return self._add(eng, fn, tuple(reads), tuple(writes), False)

    def dma(self, eng, fn, reads=(), writes=()):
        return self._add(eng, fn, tuple(reads), tuple(writes), True)

    def emit(self, final_wait_ops=()):
        nc = self.nc

        with contextlib.ExitStack() as es:
            esem = {e: es.enter_context(nc.semaphore("s_" + e)) for e in ENGS if e != "sp"}
            dsem = {}
            for e in ENGS:
                if self.dma_rr[e] > 0:
                    for s in range(min(N_DMA_SEMS, self.dma_rr[e])):
                        dsem[(e, s)] = es.enter_context(nc.semaphore("d_%s_%d" % (e, s)))
            for e in ENGS:
                c = 0
                dc = {}
                for op in self.ops[e]:
                    if op.is_dma:
                        dc[op.sem] = dc.get(op.sem, 0) + 16
                        op.val = dc[op.sem]
                        op.sem = dsem[op.sem]
                    elif op.signal:
                        c += 1
                        op.val = c
                        op.sem = esem[e]
            finals = {}
            for op in final_wait_ops:
                finals.setdefault(op.eng, []).append(op)
            block = es.enter_context(nc.Block())

            def run(e, eng):
                waited = {}
                for op in self.ops[e]:
                    for d in op.deps:
                        k = id(d.sem)
                        if waited.get(k, 0) < d.val:
                            eng.wait_ge(d.sem, d.val)
                            waited[k] = d.val
                    inst = op.fn(eng)
                    if op.signal:
                        inst.then_inc(op.sem, 16 if op.is_dma else 1)
                for (qe, slot), op in self.dma_last.items():
                    if qe == e:
                        eng.wait_ge(op.sem, op.val)

            if self.ops["sp"] or finals.get("sp"):
                @block.sync
                def _(eng):
                    run("sp", eng)
            if self.ops["act"]:
                @block.scalar
                def _(eng):
                    run("act", eng)
            if self.ops["dve"]:
                @block.vector
                def _(eng):
                    run("dve", eng)
            if self.ops["pool"]:
                @block.gpsimd
                def _(eng):
                    run("pool", eng)
            if self.ops["pe"]:
                @block.tensor
                def _(eng):
                    run("pe", eng)


F32 = mybir.dt.float32
BF16 = mybir.dt.bfloat16
AF = mybir.ActivationFunctionType
ALU = mybir.AluOpType

D = 2048
DC = 16
SEQ = 4096
HPC_R = 4
RMS_EPS = 1e-6
GN_EPS = 1e-5


def norm_phase(nc, S, xT, gcol, scr, ones_t, eps_t, bank_a, ntile, tag, xt1, xo1, extra):
    A = nc.alloc_sbuf_tensor
    xt = [xt1, xt1]
    xo = [xo1, xo1]
    sq = [A("%s_sq%d" % (tag, i), [128, 512], F32) for i in range(2)]
    rs = A("%s_rs" % tag, [128, 512], F32)

    def tile(t):
        b = 0
        S.dma("sp", lambda e: e.dma_start(
            out=xt[b][:, :, :], in_=xT[:, t * 512:(t + 1) * 512].rearrange("(c p) t -> p c t", p=128)),
            writes=["%s_xt%d" % (tag, b)] + extra)
        for kc in range(DC):
            sb = kc % 2
            S.op("act", lambda e, kc=kc, sb=sb: e.activation(out=sq[sb][:, :], in_=xt[b][:, kc, :], func=AF.Square),
                 reads=["%s_xt%d" % (tag, b)], writes=["%s_sq%d" % (tag, sb)])
            S.op("pe", lambda e, kc=kc, sb=sb: e.matmul(bank_a[0][:, :], lhsT=ones_t[:, :], rhs=sq[sb][:, :],
                                                          start=(kc == 0), stop=(kc == DC - 1)),
                 reads=["ones_t", "%s_sq%d" % (tag, sb)], writes=[bank_a[1]])
        S.op("act", lambda e: e.activation(out=rs[:, :], in_=bank_a[0][:, :], func=AF.Sqrt, bias=eps_t[:, 0:1], scale=1.0 / D),
             reads=[bank_a[1], "eps_t"], writes=["%s_rs" % tag])
        S.op("dve", lambda e: e.reciprocal(out=rs[:, :], in_=rs[:, :]), reads=["%s_rs" % tag], writes=["%s_rs" % tag])
        for kc in range(DC):
            S.op("dve", lambda e, kc=kc: e.scalar_tensor_tensor(
                out=xo[b][:, kc, :], in0=xt[b][:, kc, :], scalar=gcol[:, kc:kc + 1], in1=rs[:, :],
                op0=ALU.mult, op1=ALU.mult),
                reads=["%s_xt%d" % (tag, b), "%s_rs" % tag, "gcol"], writes=["%s_xo%d" % (tag, b)] + extra)
        S.dma("sp", lambda e: e.dma_start(
            out=scr[:, t * 512:(t + 1) * 512].rearrange("(c p) t -> p c t", p=128), in_=xo[b][:, :, :]),
            reads=["%s_xo%d" % (tag, b)] + extra, writes=["scr:%d" % t])
    for t in range(ntile):
        tile(t)


def build_ret(nheads=HPC_R, ntile=SEQ // 512, debug=False, stage=9):
    nc = bass.Bass("TRN2", target_bir_lowering=False)
    T = ntile * 512
    xT = nc.dram_tensor("xT", [D, T], F32, kind="ExternalInput").ap()
    w_in = nc.dram_tensor("w_in", [HPC_R, D, 1536], F32, kind="ExternalInput").ap()
    gmix = nc.dram_tensor("gmix", [128, DC], F32, kind="ExternalInput").ap()
    cosd = nc.dram_tensor("cosT", [128, SEQ], F32, kind="ExternalInput").ap()
    sind = nc.dram_tensor("sinT", [128, SEQ], F32, kind="ExternalInput").ap()
    decd = nc.dram_tensor("dec", [128, HPC_R * 128], F32, kind="ExternalInput").ap()
    xid = nc.dram_tensor("xi", [128, HPC_R * 512], F32, kind="ExternalInput").ap()
    zetad = nc.dram_tensor("zeta", [128, HPC_R], F32, kind="ExternalInput").ap()
    cdd = nc.dram_tensor("cd", [128, HPC_R], F32, kind="ExternalInput").ap()
    gnd = nc.dram_tensor("gnrep", [128, HPC_R * 512], F32, kind="ExternalInput").ap()
    onesd = nc.dram_tensor("ones", [128, 128], F32, kind="ExternalInput").ap()
    identd = nc.dram_tensor("identb", [128, 128], BF16, kind="ExternalInput").ap()
    y = nc.dram_tensor("y", [T, HPC_R * 512], BF16, kind="ExternalOutput").ap()
    scr = nc.dram_tensor("xn_scr", [D, T], BF16).ap()

    S = Sched(nc)
    A = nc.alloc_sbuf_tensor
    bank = [nc.alloc_psum_tensor("bank%d" % i, [128, 512], F32) for i in range(7)]
    xbank = nc.alloc_psum_tensor("xbank", [128, 1024], BF16)
    xbank_f = xbank[:, 512:1024].bitcast(F32)
    ones_t = A("ones_t", [128, 128], F32)
    ident = A("ident", [128, 128], BF16)
    eps_t = A("eps_t", [128, 1], F32)
    geps_t = A("geps_t", [128, 1], F32)
    gcol = A("gcol", [128, DC], F32)
    cs_t = [A("cs_t%d" % i, [128, 2, 512], F32) for i in range(2)]
    dec_t = A("dec_t", [128, HPC_R * 128], F32)
    xi_t = A("xi_t", [128, HPC_R * 512], F32)
    zeta_t = A("zeta_t", [128, HPC_R], F32)
    cd_t = A("cd_t", [128, HPC_R], F32)
    gn_t = A("gn_t", [128, HPC_R * 512], F32)
    for (dst, src, nm) in [(ones_t, onesd, "ones_t"), (ident, identd, "ident"), (gcol, gmix, "gcol"),
                           (dec_t, decd, "dec_t"),
                           (xi_t, xid, "xi_t"), (zeta_t, zetad, "zeta_t"), (cd_t, cdd, "cd_t"), (gn_t, gnd, "gn_t")]:
        S.dma("sp", lambda e, dst=dst, src=src: e.dma_start(out=dst[:], in_=src), writes=[nm])
    S.op("dve", lambda e: e.memset(eps_t[:], RMS_EPS), writes=["eps_t"])
    S.op("dve", lambda e: e.memset(geps_t[:], GN_EPS), writes=["geps_t"])

    wb = [A("wb%d" % i, [128, DC, 1536], BF16) for i in range(2)]
    wflat = wb[1][:, :, :].rearrange("p c n -> p (c n)")
    xt1 = wflat[:, 0:DC * 512 * 2].bitcast(F32).rearrange("p (c t) -> p c t", c=DC)
    xo1 = wflat[:, DC * 512 * 2:DC * 512 * 3].rearrange("p (c t) -> p c t", c=DC)
    norm_phase(nc, S, xT, gcol, scr, ones_t, eps_t, (bank[0], "bk0"), ntile, "np", xt1, xo1,
               ["wb1:0", "wb1:512", "wb1:1024"])

    xn = [A("xn%d" % i, [128, DC, 512], BF16) for i in range(2)]
    t1 = A("t1", [128, 512], F32)
    t2 = A("t2", [128, 512], F32)
    qr = A("qr", [128, 2, 512], BF16)
    qx = A("qx", [128, 2, 512], BF16)
    kr = A("kr", [128, 2, 512], BF16)
    ktok = A("ktok", [128, 256], BF16)
    vb = A("vb", [128, 512], BF16)
    vz = A("vz", [128, 512], BF16)
    sgt = A("sgt", [128, 512], F32)
    stm = A("stm", [128, 128], BF16)
    state = A("state", [128, 2, 512], F32)
    state_b = A("state_b", [128, 2, 512], BF16)
    bst = A("bst", [128, 6], F32)
    mv = A("mv", [128, 2], F32)
    rstd = A("rstd", [128, 1], F32)
    on = A("on", [128, 512], F32)
    yb = [A("yb%d" % i, [128, 512], BF16) for i in range(2)]
    ycnt = [0]
    xcnt = [0]

    def load_w(hd):
        b = hd % 2
        for (lo, hi) in [(0, 512), (512, 1024), (1024, 1536)]:
            S.dma("pool", lambda e, lo=lo, hi=hi: e.dma_start(
                out=wb[b][:, :, lo:hi], in_=w_in[hd, :, lo:hi].rearrange("(c p) n -> p c n", p=128)),
                writes=["wb%d:%d" % (b, lo)])

    def load_xn(t):
        b = xcnt[0] % 2
        xcnt[0] += 1
        S.dma("sp", lambda e: e.dma_start(
            out=xn[b][:, :, :], in_=scr[:, t * 512:(t + 1) * 512].rearrange("(c p) t -> p c t", p=128)),
            reads=["scr:%d" % t], writes=["xn%d" % b])
        return b

    def rotary(src, dst, csb, nm):
        (pa, na), (pb, nb) = src
        cs = cs_t[csb][:, 0, :]
        sn = cs_t[csb][:, 1, :]
        cos_n = sin_n = "cs_t%d" % csb
        S.op("dve", lambda e: e.tensor_tensor(out=t1[:, :], in0=pa[:, :], in1=cs, op=ALU.mult), reads=[na, cos_n], writes=["t1"])
        S.op("dve", lambda e: e.tensor_tensor(out=t2[:, :], in0=pb[:, :], in1=sn, op=ALU.mult), reads=[nb, sin_n], writes=["t2"])
        S.op("pool", lambda e: e.tensor_tensor(out=dst[:, 0, :], in0=t1[:, :], in1=t2[:, :], op=ALU.subtract),
             reads=["t1", "t2"], writes=[nm + "0"])
        S.op("dve", lambda e: e.tensor_tensor(out=t1[:, :], in0=pb[:, :], in1=cs, op=ALU.mult), reads=[nb, cos_n], writes=["t1"])
        S.op("dve", lambda e: e.tensor_tensor(out=t2[:, :], in0=pa[:, :], in1=sn, op=ALU.mult), reads=[na, sin_n], writes=["t2"])
        S.op("pool", lambda e: e.tensor_tensor(out=dst[:, 1, :], in0=t1[:, :], in1=t2[:, :], op=ALU.add),
             reads=["t1", "t2"], writes=[nm + "1"])

    def head_tile(hd, t, wbi, xb):
        w = wb[wbi]
        wn = ["wb%d:%d" % (wbi, lo) for lo in (0, 512, 1024)]
        xnn = "xn%d" % xb
        x_ = xn[xb]
        csb = xb
        S.dma("sp", lambda e: e.dma_start(out=cs_t[csb][:, 0, :], in_=cosd[:, t * 512:(t + 1) * 512]), writes=["cs_t%d" % csb])
        S.dma("sp", lambda e: e.dma_start(out=cs_t[csb][:, 1, :], in_=sind[:, t * 512:(t + 1) * 512]), writes=["cs_t%d" % csb])
        for (col, dst, nm) in [(0, qr, "qr"), (256, kr, "kr")]:
            for dch in range(2):
                for kc in range(DC):
                    S.op("pe", lambda e, dch=dch, kc=kc, col=col: e.matmul(
                        bank[dch][:, :], lhsT=w[:, kc, col + dch * 128:col + (dch + 1) * 128], rhs=x_[:, kc, :],
                        start=(kc == 0), stop=(kc == DC - 1)),
                        reads=[wn[0], xnn], writes=["bk%d" % dch])
            rotary(((bank[0], "bk0"), (bank[1], "bk1")), dst, csb, nm)
        if stage < 2:
            return
        for dch in range(2):
            S.op("dve", lambda e, dch=dch: e.tensor_tensor(
                out=qx[:, dch, :], in0=qr[:, dch, :], in1=xi_t[:, hd * 512:(hd + 1) * 512], op=ALU.mult),
                reads=["qr%d" % dch, "xi_t"], writes=["qx%d" % dch])
        def chunk(c):
            cs = slice(c * 128, (c + 1) * 128)
            first = (t == 0 and c == 0)
            for kc in range(DC):
                S.op("pe", lambda e, kc=kc: e.matmul(bank[2][:, :], lhsT=x_[:, kc, cs], rhs=w[:, kc, 512:1024],
                                                      start=(kc == 0), stop=(kc == DC - 1)),
                     reads=[wn[1], xnn], writes=["bk2"])
            S.op("act", lambda e: e.activation(out=vb[:, :], in_=bank[2][:, :], func=AF.Copy), reads=["bk2"], writes=["vb"])
            if stage < 2.2:
                return
            S.op("act", lambda e: e.activation(out=vz[:, :], in_=bank[2][:, :], func=AF.Copy, scale=zeta_t[:, hd:hd + 1]),
                 reads=["bk2", "zeta_t"], writes=["vz"])
            if stage < 2.3:
                return
            for kc in range(DC):
                S.op("pe", lambda e, kc=kc: e.matmul(bank[3][:, :], lhsT=x_[:, kc, cs], rhs=w[:, kc, 1024:1536],
                                                      start=(kc == 0), stop=(kc == DC - 1)),
                     reads=[wn[2], xnn], writes=["bk3"])
            S.op("act", lambda e: e.activation(out=sgt[:, :], in_=bank[3][:, :], func=AF.Silu), reads=["bk3"], writes=["sgt"])
            if stage < 2.4:
                return
            S.op("dve", lambda e: e.tensor_tensor(out=sgt[:, :], in0=sgt[:, :], in1=gn_t[:, hd * 512:(hd + 1) * 512], op=ALU.mult),
                 reads=["sgt", "gn_t"], writes=["sgt"])
            if stage < 3:
                return
            for dch in range(2):
                S.op("pe", lambda e, dch=dch: e.matmul(bank[0][:, 0:128], lhsT=kr[:, dch, cs], rhs=qr[:, dch, cs],
                                                        start=(dch == 0), stop=(dch == 1)),
                     reads=["kr%d" % dch, "qr%d" % dch], writes=["bk0"])
            S.op("dve", lambda e: e.tensor_tensor(out=stm[:, :], in0=bank[0][:, 0:128], in1=dec_t[:, hd * 128:(hd + 1) * 128], op=ALU.mult),
                 reads=["bk0", "dec_t"], writes=["stm"])
            if stage < 4:
                return
            for dch in range(2):
                S.op("pe", lambda e, dch=dch: e.transpose(out=xbank[:, dch * 128:(dch + 1) * 128], in_=kr[:, dch, cs], identity=ident[:, :]),
                     reads=["kr%d" % dch, "ident"], writes=["xb"])
            S.op("act", lambda e: e.activation(out=ktok[:, :], in_=xbank[:, 0:256], func=AF.Copy), reads=["xb"], writes=["ktok"])
            if stage < 5:
                return
            S.op("pe", lambda e: e.matmul(bank[4][:, :], lhsT=stm[:, :], rhs=vb[:, :], start=True, stop=first),
                 reads=["stm", "vb"], writes=["bk4"])
            if not first:
                for dch in range(2):
                    S.op("pe", lambda e, dch=dch: e.matmul(bank[4][:, :], lhsT=qx[:, dch, cs], rhs=state_b[:, dch, :],
                                                            start=False, stop=(dch == 1)),
                         reads=["qx%d" % dch, "state_b%d" % dch], writes=["bk4"])
            for dch in range(2):
                S.op("pe", lambda e, dch=dch: e.matmul(bank[5 + dch][:, :], lhsT=ktok[:, dch * 128:(dch + 1) * 128], rhs=vz[:, :],
                                                        start=True, stop=True),
                     reads=["ktok", "vz"], writes=["bk%d" % (5 + dch)])
                if first:
                    S.op("dve", lambda e, dch=dch: e.tensor_copy(out=state[:, dch, :], in_=bank[5 + dch][:, :]),
                         reads=["bk%d" % (5 + dch)], writes=["state%d" % dch])
                else:
                    S.op("dve", lambda e, dch=dch: e.scalar_tensor_tensor(
                        out=state[:, dch, :], in0=state[:, dch, :], scalar=cd_t[:, hd:hd + 1], in1=bank[5 + dch][:, :],
                        op0=ALU.mult, op1=ALU.add),
                        reads=["state%d" % dch, "bk%d" % (5 + dch), "cd_t"], writes=["state%d" % dch])
                S.op("act", lambda e, dch=dch: e.activation(out=state_b[:, dch, :], in_=state[:, dch, :], func=AF.Copy),
                     reads=["state%d" % dch], writes=["state_b%d" % dch])
            if stage < 6:
                return
            S.op("dve", lambda e: e.bn_stats(out=bst[:, :], in_=bank[4][:, :]), reads=["bk4"], writes=["bst"])
            S.op("dve", lambda e: e.bn_aggr(out=mv[:, :], in_=bst[:, :]), reads=["bst"], writes=["mv"])
            S.op("act", lambda e: e.activation(out=rstd[:, :], in_=mv[:, 1:2], func=AF.Sqrt, bias=geps_t[:, 0:1], scale=1.0),
                 reads=["mv", "geps_t"], writes=["rstd"])
            S.op("dve", lambda e: e.reciprocal(out=rstd[:, :], in_=rstd[:, :]), reads=["rstd"], writes=["rstd"])
            S.op("dve", lambda e: e.tensor_scalar(out=on[:, :], in0=bank[4][:, :], scalar1=mv[:, 0:1], scalar2=rstd[:, 0:1],
                                                  op0=ALU.subtract, op1=ALU.mult),
                 reads=["bk4", "mv", "rstd"], writes=["on"])
            ybi = ycnt[0] % 2
            ycnt[0] += 1
            S.op("dve", lambda e: e.tensor_tensor(out=yb[ybi][:, :], in0=on[:, :], in1=sgt[:, :], op=ALU.mult),
                 reads=["on", "sgt"], writes=["yb%d" % ybi])
            r0 = t * 512 + c * 128
            S.dma("sp", lambda e: e.dma_start(out=y[r0:r0 + 128, hd * 512:(hd + 1) * 512], in_=yb[ybi][:, :]),
                  reads=["yb%d" % ybi])
        for c in range(4):
            chunk(c)

    load_w(0)
    for hd in range(nheads if stage > 0 else 0):
        if hd + 1 < nheads:
            load_w(hd + 1)
        xb = load_xn(0)
        for t in range(ntile):
            nxb = load_xn(t + 1) if t + 1 < ntile else None
            head_tile(hd, t, hd % 2, xb)
            xb = nxb
    S.emit()
    return nc


def ret_consts(hh):
    half = 128
    inv = 10000.0 ** (-np.arange(half, dtype=np.float32) / half)
    pos = np.arange(SEQ, dtype=np.float32)
    ang = (pos[None, :] * inv[:, None]).astype(np.float32)
    cosT = np.cos(ang).astype(np.float32)
    sinT = np.sin(ang).astype(np.float32)
    dec = np.zeros((128, HPC_R * 128), np.float32)
    xi = np.zeros((128, HPC_R * 512), np.float32)
    zeta = np.zeros((128, HPC_R), np.float32)
    cd = np.zeros((128, HPC_R), np.float32)
    idx = np.arange(128, dtype=np.float64)
    for i in range(HPC_R):
        h = hh * HPC_R + i
        lg = np.log1p(-2.0 ** (-5.0 - h))
        diff = idx[None, :] - idx[:, None]
        dT = np.where(diff >= 0, np.exp(lg * np.maximum(diff, 0.0)), 0.0) / 16.0
        dec[:, i * 128:(i + 1) * 128] = dT
        xi[:, i * 512:(i + 1) * 512] = np.tile(np.exp(lg * (idx + 1.0)), 4)[None, :]
        zeta[:, i] = np.exp(lg * (127.0 - idx)) / 16.0
        cd[:, i] = np.exp(lg * 128.0)
    return dict(cosT=cosT, sinT=sinT, dec=dec, xi=xi, zeta=zeta, cd=cd)


F32 = mybir.dt.float32
BF16 = mybir.dt.bfloat16
AF = mybir.ActivationFunctionType
ALU = mybir.AluOpType

D = 2048
DC = 16
FF = 5632
FC = 44
NTOK = 2048
MT = 1024
RMS_EPS = 1e-6


def build_ffn(VK, final, nmacro=NTOK // MT, debug=False):
    VKC = VK // 128
    NT = NTOK + 2
    nc = bass.Bass("TRN2", target_bir_lowering=False)
    yT = nc.dram_tensor("yT", [VK, NT], BF16, kind="ExternalInput").ap()
    hT = nc.dram_tensor("hT", [D, NT], F32, kind="ExternalInput").ap()
    w_out = nc.dram_tensor("w_out", [VK, D], F32, kind="ExternalInput").ap()
    w_up = nc.dram_tensor("w_up", [D, 2 * FF], F32, kind="ExternalInput").ap()
    w_down = nc.dram_tensor("w_down", [FF, D], F32, kind="ExternalInput").ap()
    gn = nc.dram_tensor("gn", [128, DC], F32, kind="ExternalInput").ap()
    gf = nc.dram_tensor("gf", [128, DC], F32, kind="ExternalInput").ap()
    cw = nc.dram_tensor("cw", [128, 2 * FC * 3], F32, kind="ExternalInput").ap()
    cb = nc.dram_tensor("cb", [128, 2 * FC], F32, kind="ExternalInput").ap()
    onesd = nc.dram_tensor("ones", [128, 128], F32, kind="ExternalInput").ap()
    oT = nc.dram_tensor("oT", [D, NTOK], F32, kind="ExternalOutput").ap()
    hp_d = nc.dram_tensor("hp_scratch", [D, nmacro * 1026], F32, **({"kind": "ExternalOutput"} if debug else {})).ap()
    if debug:
        hn_o = nc.dram_tensor("hn_o", [128, DC, 1026], BF16, kind="ExternalOutput").ap()
        aT_o = nc.dram_tensor("aT_o", [128, FC, MT], BF16, kind="ExternalOutput").ap()

    S = Sched(nc)
    A = nc.alloc_sbuf_tensor
    big = A("big", [128, FC * MT], BF16)
    aT = big[:, :].rearrange("p (c t) -> p c t", c=FC)
    yt = big[:, 0:VKC * 514].rearrange("p (c t) -> p c t", c=VKC)
    hp_off = 32 * 514
    hp = big[:, hp_off:hp_off + DC * 514 * 2].bitcast(F32).rearrange("p (c t) -> p c t", c=DC)
    hn = A("hn", [128, DC, 1026], BF16)
    wbuf = [A("wd%d" % i, [128, FC * 128], BF16) for i in range(2)]
    wub = [A("wu%d" % i, [128, 2, DC, 128], BF16) for i in range(2)]
    urow = [A("urow%d" % i, [128, 1026], F32) for i in range(2)]
    acc = [A("acc%d" % i, [128, 1024], F32) for i in range(2)]
    sg = A("sg", [128, 1024], F32)
    hin = [A("hin%d" % i, [128, 514], F32) for i in range(2)]
    sq = [A("sq%d" % i, [128, 514], F32) for i in range(2)]
    rstd = A("rstd", [128, 514], F32)
    obuf = [A("obuf%d" % i, [128, 512], F32) for i in range(2)]
    gn_t = A("gn_t", [128, DC], F32)
    gf_t = A("gf_t", [128, DC], F32)
    cw_t = A("cw_t", [128, 2 * FC * 3], F32)
    cb_t = A("cb_t", [128, 2 * FC], F32)
    ones_t = A("ones_t", [128, 128], F32)
    eps_t = A("eps_t", [128, 1], F32)
    bank = [nc.alloc_psum_tensor("bank%d" % i, [128, 512], F32) for i in range(8)]

    S.dma("sp", lambda e: e.dma_start(out=gn_t[:], in_=gn), writes=["gn_t"])
    S.dma("sp", lambda e: e.dma_start(out=gf_t[:], in_=gf), writes=["gf_t"])
    S.dma("sp", lambda e: e.dma_start(out=cw_t[:], in_=cw), writes=["cw_t"])
    S.dma("sp", lambda e: e.dma_start(out=cb_t[:], in_=cb), writes=["cb_t"])
    S.dma("sp", lambda e: e.dma_start(out=ones_t[:], in_=onesd), writes=["ones_t"])
    S.op("dve", lambda e: e.memset(eps_t[:], RMS_EPS), writes=["eps_t"])

    AT_ALL = ["aT%d" % j for j in range(FC)]
    out_ops = []
    wslot = [0]

    def rms_finish(ps_list):
        for (p, lo, hi, bi) in ps_list:
            S.op("act", lambda e, p=p, lo=lo, hi=hi: e.activation(
                out=rstd[:, lo:hi], in_=p, func=AF.Sqrt, bias=eps_t[:, 0:1], scale=1.0 / D),
                reads=["bk%d" % bi, "eps_t"], writes=["rstd:%d" % lo])
            S.op("dve", lambda e, lo=lo, hi=hi: e.reciprocal(out=rstd[:, lo:hi], in_=rstd[:, lo:hi]),
                 reads=["rstd:%d" % lo], writes=["rstd:%d" % lo])

    def macro(m):
        c0 = m * MT
        for s in range(2):
            lo, hi = (0, 514) if s == 0 else (514, 1026)
            W = hi - lo
            ntiles = [(0, 2, bank[1], 1), (2, 514, bank[0], 0)] if s == 0 else [(0, 512, bank[0], 0)]
            stiles = [(0, 2, bank[3], 3), (2, 514, bank[2], 2)] if s == 0 else [(0, 512, bank[2], 2)]
            S.dma("sp", lambda e, lo=lo, hi=hi, W=W: e.dma_start(
                out=yt[:, :, 0:W], in_=yT[:, c0 + lo:c0 + hi].rearrange("(c p) t -> p c t", p=128)),
                writes=["yt"] + AT_ALL)
            for dc in range(DC):
                wb = wslot[0] % 2
                wslot[0] += 1
                wv = wbuf[wb][:, 0:VKC * 128].rearrange("p (c n) -> p c n", c=VKC)
                S.dma("pool", lambda e, wv=wv, dc=dc: e.dma_start(
                    out=wv, in_=w_out[:, dc * 128:(dc + 1) * 128].rearrange("(c p) n -> p c n", p=128)),
                    writes=["wd%d" % wb])
                hb = dc % 2
                S.dma("sp", lambda e, hb=hb, dc=dc, lo=lo, hi=hi, W=W: e.dma_start(
                    out=hin[hb][:, 0:W], in_=hT[dc * 128:(dc + 1) * 128, c0 + lo:c0 + hi]),
                    writes=["hin%d" % hb])
                for (a, b, pb, bi) in ntiles:
                    for vc in range(VKC):
                        S.op("pe", lambda e, a=a, b=b, pb=pb, vc=vc, wv=wv: e.matmul(
                            pb[:, 0:b - a], lhsT=wv[:, vc, :], rhs=yt[:, vc, a:b],
                            start=(vc == 0), stop=(vc == VKC - 1)),
                            reads=["wd%d" % wb, "yt"], writes=["bk%d" % bi])
                    S.op("dve", lambda e, a=a, b=b, pb=pb, dc=dc, hb=hb: e.tensor_tensor(
                        out=hp[:, dc, a:b], in0=pb[:, 0:b - a], in1=hin[hb][:, a:b], op=ALU.add),
                        reads=["bk%d" % bi, "hin%d" % hb], writes=["hp%d" % dc] + AT_ALL)
                S.op("act", lambda e, dc=dc, hb=hb, W=W: e.activation(
                    out=sq[hb][:, 0:W], in_=hp[:, dc, 0:W], func=AF.Square),
                    reads=["hp%d" % dc], writes=["sq%d" % hb])
                for (a, b, pb, bi) in stiles:
                    S.op("pe", lambda e, a=a, b=b, pb=pb, dc=dc, hb=hb: e.matmul(
                        pb[:, 0:b - a], lhsT=ones_t[:, :], rhs=sq[hb][:, a:b],
                        start=(dc == 0), stop=(dc == DC - 1)),
                        reads=["ones_t", "sq%d" % hb], writes=["bk%d" % bi])
            rms_finish([(pb[:, 0:b - a], a, b, bi) for (a, b, pb, bi) in stiles])
            for kc in range(DC):
                S.op("dve", lambda e, kc=kc, lo=lo, hi=hi, W=W: e.scalar_tensor_tensor(
                    out=hn[:, kc, lo:hi], in0=hp[:, kc, 0:W], scalar=gn_t[:, kc:kc + 1], in1=rstd[:, 0:W],
                    op0=ALU.mult, op1=ALU.mult),
                    reads=["hp%d" % kc, "gn_t"] + ["rstd:%d" % a for (a, b, pb, bi) in stiles],
                    writes=["hn%d:%d" % (kc, s)])
            S.dma("sp", lambda e, lo=lo, hi=hi, W=W: e.dma_start(
                out=hp_d[:, m * 1026 + lo:m * 1026 + hi].rearrange("(c p) t -> p c t", p=128),
                in_=hp[:, :, 0:W]),
                reads=["hp%d" % k for k in range(DC)], writes=["hpd:%d:%d" % (m, s)])

        def load_wu(j):
            ub = j % 2
            for gv in range(2):
                col = gv * FF + j * 128
                S.dma("pool", lambda e, ub=ub, gv=gv, col=col: e.dma_start(
                    out=wub[ub][:, gv, :, :], in_=w_up[:, col:col + 128].rearrange("(c p) n -> p c n", p=128)),
                    writes=["wu%d:%d" % (ub, gv)])
        load_wu(0)
        HN_ALL = ["hn%d:%d" % (k, s) for k in range(DC) for s in range(2)]
        for j in range(FC):
            ub = j % 2
            if j + 1 < FC:
                load_wu(j + 1)
            for gv in range(2):
                main = (bank[4], bank[5]) if gv == 0 else (bank[6], bank[7])
                hcol = 2 * gv
                tiles = [(0, 2, bank[1][:, hcol:hcol + 2], "bk1"),
                         (2, 514, main[0][:, :], "bk%d" % (4 + 2 * gv)),
                         (514, 1026, main[1][:, :], "bk%d" % (5 + 2 * gv))]
                for (a, b, pap, pn) in tiles:
                    for kc in range(DC):
                        S.op("pe", lambda e, a=a, b=b, pap=pap, kc=kc, ub=ub, gv=gv: e.matmul(
                            pap, lhsT=wub[ub][:, gv, kc, :], rhs=hn[:, kc, a:b],
                            start=(kc == 0), stop=(kc == DC - 1)),
                            reads=["wu%d:%d" % (ub, gv)] + HN_ALL, writes=[pn])
                ur = urow[gv]
                for (a, b, pap, pn) in tiles:
                    S.op("act", lambda e, a=a, b=b, pap=pap, ur=ur: e.activation(
                        out=ur[:, a:b], in_=pap, func=AF.Copy),
                        reads=[pn], writes=["urow%d" % gv])
                ch = gv * FC + j
                ac = acc[gv]
                S.op("dve", lambda e, ur=ur, ac=ac, ch=ch: e.tensor_scalar(
                    out=ac[:, :], in0=ur[:, 2:1026], scalar1=cw_t[:, ch * 3 + 2:ch * 3 + 3],
                    scalar2=cb_t[:, ch:ch + 1], op0=ALU.mult, op1=ALU.add),
                    reads=["urow%d" % gv, "cw_t", "cb_t"], writes=["acc%d" % gv])
                S.op("dve", lambda e, ur=ur, ac=ac, ch=ch: e.scalar_tensor_tensor(
                    out=ac[:, :], in0=ur[:, 1:1025], scalar=cw_t[:, ch * 3 + 1:ch * 3 + 2], in1=ac[:, :],
                    op0=ALU.mult, op1=ALU.add),
                    reads=["urow%d" % gv, "cw_t", "acc%d" % gv], writes=["acc%d" % gv])
                S.op("dve", lambda e, ur=ur, ac=ac, ch=ch: e.scalar_tensor_tensor(
                    out=ac[:, :], in0=ur[:, 0:1024], scalar=cw_t[:, ch * 3:ch * 3 + 1], in1=ac[:, :],
                    op0=ALU.mult, op1=ALU.add),
                    reads=["urow%d" % gv, "cw_t", "acc%d" % gv], writes=["acc%d" % gv])
                if gv == 0:
                    S.op("act", lambda e, ac=ac: e.activation(out=sg[:, :], in_=ac[:, :], func=AF.Silu),
                         reads=["acc0"], writes=["sg"])
            S.op("dve", lambda e, j=j: e.tensor_tensor(
                out=aT[:, j, :], in0=sg[:, :], in1=acc[1][:, :], op=ALU.mult),
                reads=["sg", "acc1"], writes=["aT%d" % j, "yt"] + ["hp%d" % k for k in range(DC)])

        if debug and m == 0:
            S.dma("sp", lambda e: e.dma_start(out=hn_o, in_=hn[:, :, :]), reads=HN_ALL)
            S.dma("sp", lambda e: e.dma_start(out=aT_o, in_=aT), reads=AT_ALL)
        def load_wd(dc):
            wb = wslot[0] % 2
            wslot[0] += 1
            wv = wbuf[wb][:, :].rearrange("p (c n) -> p c n", c=FC)
            S.dma("pool", lambda e, wv=wv, dc=dc: e.dma_start(
                out=wv, in_=w_down[:, dc * 128:(dc + 1) * 128].rearrange("(c p) n -> p c n", p=128)),
                writes=["wd%d" % wb])
            return wb, wv
        nxt = load_wd(0)
        for dc in range(DC):
            wb, wv = nxt
            if dc + 1 < DC:
                nxt = load_wd(dc + 1)
            for t in range(2):
                bi = [0, 2, 3, 4][(2 * dc + t) % 4]
                pb = bank[bi]
                ob = (2 * dc + t) % 2
                S.dma("sp", lambda e, dc=dc, t=t, ob=ob: e.dma_start(
                    out=hin[ob][:, 0:512],
                    in_=hp_d[dc * 128:(dc + 1) * 128, m * 1026 + 2 + t * 512:m * 1026 + 2 + (t + 1) * 512]),
                    reads=["hpd:%d:%d" % (m, 0), "hpd:%d:%d" % (m, 1)], writes=["hin%d" % ob])
                for fc in range(FC):
                    S.op("pe", lambda e, pb=pb, fc=fc, t=t, wv=wv: e.matmul(
                        pb[:, :], lhsT=wv[:, fc, :], rhs=aT[:, fc, t * 512:(t + 1) * 512],
                        start=(fc == 0), stop=(fc == FC - 1)),
                        reads=["wd%d" % wb, "aT%d" % fc], writes=["bk%d" % bi])
                S.op("dve", lambda e, pb=pb, ob=ob: e.tensor_tensor(
                    out=obuf[ob][:, :], in0=pb[:, :], in1=hin[ob][:, 0:512], op=ALU.add),
                    reads=["bk%d" % bi, "hin%d" % ob], writes=["obuf%d" % ob])
                if final:
                    S.op("act", lambda e, ob=ob: e.activation(out=sq[ob][:, 0:512], in_=obuf[ob][:, :], func=AF.Square),
                         reads=["obuf%d" % ob], writes=["sq%d" % ob])
                    S.op("pe", lambda e, ob=ob, t=t, dc=dc: e.matmul(
                        bank[5 + t][:, :], lhsT=ones_t[:, :], rhs=sq[ob][:, 0:512],
                        start=(dc == 0), stop=(dc == DC - 1)),
                        reads=["ones_t", "sq%d" % ob], writes=["bk%d" % (5 + t)])
                o = S.dma("sp", lambda e, dc=dc, t=t, ob=ob: e.dma_start(
                    out=oT[dc * 128:(dc + 1) * 128, c0 + t * 512:c0 + (t + 1) * 512], in_=obuf[ob][:, :]),
                    reads=["obuf%d" % ob], writes=["oT:%d:%d:%d" % (m, dc, t)])
                if not final:
                    out_ops.append(o)
        if final:
            for t in range(2):
                S.op("act", lambda e, t=t: e.activation(
                    out=rstd[:, 0:512], in_=bank[5 + t][:, :], func=AF.Sqrt, bias=eps_t[:, 0:1], scale=1.0 / D),
                    reads=["bk%d" % (5 + t), "eps_t"], writes=["rstd:f"])
                S.op("dve", lambda e: e.reciprocal(out=rstd[:, 0:512], in_=rstd[:, 0:512]),
                     reads=["rstd:f"], writes=["rstd:f"])
                for dc in range(DC):
                    ob = dc % 2
                    S.dma("sp", lambda e, dc=dc, t=t, ob=ob: e.dma_start(
                        out=hin[ob][:, 0:512], in_=oT[dc * 128:(dc + 1) * 128, c0 + t * 512:c0 + (t + 1) * 512]),
                        reads=["oT:%d:%d:%d" % (m, dc, t)], writes=["hin%d" % ob])
                    S.op("dve", lambda e, dc=dc, ob=ob: e.scalar_tensor_tensor(
                        out=obuf[ob][:, :], in0=hin[ob][:, 0:512], scalar=gf_t[:, dc:dc + 1], in1=rstd[:, 0:512],
                        op0=ALU.mult, op1=ALU.mult),
                        reads=["hin%d" % ob, "gf_t", "rstd:f"], writes=["obuf%d" % ob])
                    o = S.dma("sp", lambda e, dc=dc, t=t, ob=ob: e.dma_start(
                        out=oT[dc * 128:(dc + 1) * 128, c0 + t * 512:c0 + (t + 1) * 512], in_=obuf[ob][:, :]),
                        reads=["obuf%d" % ob], writes=["oT:%d:%d:%d" % (m, dc, t)])
                    out_ops.append(o)
    for m in range(nmacro):
        macro(m)
    S.emit(final_wait_ops=out_ops)
    return nc


F32 = mybir.dt.float32
BF16 = mybir.dt.bfloat16
AF = mybir.ActivationFunctionType
ALU = mybir.AluOpType

HPC_M = 8
NEG = -30000.0


def build_moba(nheads=HPC_M, nblk=SEQ // 256, stage=3):
    nc = bass.Bass("TRN2", target_bir_lowering=False)
    T = nblk * 256
    ntile = T // 512
    xT = nc.dram_tensor("xT", [D, T], F32, kind="ExternalInput").ap()
    w_qkv = nc.dram_tensor("w_qkv", [HPC_M, D, 384], F32, kind="ExternalInput").ap()
    gmix = nc.dram_tensor("gmix", [128, DC], F32, kind="ExternalInput").ap()
    btd = nc.dram_tensor("bt", [HPC_M, 128, 1024], F32, kind="ExternalInput").ap()
    t31d = nc.dram_tensor("t31", [128, HPC_M], F32, kind="ExternalInput").ap()
    ed = nc.dram_tensor("emat", [16, 16 * 128], BF16, kind="ExternalInput").ap()
    onesd = nc.dram_tensor("ones", [128, 128], F32, kind="ExternalInput").ap()
    identd = nc.dram_tensor("identf", [128, 128], F32, kind="ExternalInput").ap()
    o = nc.dram_tensor("o", [T, HPC_M * 128], BF16, kind="ExternalOutput").ap()
    scr = nc.dram_tensor("xn_scr", [D, T], BF16).ap()

    S = Sched(nc)
    A = nc.alloc_sbuf_tensor
    bank = [nc.alloc_psum_tensor("bank%d" % i, [128, 512], F32) for i in range(8)]
    ones_t = A("ones_t", [128, 128], F32)
    identf = A("identf_sb", [128, 128], F32)
    eps_t = A("eps_t", [128, 1], F32)
    gcol = A("gcol", [128, DC], F32)
    t31 = A("t31_sb", [128, HPC_M], F32)
    emat = A("emat_sb", [16, 16 * 128], BF16)
    X = os.environ.get("KCX", "")
    lst = [(ones_t, onesd, "ones_t"), (identf, identd, "identf"), (gcol, gmix, "gcol"), (t31, t31d, "t31"), (emat, ed, "emat")]
    if "e" in X:
        lst = lst[:3]
    for (dst, src, nm) in lst:
        S.dma("sp", lambda e, dst=dst, src=src: e.dma_start(out=dst[:], in_=src), writes=[nm])
    S.op("dve", lambda e: e.memset(eps_t[:], 1e-6), writes=["eps_t"])

    xt1 = A("xt1", [128, DC, 512], F32)
    xo1 = A("xo1", [128, DC, 512], BF16)
    norm_phase(nc, S, xT, gcol, scr, ones_t, eps_t, (bank[0], "bk0"), ntile, "np", xt1, xo1, [])

    wb = [A("wb%d" % i, [128, DC, 384], BF16) for i in range(2)]
    xn = [A("xn%d" % i, [128, DC, 512], BF16) for i in range(2)]
    qT = A("qT", [128, T], BF16)
    kT = A("kT", [128, T], BF16)
    vaug = A("vaug", [128, T // 128, 129], BF16)
    kms = A("kms", [128, 16], F32)
    kmb = A("kmb", [128, 16], BF16)
    gpad = A("gpad", [128, 16], F32)
    m8 = A("m8", [128, 8], F32)
    mbq = A("mbq", [128, 16], F32)
    mbT = A("mbT", [16, T], BF16)
    bt = [A("bt%d" % i, [128, 1024], F32) for i in range(2)]
    tmp = [A("tmp%d" % i, [128, 256], F32) for i in range(2)]
    pT = [A("pT%d" % i, [128, 256], BF16) for i in range(3)]
    rinv = A("rinv", [128, 1], F32)
    ob = [A("ob%d" % i, [128, 128], BF16) for i in range(2)]
    cnt = {"x": 0, "p": 0, "s": 0, "o": 0, "t": 0}
    scale = 1.0 / math.sqrt(128.0)

    S.op("dve", lambda e: e.memset(vaug[:, :, :], 1.0), writes=["vaug_ones"])
    S.op("dve", lambda e: e.memset(kms[:, :], 0.0), writes=["kms"])

    def load_w(hd):
        b = hd % 2
        S.dma("pool", lambda e: e.dma_start(out=wb[b][:, :, :], in_=w_qkv[hd].rearrange("(c p) n -> p c n", p=128)),
              writes=["wb%d" % b])
        if "b" not in X:
            S.dma("sp", lambda e: e.dma_start(out=bt[b][:, :], in_=btd[hd]), writes=["bt%d" % b])

    def proj_tile(hd, t):
        b = cnt["x"] % 2
        cnt["x"] += 1
        wbi = hd % 2
        w = wb[wbi]
        S.dma("sp", lambda e: e.dma_start(
            out=xn[b][:, :, :], in_=scr[:, t * 512:(t + 1) * 512].rearrange("(c p) t -> p c t", p=128)),
            reads=["scr:%d" % t], writes=["xn%d" % b])
        ts = slice(t * 512, (t + 1) * 512)
        for kc in range(DC):
            S.op("pe", lambda e, kc=kc: e.matmul(bank[0][:, :], lhsT=w[:, kc, 0:128], rhs=xn[b][:, kc, :],
                                                  start=(kc == 0), stop=(kc == DC - 1)),
                 reads=["wb%d" % wbi, "xn%d" % b], writes=["bk0"])
        S.op("act", lambda e: e.activation(out=qT[:, ts], in_=bank[0][:, :], func=AF.Copy, scale=scale),
             reads=["bk0"], writes=["qT:%d" % t])
        for kc in range(DC):
            S.op("pe", lambda e, kc=kc: e.matmul(bank[1][:, :], lhsT=w[:, kc, 128:256], rhs=xn[b][:, kc, :],
                                                  start=(kc == 0), stop=(kc == DC - 1)),
                 reads=["wb%d" % wbi, "xn%d" % b], writes=["bk1"])
        S.op("act", lambda e: e.activation(out=kT[:, ts], in_=bank[1][:, :], func=AF.Copy), reads=["bk1"], writes=["kT:%d" % t])
        for g2 in range(0 if "k" in X else 2):
            S.op("dve", lambda e, g2=g2: e.tensor_reduce(out=kms[:, 2 * t + g2:2 * t + g2 + 1], in_=bank[1][:, g2 * 256:(g2 + 1) * 256],
                                                         axis=mybir.AxisListType.X, op=ALU.add),
                 reads=["bk1", "kT:%d" % t], writes=["kms"])
        for c in range(0 if "v" in X else 4):
            for kc in range(DC):
                S.op("pe", lambda e, kc=kc, c=c: e.matmul(bank[2][:, c * 128:(c + 1) * 128], lhsT=xn[b][:, kc, c * 128:(c + 1) * 128],
                                                           rhs=w[:, kc, 256:384], start=(kc == 0), stop=(kc == DC - 1)),
                     reads=["wb%d" % wbi, "xn%d" % b], writes=["bk2"])
        for c in range(0 if "v" in X else 4):
            S.op("act", lambda e, c=c: e.activation(out=vaug[:, 4 * t + c, 0:128], in_=bank[2][:, c * 128:(c + 1) * 128], func=AF.Copy),
                 reads=["bk2", "vaug_ones"], writes=["v:%d" % t])

    def gate_tile(hd, qt):
        qb = qt // 2
        qs = slice(qt * 128, (qt + 1) * 128)
        S.op("pe", lambda e: e.matmul(bank[3][:, 0:16], lhsT=qT[:, qs], rhs=kmb[:, :], start=True, stop=True),
             reads=["qT:%d" % (qt // 4), "kmb"], writes=["bk3"])
        S.op("dve", lambda e: e.memset(gpad[:, :], -1e30), writes=["gpad"])
        if qb > 0:
            S.op("dve", lambda e: e.tensor_copy(out=gpad[:, 0:qb], in_=bank[3][:, 0:qb]), reads=["bk3", "gpad"], writes=["gpad"])
        S.op("dve", lambda e: e.max(out=m8[:, :], in_=gpad[:, :]), reads=["gpad"], writes=["m8"])
        S.op("dve", lambda e: e.tensor_scalar(out=mbq[:, :], in0=gpad[:, :], scalar1=m8[:, 2:3], scalar2=1.0,
                                              op0=ALU.is_ge, op1=ALU.subtract),
             reads=["gpad", "m8"], writes=["mbq"])
        S.op("act", lambda e: e.activation(out=mbq[:, :], in_=mbq[:, :], func=AF.Copy, scale=-NEG),
             reads=["mbq"], writes=["mbq"])
        S.op("pe", lambda e: e.transpose(out=bank[0][0:16, 0:128], in_=mbq[:, :], identity=identf[:, :]),
             reads=["mbq", "identf"], writes=["bk0"])
        S.op("act", lambda e: e.activation(out=mbT[:, qs], in_=bank[0][0:16, 0:128], func=AF.Copy),
             reads=["bk0"], writes=["mbT:%d" % (qt // 2)])

    def attn_block(hd, qb):
        qsl = slice(qb * 256, (qb + 1) * 256)
        btb = bt[hd % 2]
        nch = 2 * (qb + 1)
        obk = [bank[6], bank[7]]
        i = 0
        for n in range(qb + 1):
            for kc2 in range(2):
                kch = n * 2 + kc2
                sb = cnt["s"] % 2
                cnt["s"] += 1
                sbank = bank[4 + sb]
                sname = "bk%d" % (4 + sb)
                past = n < qb
                S.op("pe", lambda e, kch=kch, sbank=sbank, past=past: e.matmul(
                    sbank[:, 0:256], lhsT=kT[:, kch * 128:(kch + 1) * 128], rhs=qT[:, qsl], start=True, stop=not past),
                    reads=["kT:%d" % (kch // 4), "qT:%d" % (qb // 2)], writes=[sname])
                if past:
                    S.op("pe", lambda e, n=n, sbank=sbank: e.matmul(
                        sbank[:, 0:256], lhsT=emat[:, n * 128:(n + 1) * 128], rhs=mbT[:, qsl], start=False, stop=True),
                        reads=["emat", "mbT:%d" % qb], writes=[sname])
                pb = cnt["p"] % 3
                cnt["p"] += 1
                if n >= qb - 1:
                    kind = 0 if n == qb else 1
                    tb = cnt["t"] % 2
                    cnt["t"] += 1
                    off = (kind * 2 + kc2) * 256
                    S.op("dve", lambda e, sbank=sbank, tb=tb, off=off: e.tensor_tensor(
                        out=tmp[tb][:, :], in0=sbank[:, 0:256], in1=btb[:, off:off + 256], op=ALU.add),
                        reads=[sname, "bt%d" % (hd % 2)], writes=["tmp%d" % tb])
                    S.op("act", lambda e, tb=tb, pb=pb: e.activation(out=pT[pb][:, :], in_=tmp[tb][:, :], func=AF.Exp),
                         reads=["tmp%d" % tb], writes=["pT%d" % pb])
                else:
                    S.op("act", lambda e, sbank=sbank, pb=pb: e.activation(
                        out=pT[pb][:, :], in_=sbank[:, 0:256], func=AF.Exp, bias=t31[:, hd:hd + 1], scale=1.0),
                        reads=[sname, "t31"], writes=["pT%d" % pb])
                for qt in range(2):
                    S.op("pe", lambda e, qt=qt, pb=pb, kch=kch, i=i: e.matmul(
                        obk[qt][:, 0:129], lhsT=pT[pb][:, qt * 128:(qt + 1) * 128], rhs=vaug[:, kch, :],
                        start=(i == 0), stop=(i == nch - 1)),
                        reads=["pT%d" % pb, "v:%d" % (kch // 4)], writes=["bk%d" % (6 + qt)])
                i += 1
        for qt in range(2):
            obi = cnt["o"] % 2
            cnt["o"] += 1
            S.op("dve", lambda e, qt=qt: e.reciprocal(out=rinv[:, :], in_=obk[qt][:, 128:129]), reads=["bk%d" % (6 + qt)], writes=["rinv"])
            S.op("act", lambda e, qt=qt, obi=obi: e.activation(out=ob[obi][:, :], in_=obk[qt][:, 0:128], func=AF.Copy, scale=rinv[:, 0:1]),
                 reads=["bk%d" % (6 + qt), "rinv"], writes=["ob%d" % obi])
            r0 = qb * 256 + qt * 128
            S.dma("sp", lambda e, obi=obi, r0=r0: e.dma_start(out=o[r0:r0 + 128, hd * 128:(hd + 1) * 128], in_=ob[obi][:, :]),
                  reads=["ob%d" % obi])

    load_w(0)
    for hd in range(nheads):
        if hd + 1 < nheads:
            load_w(hd + 1)
        for t in range(ntile):
            proj_tile(hd, t)
        S.op("act", lambda e: e.activation(out=kmb[:, :], in_=kms[:, :], func=AF.Copy), reads=["kms"], writes=["kmb"])
        for qt in range(T // 128 if stage >= 2 else 0):
            gate_tile(hd, qt)
        for qb in range(nblk if stage >= 3 else 0):
            attn_block(hd, qb)
    S.emit()
    return nc


def t5_bucket_np(rel):
    n = np.maximum(rel, 0)
    max_exact = 16
    nf = np.maximum(n, max_exact).astype(np.float32)
    large = max_exact + (np.log(nf / max_exact) / math.log(128 / max_exact) * (32 - max_exact)).astype(np.int32)
    large = np.minimum(large, 31)
    return np.where(n < max_exact, n, large)


def moba_consts(rel_bias, hh):
    bt = np.zeros((HPC_M, 128, 1024), np.float32)
    t31 = np.zeros((128, HPC_M), np.float32)
    key = np.arange(256)[:, None]
    q = np.arange(256)[None, :]
    for i in range(HPC_M):
        h = hh * HPC_M + i
        rel_own = q - key
        rel_adj = q + 256 - key
        b_own = np.where(rel_own >= 0, rel_bias[t5_bucket_np(rel_own), h], np.float32(NEG)).astype(np.float32)
        b_adj = rel_bias[t5_bucket_np(rel_adj), h].astype(np.float32)
        for kind, bm in ((0, b_own), (1, b_adj)):
            for kc2 in range(2):
                bt[i, :, (kind * 2 + kc2) * 256:(kind * 2 + kc2 + 1) * 256] = bm[kc2 * 128:(kc2 + 1) * 128, :]
        t31[:, i] = rel_bias[31, h]
    emat = np.zeros((16, 16 * 128), np.float32)
    for n in range(16):
        emat[n, n * 128:(n + 1) * 128] = 1.0
    return bt, t31, emat

def _lay_vec(v, nch):
    return np.ascontiguousarray(np.asarray(v, np.float32).reshape(nch, 128).T)


def _ffn_stage(VK, final, y_full, h_full, w_out, w_up, w_down, g, gfin, cw, cb):
    nc = build_ffn(VK, final)
    cw_l = np.ascontiguousarray(cw.T.reshape(88, 128, 3).transpose(1, 0, 2).reshape(128, 88 * 3))
    common = dict(w_out=np.ascontiguousarray(w_out), w_up=np.ascontiguousarray(w_up), w_down=np.ascontiguousarray(w_down),
                  gn=_lay_vec(g, 16), gf=_lay_vec(gfin, 16), cw=cw_l, cb=_lay_vec(cb, 88),
                  ones=np.ones((128, 128), np.float32))
    in_maps = []
    for c in range(8):
        b, half = c // 2, c % 2
        t0 = half * 2048
        yT = np.zeros((VK, 2050), ml_dtypes.bfloat16)
        hT = np.zeros((2048, 2050), np.float32)
        lo = max(t0 - 2, 0)
        yT[:, 2050 - (t0 + 2048 - lo):] = y_full[b, lo:t0 + 2048].T
        hT[:, 2050 - (t0 + 2048 - lo):] = h_full[b, lo:t0 + 2048].T
        in_maps.append(dict(yT=yT, hT=hT, **common))
    res = run_bass_kernel_spmd(nc, in_maps, core_ids=list(range(8)))
    out = np.empty((4, 4096, 2048), np.float32)
    for c in range(8):
        b, half = c // 2, c % 2
        out[b, half * 2048:(half + 1) * 2048] = res.results[c]["oT"].T
    return out


def kernel(x, mix_norm, ret_w_in, ret_gn, ret_w_out, moba_w_qkv, moba_w_out, rel_bias,
           ffn_norm, ffn_w_up, ffn_conv_w, ffn_conv_b, ffn_w_down, final_norm):
    x = np.asarray(x, np.float32)
    ones = np.ones((128, 128), np.float32)
    w_in = np.asarray(ret_w_in[0], np.float32)
    gn = np.asarray(ret_gn[0], np.float32)

    def ret_head_w(h):
        return np.concatenate([w_in[:, h * 256:(h + 1) * 256], w_in[:, 2048 + h * 256:2048 + (h + 1) * 256],
                               w_in[:, 4096 + h * 512:4096 + (h + 1) * 512], w_in[:, 8192 + h * 512:8192 + (h + 1) * 512]], axis=1)
    nc = build_ret()
    in_maps = []
    for c in range(8):
        b, hh = c // 2, c % 2
        heads = [hh * 4 + i for i in range(4)]
        wh = np.ascontiguousarray(np.stack([ret_head_w(h) for h in heads]))
        gnrep = np.ascontiguousarray(np.broadcast_to(np.concatenate([gn[h * 512:(h + 1) * 512] for h in heads])[None, :], (128, 2048)))
        in_maps.append(dict(xT=np.ascontiguousarray(x[b].T), w_in=wh, gmix=_lay_vec(mix_norm[0], 16), gnrep=gnrep, ones=ones,
                            identb=np.eye(128, dtype=np.float32).astype(ml_dtypes.bfloat16), **ret_consts(hh)))
    res = run_bass_kernel_spmd(nc, in_maps, core_ids=list(range(8)))
    y0 = np.empty((4, 4096, 4096), ml_dtypes.bfloat16)
    for c in range(8):
        b, hh = c // 2, c % 2
        y0[b, :, hh * 2048:(hh + 1) * 2048] = res.results[c]["y"]
    h1 = _ffn_stage(4096, False, y0, x, ret_w_out[0], ffn_w_up[0], ffn_w_down[0], ffn_norm[0], final_norm,
                    np.asarray(ffn_conv_w[0], np.float32), np.asarray(ffn_conv_b[0], np.float32))
    wq = np.asarray(moba_w_qkv[0], np.float32)
    rb = np.asarray(rel_bias, np.float32)

    def moba_head_w(h):
        return np.concatenate([wq[:, h * 128:(h + 1) * 128], wq[:, 2048 + h * 128:2048 + (h + 1) * 128],
                               wq[:, 4096 + h * 128:4096 + (h + 1) * 128]], axis=1)
    nc = build_moba()
    in_maps = []
    for c in range(8):
        b, hh = c // 2, c % 2
        heads = [hh * 8 + i for i in range(8)]
        wh = np.ascontiguousarray(np.stack([moba_head_w(h) for h in heads]))
        bt, t31, emat = moba_consts(rb, hh)
        in_maps.append(dict(xT=np.ascontiguousarray(h1[b].T), w_qkv=wh, gmix=_lay_vec(mix_norm[1], 16), bt=bt, t31=t31,
                            emat=emat.astype(ml_dtypes.bfloat16), ones=ones, identf=np.eye(128, dtype=np.float32)))
    res = run_bass_kernel_spmd(nc, in_maps, core_ids=list(range(8)))
    o1 = np.empty((4, 4096, 2048), ml_dtypes.bfloat16)
    for c in range(8):
        b, hh = c // 2, c % 2
        o1[b, :, hh * 1024:(hh + 1) * 1024] = res.results[c]["o"]
    out = _ffn_stage(2048, True, o1, h1, moba_w_out[0], ffn_w_up[1], ffn_w_down[1], ffn_norm[1], final_norm,
                     np.asarray(ffn_conv_w[1], np.float32), np.asarray(ffn_conv_b[1], np.float32))
    return out.astype(np.float32)
```

```python
import math
import os
import contextlib
import numpy as np
import ml_dtypes
import concourse.bass as bass
import concourse.mybir as mybir
from concourse.bass_utils import run_bass_kernel_spmd


ENGS = ("pe", "act", "dve", "pool", "sp")
N_DMA_SEMS = 12
STRICT = True


class _Op:
    __slots__ = ("eng", "fn", "deps", "signal", "sem", "val", "is_dma", "idx")

    def __init__(self, eng, fn, is_dma):
        self.eng = eng
        self.fn = fn
        self.deps = []
        self.signal = False
        self.sem = None
        self.val = 0
        self.is_dma = is_dma


class Sched:
    def __init__(self, nc):
        self.nc = nc
        self.ops = {e: [] for e in ENGS}
        self.last_w = {}
        self.readers = {}
        self.dma_rr = {e: 0 for e in ENGS}
        self.dma_last = {}

    def _add(self, eng, fn, reads, writes, is_dma):
        op = _Op(eng, fn, is_dma)
        deps = []
        for r in reads:
            w = self.last_w.get(r)
            if w is not None:
                deps.append(w)
            if r.startswith("bk") or r.startswith("xb"):
                for rd in self.readers.get(r, ()):
                    if rd.eng != eng:
                        deps.append(rd)
        for r in writes:
            w = self.last_w.get(r)
            if w is not None and (w.eng != eng or w.is_dma or is_dma or (STRICT and eng != "pe")):
                deps.append(w)
            for rd in self.readers.get(r, ()):
                if rd.eng != eng or rd.is_dma or is_dma or STRICT:
                    deps.append(rd)
        if is_dma:
            slot = self.dma_rr[eng] % N_DMA_SEMS
            self.dma_rr[eng] += 1
            prev = self.dma_last.get((eng, slot))
            if prev is not None:
                deps.append(prev)
            self.dma_last[(eng, slot)] = op
            op.sem = (eng, slot)
            op.signal = True
        seen = set()
        for d in deps:
            if id(d) in seen or d is op:
                continue
            seen.add(id(d))
            d.signal = True
            op.deps.append(d)
        for r in reads:
            lst = self.readers.setdefault(r, [])
            if not is_dma:
                lst[:] = [o for o in lst if o.is_dma or o.eng != eng]
            lst.append(op)
        for r in writes:
            self.last_w[r] = op
            self.readers[r] = []
        self.ops[eng].append(op)
        return op

    def fence(self, keep=()):
        lasts = []
        for e in ENGS:
            if e == "sp":
                continue
            for op in reversed(self.ops[e]):
                if not op.is_dma:
                    lasts.append(op)
                    break
        lasts.extend(self.dma_last.values())
        for e in ENGS:
            f = _Op(e, lambda eng: eng.nop(), False)
            for d in lasts:
                if d.eng == e and not d.is_dma:
                    continue
                d.signal = True
                f.deps.append(d)
            self.ops[e].append(f)
        self.last_w.clear()
        self.readers.clear()

    def op(self, eng, fn, reads=(), writes=()):
        return self._add(eng, fn, tuple(reads), tuple(writes), False)

    def dma(self, eng, fn, reads=(), writes=()):
        return self._add(eng, fn, tuple(reads), tuple(writes), True)

    def emit(self, final_wait_ops=()):
        nc = self.nc

        with contextlib.ExitStack() as es:
            esem = {e: es.enter_context(nc.semaphore("s_" + e)) for e in ENGS if e != "sp"}
            dsem = {}
            for e in ENGS:
                if self.dma_rr[e] > 0:
                    for s in range(min(N_DMA_SEMS, self.dma_rr[e])):
                        dsem[(e, s)] = es.enter_context(nc.semaphore("d_%s_%d" % (e, s)))
            for e in ENGS:
                c = 0
                dc = {}
                for op in self.ops[e]:
                    if op.is_dma:
                        dc[op.sem] = dc.get(op.sem, 0) + 16
                        op.val = dc[op.sem]
                        op.sem = dsem[op.sem]
                    elif op.signal:
                        c += 1
                        op.val = c
                        op.sem = esem[e]
            self.sem_max = {e: max([op.val for op in self.ops[e] if not op.is_dma] + [0]) for e in ENGS}
            self.dma_max = max([op.val for e in ENGS for op in self.ops[e] if op.is_dma] + [0])
            self.n_ops = {e: len(self.ops[e]) for e in ENGS}
            finals = {}
            for op in final_wait_ops:
                finals.setdefault(op.eng, []).append(op)
            block = es.enter_context(nc.Block())

            def run(e, eng):
                waited = {}
                for op in self.ops[e]:
                    for d in op.deps:
                        k = id(d.sem)
                        if waited.get(k, 0) < d.val:
                            eng.wait_ge(d.sem, d.val)
                            waited[k] = d.val
                    inst = op.fn(eng)
                    if op.signal:
                        inst.then_inc(op.sem, 16 if op.is_dma else 1)
                for (qe, slot), op in self.dma_last.items():
                    if qe == e:
                        eng.wait_ge(op.sem, op.val)

            if self.ops["sp"] or finals.get("sp"):
                @block.sync
                def _(eng):
                    run("sp", eng)
            if self.ops["act"]:
                @block.scalar
                def _(eng):
                    run("act", eng)
            if self.ops["dve"]:
                @block.vector
                def _(eng):
                    run("dve", eng)
            if self.ops["pool"]:
                @block.gpsimd
                def _(eng):
                    run("pool", eng)
            if self.ops["pe"]:
                @block.tensor
                def _(eng):
                    run("pe", eng)


F32 = mybir.dt.float32
BF16 = mybir.dt.bfloat16
AF = mybir.ActivationFunctionType
ALU = mybir.AluOpType
AX = mybir.AxisListType

D = 2048
DC = 16
FF = 5632
FC = 44
MT = 1024
RH = 8
MH = 16
NEG = -30000.0
_DT = {F32: 4, BF16: 2}


class Arena:
    def __init__(self, nc, nbytes):
        self.t = nc.alloc_sbuf_tensor("arena", [128, nbytes // 2], BF16)
        self.nbytes = nbytes
        self.off = 0

    def alloc(self, shape, dtype):
        size = int(np.prod(shape[1:])) * _DT[dtype]
        off = (self.off + 31) // 32 * 32
        assert off + size <= self.nbytes, ("arena overflow", off, size, self.nbytes)
        v = self.t[0:shape[0], off // 2:(off + size) // 2]
        if dtype != BF16:
            v = v.bitcast(dtype)
        if len(shape) == 3:
            v = v.rearrange("p (a b) -> p a b", a=shape[1])
        elif len(shape) == 4:
            v = v.rearrange("p (a b c) -> p a b c", a=shape[1], b=shape[2])
        self.off = off + size
        return v


def emit_norm(S, AR, bank, xT, gcol, scr, ones_t, eps_t, ntile):
    xt = AR.alloc([128, DC, 512], F32)
    xo = AR.alloc([128, DC, 512], BF16)
    sq = [AR.alloc([128, 512], F32) for _ in range(2)]
    rs = AR.alloc([128, 512], F32)

    def tile(t):
        S.dma("sp", lambda e: e.dma_start(out=xt, in_=xT[:, t * 512:(t + 1) * 512].rearrange("(c p) t -> p c t", p=128)),
              writes=["n_xt"])
        for kc in range(DC):
            sb = kc % 2
            S.op("act", lambda e, kc=kc, sb=sb: e.activation(out=sq[sb], in_=xt[:, kc, :], func=AF.Square),
                 reads=["n_xt"], writes=["n_sq%d" % sb])
            S.op("pe", lambda e, kc=kc, sb=sb: e.matmul(bank[0][:, :], lhsT=ones_t, rhs=sq[sb], start=(kc == 0), stop=(kc == DC - 1)),
                 reads=["ones_t", "n_sq%d" % sb], writes=["bk0"])
        S.op("act", lambda e: e.activation(out=rs, in_=bank[0][:, :], func=AF.Sqrt, bias=eps_t[:, 0:1], scale=1.0 / D),
             reads=["bk0", "eps_t"], writes=["n_rs"])
        S.op("dve", lambda e: e.reciprocal(out=rs, in_=rs), reads=["n_rs"], writes=["n_rs"])
        for kc in range(DC):
            S.op("dve", lambda e, kc=kc: e.scalar_tensor_tensor(out=xo[:, kc, :], in0=xt[:, kc, :], scalar=gcol[:, kc:kc + 1], in1=rs,
                                                                  op0=ALU.mult, op1=ALU.mult),
                 reads=["n_xt", "n_rs", "gcol"], writes=["n_xo"])
        S.dma("sp", lambda e: e.dma_start(out=scr[:, t * 512:(t + 1) * 512].rearrange("(c p) t -> p c t", p=128), in_=xo),
              reads=["n_xo"], writes=["scr:%d" % t])
    for t in range(ntile):
        tile(t)


def emit_ret(nc, S, AR, bank, xbank, io, T):
    ntile = T // 512
    ones_t, identb, eps_t, geps_t = io["ones_t"], io["identb"], io["eps_t"], io["geps_t"]
    gcol = AR.alloc([128, DC], F32)
    zeta_t = AR.alloc([128, RH], F32)
    cd_t = AR.alloc([128, RH], F32)
    for (dst, src, nm) in [(gcol, io["gmix0"], "gcol"), (zeta_t, io["zeta"], "zeta_t"), (cd_t, io["cd"], "cd_t")]:
        S.dma("sp", lambda e, dst=dst, src=src: e.dma_start(out=dst, in_=src), writes=[nm])
    mark = AR.off
    emit_norm(S, AR, bank, io["xT"], gcol, io["scr"], ones_t, eps_t, ntile)
    AR.off = mark
    S.fence()
    wb = [AR.alloc([128, DC, 1536], BF16) for _ in range(2)]
    xn = [AR.alloc([128, DC, 512], BF16) for _ in range(2)]
    cs_t = [AR.alloc([128, 2, 512], F32) for _ in range(2)]
    hc = [dict(dec=AR.alloc([128, 128], F32), xi=AR.alloc([128, 512], F32), gn=AR.alloc([128, 512], F32)) for _ in range(2)]
    t1 = AR.alloc([128, 512], F32)
    t2 = AR.alloc([128, 512], F32)
    qr = AR.alloc([128, 2, 512], BF16)
    qx = AR.alloc([128, 2, 512], BF16)
    kr = AR.alloc([128, 2, 512], BF16)
    ktok = AR.alloc([128, 256], BF16)
    vb = AR.alloc([128, 512], BF16)
    vz = AR.alloc([128, 512], BF16)
    sgt = AR.alloc([128, 512], F32)
    stm = AR.alloc([128, 128], BF16)
    state = AR.alloc([128, 2, 512], F32)
    state_b = AR.alloc([128, 2, 512], BF16)
    bst = AR.alloc([128, 6], F32)
    mv = AR.alloc([128, 2], F32)
    rstd = AR.alloc([128, 1], F32)
    on = AR.alloc([128, 512], F32)
    yb = AR.alloc([128, 512], BF16)
    yTt = [AR.alloc([128, 4, 512], BF16) for _ in range(2)]
    w_in, cosd, sind, yT, scr = io["ret_w"], io["cosT"], io["sinT"], io["yT"], io["scr"]
    cnt = {"x": 0, "y": 0}

    def load_w(hd):
        b = hd % 2
        for (lo, hi) in [(0, 512), (512, 1024), (1024, 1536)]:
            S.dma("pool", lambda e, lo=lo, hi=hi: e.dma_start(
                out=wb[b][:, :, lo:hi], in_=w_in[hd, :, lo:hi].rearrange("(c p) n -> p c n", p=128)),
                writes=["wb%d:%d" % (b, lo)])
        S.dma("sp", lambda e: e.dma_start(out=hc[b]["dec"], in_=io["dec"][:, hd * 128:(hd + 1) * 128]), writes=["hc%d" % b])
        S.dma("sp", lambda e: e.dma_start(out=hc[b]["xi"], in_=io["xi"][:, hd * 512:(hd + 1) * 512]), writes=["hc%d" % b])
        S.dma("sp", lambda e: e.dma_start(out=hc[b]["gn"], in_=io["gnrep"][:, hd * 512:(hd + 1) * 512]), writes=["hc%d" % b])

    def load_xn(t):
        b = cnt["x"] % 2
        cnt["x"] += 1
        S.dma("sp", lambda e: e.dma_start(out=xn[b], in_=scr[:, t * 512:(t + 1) * 512].rearrange("(c p) t -> p c t", p=128)),
              reads=["scr:%d" % t], writes=["xn%d" % b])
        return b

    def rotary(dst, csb, nm):
        pa, pb = bank[0], bank[1]
        cs = cs_t[csb][:, 0, :]
        sn = cs_t[csb][:, 1, :]
        cn = "cs_t%d" % csb
        S.op("dve", lambda e: e.tensor_tensor(out=t1, in0=pa[:, :], in1=cs, op=ALU.mult), reads=["bk0", cn], writes=["t1"])
        S.op("dve", lambda e: e.tensor_tensor(out=t2, in0=pb[:, :], in1=sn, op=ALU.mult), reads=["bk1", cn], writes=["t2"])
        S.op("pool", lambda e: e.tensor_tensor(out=dst[:, 0, :], in0=t1, in1=t2, op=ALU.subtract), reads=["t1", "t2"], writes=[nm + "0"])
        S.op("dve", lambda e: e.tensor_tensor(out=t1, in0=pb[:, :], in1=cs, op=ALU.mult), reads=["bk1", cn], writes=["t1"])
        S.op("dve", lambda e: e.tensor_tensor(out=t2, in0=pa[:, :], in1=sn, op=ALU.mult), reads=["bk0", cn], writes=["t2"])
        S.op("pool", lambda e: e.tensor_tensor(out=dst[:, 1, :], in0=t1, in1=t2, op=ALU.add), reads=["t1", "t2"], writes=[nm + "1"])

    def head_tile(hd, t, xb):
        wbi = hd % 2
        w = wb[wbi]
        hcb = hc[wbi]
        hcn = "hc%d" % wbi
        wn = ["wb%d:%d" % (wbi, lo) for lo in (0, 512, 1024)]
        xnn = "xn%d" % xb
        x_ = xn[xb]
        csb = xb
        yti = cnt["y"] % 2
        cnt["y"] += 1
        S.dma("sp", lambda e: e.dma_start(out=cs_t[csb][:, 0, :], in_=cosd[:, t * 512:(t + 1) * 512]), writes=["cs_t%d" % csb])
        S.dma("sp", lambda e: e.dma_start(out=cs_t[csb][:, 1, :], in_=sind[:, t * 512:(t + 1) * 512]), writes=["cs_t%d" % csb])
        for (col, dst, nm) in [(0, qr, "qr"), (256, kr, "kr")]:
            for dch in range(2):
                for kc in range(DC):
                    S.op("pe", lambda e, dch=dch, kc=kc, col=col: e.matmul(
                        bank[dch][:, :], lhsT=w[:, kc, col + dch * 128:col + (dch + 1) * 128], rhs=x_[:, kc, :],
                        start=(kc == 0), stop=(kc == DC - 1)),
                        reads=[wn[0], xnn], writes=["bk%d" % dch])
            rotary(dst, csb, nm)
        for dch in range(2):
            S.op("dve", lambda e, dch=dch: e.tensor_tensor(out=qx[:, dch, :], in0=qr[:, dch, :], in1=hcb["xi"], op=ALU.mult),
                 reads=["qr%d" % dch, hcn], writes=["qx%d" % dch])

        def chunk(c):
            cs = slice(c * 128, (c + 1) * 128)
            first = (t == 0 and c == 0)
            for kc in range(DC):
                S.op("pe", lambda e, kc=kc: e.matmul(bank[2][:, :], lhsT=x_[:, kc, cs], rhs=w[:, kc, 512:1024],
                                                      start=(kc == 0), stop=(kc == DC - 1)),
                     reads=[wn[1], xnn], writes=["bk2"])
            S.op("act", lambda e: e.activation(out=vb, in_=bank[2][:, :], func=AF.Copy), reads=["bk2"], writes=["vb"])
            S.op("act", lambda e: e.activation(out=vz, in_=bank[2][:, :], func=AF.Copy, scale=zeta_t[:, hd:hd + 1]),
                 reads=["bk2", "zeta_t"], writes=["vz"])
            for kc in range(DC):
                S.op("pe", lambda e, kc=kc: e.matmul(bank[3][:, :], lhsT=x_[:, kc, cs], rhs=w[:, kc, 1024:1536],
                                                      start=(kc == 0), stop=(kc == DC - 1)),
                     reads=[wn[2], xnn], writes=["bk3"])
            S.op("act", lambda e: e.activation(out=sgt, in_=bank[3][:, :], func=AF.Silu), reads=["bk3"], writes=["sgt"])
            S.op("dve", lambda e: e.tensor_tensor(out=sgt, in0=sgt, in1=hcb["gn"], op=ALU.mult), reads=["sgt", hcn], writes=["sgt"])
            for dch in range(2):
                S.op("pe", lambda e, dch=dch: e.matmul(bank[0][:, 0:128], lhsT=kr[:, dch, cs], rhs=qr[:, dch, cs],
                                                        start=(dch == 0), stop=(dch == 1)),
                     reads=["kr%d" % dch, "qr%d" % dch], writes=["bk0"])
            S.op("dve", lambda e: e.tensor_tensor(out=stm, in0=bank[0][:, 0:128], in1=hcb["dec"], op=ALU.mult),
                 reads=["bk0", hcn], writes=["stm"])
            for dch in range(2):
                S.op("pe", lambda e, dch=dch: e.transpose(out=xbank[:, dch * 128:(dch + 1) * 128], in_=kr[:, dch, cs], identity=identb),
                     reads=["kr%d" % dch, "identb"], writes=["xb"])
            S.op("act", lambda e: e.activation(out=ktok, in_=xbank[:, 0:256], func=AF.Copy), reads=["xb"], writes=["ktok"])
            S.op("pe", lambda e: e.matmul(bank[4][:, :], lhsT=stm, rhs=vb, start=True, stop=first), reads=["stm", "vb"], writes=["bk4"])
            if not first:
                for dch in range(2):
                    S.op("pe", lambda e, dch=dch: e.matmul(bank[4][:, :], lhsT=qx[:, dch, cs], rhs=state_b[:, dch, :],
                                                            start=False, stop=(dch == 1)),
                         reads=["qx%d" % dch, "state_b%d" % dch], writes=["bk4"])
            for dch in range(2):
                S.op("pe", lambda e, dch=dch: e.matmul(bank[5 + dch][:, :], lhsT=ktok[:, dch * 128:(dch + 1) * 128], rhs=vz,
                                                        start=True, stop=True),
                     reads=["ktok", "vz"], writes=["bk%d" % (5 + dch)])
                if first:
                    S.op("dve", lambda e, dch=dch: e.tensor_copy(out=state[:, dch, :], in_=bank[5 + dch][:, :]),
                         reads=["bk%d" % (5 + dch)], writes=["state%d" % dch])
                else:
                    S.op("dve", lambda e, dch=dch: e.scalar_tensor_tensor(
                        out=state[:, dch, :], in0=state[:, dch, :], scalar=cd_t[:, hd:hd + 1], in1=bank[5 + dch][:, :],
                        op0=ALU.mult, op1=ALU.add),
                        reads=["state%d" % dch, "bk%d" % (5 + dch), "cd_t"], writes=["state%d" % dch])
                S.op("act", lambda e, dch=dch: e.activation(out=state_b[:, dch, :], in_=state[:, dch, :], func=AF.Copy),
                     reads=["state%d" % dch], writes=["state_b%d" % dch])
            S.op("dve", lambda e: e.bn_stats(out=bst, in_=bank[4][:, :]), reads=["bk4"], writes=["bst"])
            S.op("dve", lambda e: e.bn_aggr(out=mv, in_=bst), reads=["bst"], writes=["mv"])
            S.op("act", lambda e: e.activation(out=rstd, in_=mv[:, 1:2], func=AF.Sqrt, bias=geps_t[:, 0:1], scale=1.0),
                 reads=["mv", "geps_t"], writes=["rstd"])
            S.op("dve", lambda e: e.reciprocal(out=rstd, in_=rstd), reads=["rstd"], writes=["rstd"])
            S.op("dve", lambda e: e.tensor_scalar(out=on, in0=bank[4][:, :], scalar1=mv[:, 0:1], scalar2=rstd[:, 0:1],
                                                  op0=ALU.subtract, op1=ALU.mult),
                 reads=["bk4", "mv", "rstd"], writes=["on"])
            S.op("dve", lambda e: e.tensor_tensor(out=yb, in0=on, in1=sgt, op=ALU.mult), reads=["on", "sgt"], writes=["yb"])
            for i in range(4):
                S.op("pe", lambda e, i=i: e.transpose(out=xbank[:, 256 + i * 128:256 + (i + 1) * 128], in_=yb[:, i * 128:(i + 1) * 128],
                                                      identity=identb),
                     reads=["yb", "identb"], writes=["xb"])
            S.op("act", lambda e: e.activation(out=yTt[yti][:, :, cs], in_=xbank[:, 256:768].rearrange("p (i t) -> p i t", i=4),
                                               func=AF.Copy),
                 reads=["xb"], writes=["yTt%d" % yti])
        for c in range(4):
            chunk(c)
        S.dma("sp", lambda e: e.dma_start(
            out=yT[hd * 512:(hd + 1) * 512, 2 + t * 512:2 + (t + 1) * 512].rearrange("(i p) t -> p i t", p=128), in_=yTt[yti]),
            reads=["yTt%d" % yti])

    load_w(0)
    for hd in range(RH):
        if hd + 1 < RH:
            load_w(hd + 1)
        xb = load_xn(0)
        for t in range(ntile):
            nxb = load_xn(t + 1) if t + 1 < ntile else None
            head_tile(hd, t, xb)
            xb = nxb


def emit_ffn(nc, S, AR, bank, io, T, VK, final, tag):
    VKC = VK // 128
    nmacro = T // MT
    yT, hT, w_out, w_up, w_down, hp_d, dst = io["yT"], io["hT"], io["w_out"], io["w_up"], io["w_down"], io["hp_d"], io["dst"]
    ones_t, eps_t = io["ones_t"], io["eps_t"]
    big = AR.alloc([128, FC * MT], BF16)
    aT = big.rearrange("p (c t) -> p c t", c=FC)
    yt = big[:, 0:VKC * 514].rearrange("p (c t) -> p c t", c=VKC)
    hp = big[:, 32 * 514:32 * 514 + DC * 514 * 2].bitcast(F32).rearrange("p (c t) -> p c t", c=DC)
    hn = AR.alloc([128, DC, 1026], BF16)
    wbuf = [AR.alloc([128, FC * 128], BF16) for _ in range(2)]
    wub = [AR.alloc([128, 2, DC, 128], BF16) for _ in range(2)]
    urow = [AR.alloc([128, 1026], F32) for _ in range(2)]
    acc = [AR.alloc([128, 1024], F32) for _ in range(2)]
    sg = AR.alloc([128, 1024], F32)
    hin = [AR.alloc([128, 514], F32) for _ in range(2)]
    sq = [AR.alloc([128, 514], F32) for _ in range(2)]
    rstd = AR.alloc([128, 514], F32)
    obuf = [AR.alloc([128, 512], F32) for _ in range(2)]
    gn_t = AR.alloc([128, DC], F32)
    gf_t = AR.alloc([128, DC], F32)
    cw_t = AR.alloc([128, 2 * FC * 3], F32)
    cb_t = AR.alloc([128, 2 * FC], F32)
    for (d_, s_, nm) in [(gn_t, io["gn"], "gn_t"), (gf_t, io["gf"], "gf_t"), (cw_t, io["cw"], "cw_t"), (cb_t, io["cb"], "cb_t")]:
        S.dma("sp", lambda e, d_=d_, s_=s_: e.dma_start(out=d_, in_=s_), writes=[nm])
    AT_ALL = ["aT%d" % j for j in range(FC)]
    HP_ALL = ["hp%d" % k for k in range(DC)]
    wslot = [0]

    def rms_finish(ps_list):
        for (p, lo, hi, bi) in ps_list:
            S.op("act", lambda e, p=p, lo=lo, hi=hi: e.activation(out=rstd[:, lo:hi], in_=p, func=AF.Sqrt, bias=eps_t[:, 0:1], scale=1.0 / D),
                 reads=["bk%d" % bi, "eps_t"], writes=["rstd:%d" % lo])
            S.op("dve", lambda e, lo=lo, hi=hi: e.reciprocal(out=rstd[:, lo:hi], in_=rstd[:, lo:hi]),
                 reads=["rstd:%d" % lo], writes=["rstd:%d" % lo])

    def macro(m):
        c0 = m * MT

        def sub(s):
            lo, hi = (0, 514) if s == 0 else (514, 1026)
            W = hi - lo
            ntiles = [(0, 2, bank[1], 1), (2, 514, bank[0], 0)] if s == 0 else [(0, 512, bank[0], 0)]
            stiles = [(0, 2, bank[3], 3), (2, 514, bank[2], 2)] if s == 0 else [(0, 512, bank[2], 2)]
            skip = 2 if (m == 0 and s == 0) else 0
            S.dma("sp", lambda e: e.dma_start(
                out=yt[:, :, skip:W], in_=yT[:, c0 + lo + skip:c0 + hi].rearrange("(c p) t -> p c t", p=128)),
                writes=["yt"] + AT_ALL)
            if skip:
                S.op("pool", lambda e: e.memset(yt[:, :, 0:2], 0.0), writes=["yt"] + AT_ALL)
            for dc in range(DC):
                wb = wslot[0] % 2
                wslot[0] += 1
                wv = wbuf[wb][:, 0:VKC * 128].rearrange("p (c n) -> p c n", c=VKC)
                S.dma("pool", lambda e, wv=wv, dc=dc: e.dma_start(
                    out=wv, in_=w_out[:, dc * 128:(dc + 1) * 128].rearrange("(c p) n -> p c n", p=128)), writes=["wd%d" % wb])
                hb = dc % 2
                S.dma("sp", lambda e, hb=hb, dc=dc: e.dma_start(
                    out=hin[hb][:, skip:W], in_=hT[dc * 128:(dc + 1) * 128, c0 + lo + skip:c0 + hi]), writes=["hin%d" % hb])
                if skip:
                    S.op("pool", lambda e, hb=hb: e.memset(hin[hb][:, 0:2], 0.0), writes=["hin%d" % hb])
                for (a, b, pb, bi) in ntiles:
                    for vc in range(VKC):
                        S.op("pe", lambda e, a=a, b=b, pb=pb, vc=vc, wv=wv: e.matmul(
                            pb[:, 0:b - a], lhsT=wv[:, vc, :], rhs=yt[:, vc, a:b], start=(vc == 0), stop=(vc == VKC - 1)),
                            reads=["wd%d" % wb, "yt"], writes=["bk%d" % bi])
                    S.op("dve", lambda e, a=a, b=b, pb=pb, dc=dc, hb=hb: e.tensor_tensor(
                        out=hp[:, dc, a:b], in0=pb[:, 0:b - a], in1=hin[hb][:, a:b], op=ALU.add),
                        reads=["bk%d" % bi, "hin%d" % hb], writes=["hp%d" % dc] + AT_ALL)
                S.op("act", lambda e, dc=dc, hb=hb: e.activation(out=sq[hb][:, 0:W], in_=hp[:, dc, 0:W], func=AF.Square),
                     reads=["hp%d" % dc], writes=["sq%d" % hb])
                for (a, b, pb, bi) in stiles:
                    S.op("pe", lambda e, a=a, b=b, pb=pb, dc=dc, hb=hb: e.matmul(
                        pb[:, 0:b - a], lhsT=ones_t, rhs=sq[hb][:, a:b], start=(dc == 0), stop=(dc == DC - 1)),
                        reads=["ones_t", "sq%d" % hb], writes=["bk%d" % bi])
            rms_finish([(pb[:, 0:b - a], a, b, bi) for (a, b, pb, bi) in stiles])
            for kc in range(DC):
                S.op("dve", lambda e, kc=kc: e.scalar_tensor_tensor(
                    out=hn[:, kc, lo:hi], in0=hp[:, kc, 0:W], scalar=gn_t[:, kc:kc + 1], in1=rstd[:, 0:W], op0=ALU.mult, op1=ALU.mult),
                    reads=["hp%d" % kc, "gn_t"] + ["rstd:%d" % a for (a, b, pb, bi) in stiles], writes=["hn%d:%d" % (kc, s)])
            S.dma("sp", lambda e: e.dma_start(
                out=hp_d[:, m * 1026 + lo:m * 1026 + hi].rearrange("(c p) t -> p c t", p=128), in_=hp[:, :, 0:W]),
                reads=HP_ALL, writes=["hpd:%d:%d" % (m, s)])
        for s in range(2):
            sub(s)

        def load_wu(j):
            ub = j % 2
            for gv in range(2):
                col = gv * FF + j * 128
                S.dma("pool", lambda e, gv=gv, col=col: e.dma_start(
                    out=wub[ub][:, gv, :, :], in_=w_up[:, col:col + 128].rearrange("(c p) n -> p c n", p=128)),
                    writes=["wu%d:%d" % (ub, gv)])
        load_wu(0)
        HN_ALL = ["hn%d:%d" % (k, s) for k in range(DC) for s in range(2)]

        def upj(j):
            ub = j % 2
            for gv in range(2):
                main = (bank[4], bank[5]) if gv == 0 else (bank[6], bank[7])
                hcol = 2 * gv
                tiles = [(0, 2, bank[1][:, hcol:hcol + 2], "bk1"), (2, 514, main[0][:, :], "bk%d" % (4 + 2 * gv)),
                         (514, 1026, main[1][:, :], "bk%d" % (5 + 2 * gv))]
                for (a, b, pap, pn) in tiles:
                    for kc in range(DC):
                        S.op("pe", lambda e, a=a, b=b, pap=pap, kc=kc, gv=gv: e.matmul(
                            pap, lhsT=wub[ub][:, gv, kc, :], rhs=hn[:, kc, a:b], start=(kc == 0), stop=(kc == DC - 1)),
                            reads=["wu%d:%d" % (ub, gv)] + HN_ALL, writes=[pn])
                ur = urow[gv]
                for (a, b, pap, pn) in tiles:
                    S.op("act", lambda e, a=a, b=b, pap=pap, ur=ur: e.activation(out=ur[:, a:b], in_=pap, func=AF.Copy),
                         reads=[pn], writes=["urow%d" % gv])
                ch = gv * FC + j
                ac = acc[gv]
                S.op("dve", lambda e, ur=ur, ac=ac, ch=ch: e.tensor_scalar(
                    out=ac, in0=ur[:, 2:1026], scalar1=cw_t[:, ch * 3 + 2:ch * 3 + 3], scalar2=cb_t[:, ch:ch + 1],
                    op0=ALU.mult, op1=ALU.add), reads=["urow%d" % gv, "cw_t", "cb_t"], writes=["acc%d" % gv])
                S.op("dve", lambda e, ur=ur, ac=ac, ch=ch: e.scalar_tensor_tensor(
                    out=ac, in0=ur[:, 1:1025], scalar=cw_t[:, ch * 3 + 1:ch * 3 + 2], in1=ac, op0=ALU.mult, op1=ALU.add),
                    reads=["urow%d" % gv, "cw_t", "acc%d" % gv], writes=["acc%d" % gv])
                S.op("dve", lambda e, ur=ur, ac=ac, ch=ch: e.scalar_tensor_tensor(
                    out=ac, in0=ur[:, 0:1024], scalar=cw_t[:, ch * 3:ch * 3 + 1], in1=ac, op0=ALU.mult, op1=ALU.add),
                    reads=["urow%d" % gv, "cw_t", "acc%d" % gv], writes=["acc%d" % gv])
                if gv == 0:
                    S.op("act", lambda e, ac=ac: e.activation(out=sg, in_=ac, func=AF.Silu), reads=["acc0"], writes=["sg"])
            S.op("dve", lambda e: e.tensor_tensor(out=aT[:, j, :], in0=sg, in1=acc[1], op=ALU.mult),
                 reads=["sg", "acc1"], writes=["aT%d" % j, "yt"] + HP_ALL)
        for j in range(FC):
            if j + 1 < FC:
                load_wu(j + 1)
            upj(j)

        def load_wd(dc):
            wb = wslot[0] % 2
            wslot[0] += 1
            wv = wbuf[wb].rearrange("p (c n) -> p c n", c=FC)
            S.dma("pool", lambda e: e.dma_start(out=wv, in_=w_down[:, dc * 128:(dc + 1) * 128].rearrange("(c p) n -> p c n", p=128)),
                  writes=["wd%d" % wb])
            return wb, wv

        def down(dc, wb, wv):
            for t in range(2):
                bi = [0, 2, 3, 4][(2 * dc + t) % 4]
                pb = bank[bi]
                ob = (2 * dc + t) % 2
                S.dma("sp", lambda e, t=t, ob=ob: e.dma_start(
                    out=hin[ob][:, 0:512], in_=hp_d[dc * 128:(dc + 1) * 128, m * 1026 + 2 + t * 512:m * 1026 + 2 + (t + 1) * 512]),
                    reads=["hpd:%d:0" % m, "hpd:%d:1" % m], writes=["hin%d" % ob])
                for fc in range(FC):
                    S.op("pe", lambda e, pb=pb, fc=fc, t=t: e.matmul(
                        pb[:, :], lhsT=wv[:, fc, :], rhs=aT[:, fc, t * 512:(t + 1) * 512], start=(fc == 0), stop=(fc == FC - 1)),
                        reads=["wd%d" % wb, "aT%d" % fc], writes=["bk%d" % bi])
                S.op("dve", lambda e, pb=pb, ob=ob: e.tensor_tensor(out=obuf[ob], in0=pb[:, :], in1=hin[ob][:, 0:512], op=ALU.add),
                     reads=["bk%d" % bi, "hin%d" % ob], writes=["obuf%d" % ob])
                if final:
                    S.op("act", lambda e, ob=ob: e.activation(out=sq[ob][:, 0:512], in_=obuf[ob], func=AF.Square),
                         reads=["obuf%d" % ob], writes=["sq%d" % ob])
                    S.op("pe", lambda e, ob=ob, t=t: e.matmul(bank[5 + t][:, :], lhsT=ones_t, rhs=sq[ob][:, 0:512],
                                                              start=(dc == 0), stop=(dc == DC - 1)),
                         reads=["ones_t", "sq%d" % ob], writes=["bk%d" % (5 + t)])
                S.dma("sp", lambda e, t=t, ob=ob: e.dma_start(out=dst(dc, c0 + t * 512, 512), in_=obuf[ob]),
                      reads=["obuf%d" % ob], writes=["%s_o:%d:%d:%d" % (tag, m, dc, t)])
        nxt = load_wd(0)
        for dc in range(DC):
            wb, wv = nxt
            if dc + 1 < DC:
                nxt = load_wd(dc + 1)
            down(dc, wb, wv)
        if final:
            for t in range(2):
                S.op("act", lambda e, t=t: e.activation(out=rstd[:, 0:512], in_=bank[5 + t][:, :], func=AF.Sqrt, bias=eps_t[:, 0:1],
                                                        scale=1.0 / D), reads=["bk%d" % (5 + t), "eps_t"], writes=["rstd:f"])
                S.op("dve", lambda e: e.reciprocal(out=rstd[:, 0:512], in_=rstd[:, 0:512]), reads=["rstd:f"], writes=["rstd:f"])
                for dc in range(DC):
                    ob = dc % 2
                    S.dma("sp", lambda e, dc=dc, t=t, ob=ob: e.dma_start(out=hin[ob][:, 0:512], in_=dst(dc, c0 + t * 512, 512)),
                          reads=["%s_o:%d:%d:%d" % (tag, m, dc, t)], writes=["hin%d" % ob])
                    S.op("dve", lambda e, dc=dc, ob=ob: e.scalar_tensor_tensor(
                        out=obuf[ob], in0=hin[ob][:, 0:512], scalar=gf_t[:, dc:dc + 1], in1=rstd[:, 0:512], op0=ALU.mult, op1=ALU.mult),
                        reads=["hin%d" % ob, "gf_t", "rstd:f"], writes=["obuf%d" % ob])
                    S.dma("sp", lambda e, dc=dc, t=t, ob=ob: e.dma_start(out=dst(dc, c0 + t * 512, 512), in_=obuf[ob]),
                          reads=["obuf%d" % ob], writes=["%s_o:%d:%d:%d" % (tag, m, dc, t)])
    for m in range(nmacro):
        macro(m)


def emit_moba(nc, S, AR, bank, io, T):
    nblk = T // 256
    ntile = T // 512
    ones_t, identb, identf, eps_t = io["ones_t"], io["identb"], io["identf"], io["eps_t"]
    gcol = AR.alloc([128, DC], F32)
    t31 = AR.alloc([128, MH], F32)
    emat = AR.alloc([16, 16 * 128], BF16)
    for (d_, s_, nm) in [(gcol, io["gmix1"], "gcol"), (t31, io["t31"], "t31"), (emat, io["emat"], "emat")]:
        S.dma("sp", lambda e, d_=d_, s_=s_: e.dma_start(out=d_, in_=s_), writes=[nm])
    mark = AR.off
    emit_norm(S, AR, bank, io["xT"], gcol, io["scr"], ones_t, eps_t, ntile)
    AR.off = mark
    S.fence()
    wb = [AR.alloc([128, DC, 384], BF16) for _ in range(2)]
    xn = [AR.alloc([128, DC, 512], BF16) for _ in range(2)]
    qT = AR.alloc([128, T], BF16)
    kT = AR.alloc([128, T], BF16)
    vaug = AR.alloc([128, T // 128, 129], BF16)
    kms = AR.alloc([128, 16], F32)
    kmb = AR.alloc([128, 16], BF16)
    gpad = AR.alloc([128, 16], F32)
    m8 = AR.alloc([128, 8], F32)
    mbq = AR.alloc([128, 16], F32)
    mbT = AR.alloc([16, T], BF16)
    bt = [AR.alloc([128, 1024], F32) for _ in range(2)]
    tmp = [AR.alloc([128, 256], F32) for _ in range(2)]
    pT = [AR.alloc([128, 256], BF16) for _ in range(3)]
    rinv = AR.alloc([128, 1], F32)
    ob = AR.alloc([128, 128], BF16)
    oTt = [AR.alloc([128, 256], BF16) for _ in range(2)]
    cnt = {"x": 0, "p": 0, "s": 0, "o": 0, "t": 0}
    scale = 1.0 / math.sqrt(128.0)
    w_qkv, btd, scr, oT = io["moba_w"], io["bt"], io["scr"], io["oT"]
    ptb = bank[1][:, 0:128].bitcast(BF16)
    S.op("dve", lambda e: e.memset(vaug, 1.0), writes=["vaug_ones"])
    S.op("dve", lambda e: e.memset(kms, 0.0), writes=["kms"])

    def load_w(hd):
        b = hd % 2
        S.dma("pool", lambda e: e.dma_start(out=wb[b], in_=w_qkv[hd].rearrange("(c p) n -> p c n", p=128)), writes=["wb%d" % b])
        S.dma("sp", lambda e: e.dma_start(out=bt[b], in_=btd[hd]), writes=["bt%d" % b])

    def proj_tile(hd, t):
        b = cnt["x"] % 2
        cnt["x"] += 1
        wbi = hd % 2
        w = wb[wbi]
        S.dma("sp", lambda e: e.dma_start(out=xn[b], in_=scr[:, t * 512:(t + 1) * 512].rearrange("(c p) t -> p c t", p=128)),
              reads=["scr:%d" % t], writes=["xn%d" % b])
        ts = slice(t * 512, (t + 1) * 512)
        for kc in range(DC):
            S.op("pe", lambda e, kc=kc: e.matmul(bank[0][:, :], lhsT=w[:, kc, 0:128], rhs=xn[b][:, kc, :], start=(kc == 0), stop=(kc == DC - 1)),
                 reads=["wb%d" % wbi, "xn%d" % b], writes=["bk0"])
        S.op("act", lambda e: e.activation(out=qT[:, ts], in_=bank[0][:, :], func=AF.Copy, scale=scale), reads=["bk0"], writes=["qT:%d" % t])
        for kc in range(DC):
            S.op("pe", lambda e, kc=kc: e.matmul(bank[1][:, :], lhsT=w[:, kc, 128:256], rhs=xn[b][:, kc, :], start=(kc == 0), stop=(kc == DC - 1)),
                 reads=["wb%d" % wbi, "xn%d" % b], writes=["bk1"])
        S.op("act", lambda e: e.activation(out=kT[:, ts], in_=bank[1][:, :], func=AF.Copy), reads=["bk1"], writes=["kT:%d" % t])
        for g2 in range(2):
            S.op("dve", lambda e, g2=g2: e.tensor_reduce(out=kms[:, 2 * t + g2:2 * t + g2 + 1], in_=bank[1][:, g2 * 256:(g2 + 1) * 256],
                                                         axis=AX.X, op=ALU.add),
                 reads=["bk1", "kT:%d" % t], writes=["kms"])
        for c in range(4):
            for kc in range(DC):
                S.op("pe", lambda e, kc=kc, c=c: e.matmul(bank[2][:, c * 128:(c + 1) * 128], lhsT=xn[b][:, kc, c * 128:(c + 1) * 128],
                                                           rhs=w[:, kc, 256:384], start=(kc == 0), stop=(kc == DC - 1)),
                     reads=["wb%d" % wbi, "xn%d" % b], writes=["bk2"])
        for c in range(4):
            S.op("act", lambda e, c=c: e.activation(out=vaug[:, 4 * t + c, 0:128], in_=bank[2][:, c * 128:(c + 1) * 128], func=AF.Copy),
                 reads=["bk2", "vaug_ones"], writes=["v:%d" % t])

    def gate_tile(hd, qt):
        qb = qt // 2
        qs = slice(qt * 128, (qt + 1) * 128)
        S.op("pe", lambda e: e.matmul(bank[3][:, 0:16], lhsT=qT[:, qs], rhs=kmb, start=True, stop=True),
             reads=["qT:%d" % (qt // 4), "kmb"], writes=["bk3"])
        S.op("dve", lambda e: e.memset(gpad, -1e30), writes=["gpad"])
        if qb > 0:
            S.op("dve", lambda e: e.tensor_copy(out=gpad[:, 0:qb], in_=bank[3][:, 0:qb]), reads=["bk3", "gpad"], writes=["gpad"])
        S.op("dve", lambda e: e.max(out=m8, in_=gpad), reads=["gpad"], writes=["m8"])
        S.op("dve", lambda e: e.tensor_scalar(out=mbq, in0=gpad, scalar1=m8[:, 2:3], scalar2=1.0, op0=ALU.is_ge, op1=ALU.subtract),
             reads=["gpad", "m8"], writes=["mbq"])
        S.op("act", lambda e: e.activation(out=mbq, in_=mbq, func=AF.Copy, scale=-NEG), reads=["mbq"], writes=["mbq"])
        S.op("pe", lambda e: e.transpose(out=bank[0][0:16, 0:128], in_=mbq, identity=identf), reads=["mbq", "identf"], writes=["bk0"])
        S.op("act", lambda e: e.activation(out=mbT[:, qs], in_=bank[0][0:16, 0:128], func=AF.Copy), reads=["bk0"], writes=["mbT:%d" % (qt // 2)])

    def attn_block(hd, qb):
        qsl = slice(qb * 256, (qb + 1) * 256)
        btb = bt[hd % 2]
        nch = 2 * (qb + 1)
        obk = [bank[6], bank[7]]
        oti = cnt["o"] % 2
        cnt["o"] += 1

        def kchunk(n, kc2, i):
            kch = n * 2 + kc2
            sb = cnt["s"] % 2
            cnt["s"] += 1
            sbank = bank[4 + sb]
            sname = "bk%d" % (4 + sb)
            past = n < qb
            S.op("pe", lambda e: e.matmul(sbank[:, 0:256], lhsT=kT[:, kch * 128:(kch + 1) * 128], rhs=qT[:, qsl], start=True, stop=not past),
                 reads=["kT:%d" % (kch // 4), "qT:%d" % (qb // 2)], writes=[sname])
            if past:
                S.op("pe", lambda e: e.matmul(sbank[:, 0:256], lhsT=emat[:, n * 128:(n + 1) * 128], rhs=mbT[:, qsl], start=False, stop=True),
                     reads=["emat", "mbT:%d" % qb], writes=[sname])
            pb = cnt["p"] % 3
            cnt["p"] += 1
            if n >= qb - 1:
                kind = 0 if n == qb else 1
                tb = cnt["t"] % 2
                cnt["t"] += 1
                off = (kind * 2 + kc2) * 256
                S.op("dve", lambda e: e.tensor_tensor(out=tmp[tb], in0=sbank[:, 0:256], in1=btb[:, off:off + 256], op=ALU.add),
                     reads=[sname, "bt%d" % (hd % 2)], writes=["tmp%d" % tb])
                S.op("act", lambda e: e.activation(out=pT[pb], in_=tmp[tb], func=AF.Exp), reads=["tmp%d" % tb], writes=["pT%d" % pb])
            else:
                S.op("act", lambda e: e.activation(out=pT[pb], in_=sbank[:, 0:256], func=AF.Exp, bias=t31[:, hd:hd + 1], scale=1.0),
                     reads=[sname, "t31"], writes=["pT%d" % pb])
            for qt in range(2):
                S.op("pe", lambda e, qt=qt: e.matmul(obk[qt][:, 0:129], lhsT=pT[pb][:, qt * 128:(qt + 1) * 128], rhs=vaug[:, kch, :],
                                                     start=(i == 0), stop=(i == nch - 1)),
                     reads=["pT%d" % pb, "v:%d" % (kch // 4)], writes=["bk%d" % (6 + qt)])
        i = 0
        for n in range(qb + 1):
            for kc2 in range(2):
                kchunk(n, kc2, i)
                i += 1
        for qt in range(2):
            S.op("dve", lambda e, qt=qt: e.reciprocal(out=rinv, in_=obk[qt][:, 128:129]), reads=["bk%d" % (6 + qt)], writes=["rinv"])
            S.op("act", lambda e, qt=qt: e.activation(out=ob, in_=obk[qt][:, 0:128], func=AF.Copy, scale=rinv[:, 0:1]),
                 reads=["bk%d" % (6 + qt), "rinv"], writes=["ob"])
            S.op("pe", lambda e, qt=qt: e.transpose(out=ptb[:, qt * 128:(qt + 1) * 128], in_=ob, identity=identb),
                 reads=["ob", "identb"], writes=["bk1"])
            S.op("act", lambda e, qt=qt: e.activation(out=oTt[oti][:, qt * 128:(qt + 1) * 128], in_=ptb[:, qt * 128:(qt + 1) * 128], func=AF.Copy),
                 reads=["bk1"], writes=["oTt%d" % oti])
        S.dma("sp", lambda e: e.dma_start(out=oT[hd * 128:(hd + 1) * 128, 2 + qb * 256:2 + (qb + 1) * 256], in_=oTt[oti]),
              reads=["oTt%d" % oti])

    load_w(0)
    for hd in range(MH):
        if hd + 1 < MH:
            load_w(hd + 1)
        for t in range(ntile):
            proj_tile(hd, t)
        S.op("act", lambda e: e.activation(out=kmb, in_=kms, func=AF.Copy), reads=["kms"], writes=["kmb"])
        for qt in range(T // 128):
            gate_tile(hd, qt)
        for qb in range(nblk):
            attn_block(hd, qb)


def build_fused(T=4096, phases=(1, 2, 3, 4)):
    nc = bass.Bass("TRN2", target_bir_lowering=False)
    nmacro = T // MT

    def ext(name, shape, dt=F32):
        return nc.dram_tensor(name, list(shape), dt, kind="ExternalInput").ap()
    io = dict(
        xT=ext("xT", [D, 2 + T]), ret_w=ext("ret_w", [RH, D, 1536]), gmix0=ext("gmix0", [128, DC]), gmix1=ext("gmix1", [128, DC]),
        cosT=ext("cosT", [128, T]), sinT=ext("sinT", [128, T]), dec=ext("dec", [128, RH * 128]), xi=ext("xi", [128, RH * 512]),
        zeta=ext("zeta", [128, RH]), cd=ext("cd", [128, RH]), gnrep=ext("gnrep", [128, RH * 512]),
        ret_w_out=ext("ret_w_out", [4096, D]), moba_w_out=ext("moba_w_out", [2048, D]),
        moba_w=ext("moba_w", [MH, D, 384]), bt=ext("bt", [MH, 128, 1024]), t31=ext("t31", [128, MH]), emat=ext("emat", [16, 2048], BF16),
        gf=ext("gf", [128, DC]))
    for l in range(2):
        io["w_up%d" % l] = ext("w_up%d" % l, [D, 2 * FF])
        io["w_down%d" % l] = ext("w_down%d" % l, [FF, D])
        io["gn%d" % l] = ext("gn%d" % l, [128, DC])
        io["cw%d" % l] = ext("cw%d" % l, [128, 2 * FC * 3])
        io["cb%d" % l] = ext("cb%d" % l, [128, 2 * FC])
    onesd, identbd, identfd = ext("ones", [128, 128]), ext("identb", [128, 128], BF16), ext("identf", [128, 128])
    out = nc.dram_tensor("oT", [D, T], F32, kind="ExternalOutput").ap()
    scr = nc.dram_tensor("xn_scr", [D, T], BF16).ap()
    yT_s = nc.dram_tensor("yT_s", [4096, 2 + T], BF16).ap()
    h1T_s = nc.dram_tensor("h1T_s", [D, 2 + T], F32).ap()
    o1T_s = nc.dram_tensor("o1T_s", [2048, 2 + T], BF16).ap()
    hp_d = nc.dram_tensor("hp_s", [D, nmacro * 1026], F32).ap()

    S = Sched(nc)
    bank = [nc.alloc_psum_tensor("bank%d" % i, [128, 512], F32) for i in range(8)]
    xbank = bank[7][:, :].bitcast(BF16)
    AR = Arena(nc, 206 * 1024)
    ones_t = AR.alloc([128, 128], F32)
    identb = AR.alloc([128, 128], BF16)
    identf = AR.alloc([128, 128], F32)
    eps_t = AR.alloc([128, 1], F32)
    geps_t = AR.alloc([128, 1], F32)
    for (d_, s_, nm) in [(ones_t, onesd, "ones_t"), (identb, identbd, "identb"), (identf, identfd, "identf")]:
        S.dma("sp", lambda e, d_=d_, s_=s_: e.dma_start(out=d_, in_=s_), writes=[nm])
    S.op("dve", lambda e: e.memset(eps_t, 1e-6), writes=["eps_t"])
    S.op("dve", lambda e: e.memset(geps_t, 1e-5), writes=["geps_t"])
    io.update(ones_t=ones_t, identb=identb, identf=identf, eps_t=eps_t, geps_t=geps_t, scr=scr)
    base = AR.off
    KEEP = ["ones_t", "identb", "identf", "eps_t", "geps_t"]

    if 1 in phases:
        ioa = dict(io)
        ioa.update(xT=io["xT"][:, 2:2 + T], yT=yT_s)
        emit_ret(nc, S, AR, [bank[i] for i in range(7)] + [None], xbank, ioa, T)
        S.fence(keep=KEEP)
    AR.off = base
    if 2 in phases:
        iob = dict(yT=yT_s, hT=io["xT"], w_out=io["ret_w_out"], w_up=io["w_up0"], w_down=io["w_down0"], hp_d=hp_d,
                   dst=lambda dc, col, n: h1T_s[dc * 128:(dc + 1) * 128, 2 + col:2 + col + n],
                   gn=io["gn0"], gf=io["gf"], cw=io["cw0"], cb=io["cb0"], ones_t=ones_t, eps_t=eps_t)
        emit_ffn(nc, S, AR, bank, iob, T, 4096, False, "l0")
        S.fence(keep=KEEP)
    AR.off = base
    if 3 in phases:
        ioc = dict(io)
        ioc.update(xT=h1T_s[:, 2:2 + T], oT=o1T_s)
        emit_moba(nc, S, AR, bank, ioc, T)
        S.fence(keep=KEEP)
    AR.off = base
    if 4 in phases:
        iod = dict(yT=o1T_s, hT=h1T_s, w_out=io["moba_w_out"], w_up=io["w_up1"], w_down=io["w_down1"], hp_d=hp_d,
                   dst=lambda dc, col, n: out[dc * 128:(dc + 1) * 128, col:col + n],
                   gn=io["gn1"], gf=io["gf"], cw=io["cw1"], cb=io["cb1"], ones_t=ones_t, eps_t=eps_t)
        emit_ffn(nc, S, AR, bank, iod, T, 2048, True, "l1")
    S.emit()
    nc._sched_stats = (S.sem_max, S.dma_max, S.n_ops)
    return nc


def _lay_vec(v, nch):
    return np.ascontiguousarray(np.asarray(v, np.float32).reshape(nch, 128).T)


def t5_bucket_np(rel):
    n = np.maximum(rel, 0)
    max_exact = 16
    nf = np.maximum(n, max_exact).astype(np.float32)
    large = max_exact + (np.log(nf / max_exact) / math.log(128 / max_exact) * (32 - max_exact)).astype(np.int32)
    large = np.minimum(large, 31)
    return np.where(n < max_exact, n, large)


def fused_shared_inputs(T, mix_norm, ret_w_in, ret_gn, ret_w_out, moba_w_qkv, moba_w_out, rel_bias,
                        ffn_norm, ffn_w_up, ffn_conv_w, ffn_conv_b, ffn_w_down, final_norm):
    f32 = np.float32
    w_in = np.asarray(ret_w_in[0], f32)
    gn = np.asarray(ret_gn[0], f32)
    ret_w = np.ascontiguousarray(np.stack([np.concatenate(
        [w_in[:, h * 256:(h + 1) * 256], w_in[:, 2048 + h * 256:2048 + (h + 1) * 256],
         w_in[:, 4096 + h * 512:4096 + (h + 1) * 512], w_in[:, 8192 + h * 512:8192 + (h + 1) * 512]], axis=1) for h in range(RH)]))
    wq = np.asarray(moba_w_qkv[0], f32)
    moba_w = np.ascontiguousarray(np.stack([np.concatenate(
        [wq[:, h * 128:(h + 1) * 128], wq[:, 2048 + h * 128:2048 + (h + 1) * 128], wq[:, 4096 + h * 128:4096 + (h + 1) * 128]], axis=1)
        for h in range(MH)]))
    inv = 10000.0 ** (-np.arange(128, dtype=f32) / 128)
    ang = (np.arange(T, dtype=f32)[None, :] * inv[:, None]).astype(f32)
    dec = np.zeros((128, RH * 128), f32); xi = np.zeros((128, RH * 512), f32)
    zeta = np.zeros((128, RH), f32); cd = np.zeros((128, RH), f32)
    idx = np.arange(128, dtype=np.float64)
    for h in range(RH):
        lg = np.log1p(-2.0 ** (-5.0 - h))
        diff = idx[None, :] - idx[:, None]
        dec[:, h * 128:(h + 1) * 128] = np.where(diff >= 0, np.exp(lg * np.maximum(diff, 0.0)), 0.0) / 16.0
        xi[:, h * 512:(h + 1) * 512] = np.tile(np.exp(lg * (idx + 1.0)), 4)[None, :]
        zeta[:, h] = np.exp(lg * (127.0 - idx)) / 16.0
        cd[:, h] = np.exp(lg * 128.0)
    gnrep = np.ascontiguousarray(np.broadcast_to(gn[None, :], (128, RH * 512)))
    rb = np.asarray(rel_bias, f32)
    bt = np.zeros((MH, 128, 1024), f32); t31 = np.zeros((128, MH), f32)
    key = np.arange(256)[:, None]; q = np.arange(256)[None, :]
    b_own_idx = t5_bucket_np(q - key); b_adj_idx = t5_bucket_np(q + 256 - key)
    for h in range(MH):
        b_own = np.where(q - key >= 0, rb[b_own_idx, h], f32(NEG)).astype(f32)
        b_adj = rb[b_adj_idx, h].astype(f32)
        for kind, bm in ((0, b_own), (1, b_adj)):
            for kc2 in range(2):
                bt[h, :, (kind * 2 + kc2) * 256:(kind * 2 + kc2 + 1) * 256] = bm[kc2 * 128:(kc2 + 1) * 128, :]
        t31[:, h] = rb[31, h]
    emat = np.zeros((16, 2048), f32)
    for n in range(16):
        emat[n, n * 128:(n + 1) * 128] = 1.0
    d = dict(ret_w=ret_w, gmix0=_lay_vec(mix_norm[0], 16), gmix1=_lay_vec(mix_norm[1], 16),
             cosT=np.cos(ang).astype(f32), sinT=np.sin(ang).astype(f32), dec=dec, xi=xi, zeta=zeta, cd=cd, gnrep=gnrep,
             ret_w_out=np.ascontiguousarray(ret_w_out[0], dtype=f32), moba_w_out=np.ascontiguousarray(moba_w_out[0], dtype=f32),
             moba_w=moba_w, bt=bt, t31=t31, emat=emat.astype(ml_dtypes.bfloat16), gf=_lay_vec(final_norm, 16),
             ones=np.ones((128, 128), f32), identb=np.eye(128, dtype=f32).astype(ml_dtypes.bfloat16), identf=np.eye(128, dtype=f32))
    for l in range(2):
        cw = np.asarray(ffn_conv_w[l], f32)
        d["w_up%d" % l] = np.ascontiguousarray(ffn_w_up[l], dtype=f32)
        d["w_down%d" % l] = np.ascontiguousarray(ffn_w_down[l], dtype=f32)
        d["gn%d" % l] = _lay_vec(ffn_norm[l], 16)
        d["cw%d" % l] = np.ascontiguousarray(cw.T.reshape(88, 128, 3).transpose(1, 0, 2).reshape(128, 88 * 3))
        d["cb%d" % l] = _lay_vec(ffn_conv_b[l], 88)
    return d


def x_input(xb, T):
    xT = np.zeros((2048, 2 + T), np.float32)
    xT[:, 2:] = np.asarray(xb[:T], np.float32).T
    return xT

SEQ = 4096


def kernel(x, mix_norm, ret_w_in, ret_gn, ret_w_out, moba_w_qkv, moba_w_out, rel_bias,
           ffn_norm, ffn_w_up, ffn_conv_w, ffn_conv_b, ffn_w_down, final_norm):
    x = np.asarray(x, np.float32)
    shared = fused_shared_inputs(SEQ, mix_norm, ret_w_in, ret_gn, ret_w_out, moba_w_qkv, moba_w_out, rel_bias,
                                 ffn_norm, ffn_w_up, ffn_conv_w, ffn_conv_b, ffn_w_down, final_norm)
    nc = build_fused(T=SEQ)
    xin = [x_input(x[b], SEQ) for b in range(4)]
    in_maps = [dict(xT=xin[c // 2], **shared) for c in range(8)]
    res = run_bass_kernel_spmd(nc, in_maps, core_ids=list(range(8)))
    out = np.empty((4, SEQ, 2048), np.float32)
    for b in range(4):
        out[b] = res.results[2 * b]["oT"].T
    return out
```

```python
import math
import os
import contextlib
import numpy as np
import ml_dtypes
import concourse.bass as bass
import concourse.mybir as mybir
from concourse.bass_utils import run_bass_kernel_spmd


ENGS = ("pe", "act", "dve", "pool", "sp")
N_DMA_SEMS = 12
STRICT = True


class _Op:
    __slots__ = ("eng", "fn", "deps", "signal", "sem", "val", "is_dma", "idx")

    def __init__(self, eng, fn, is_dma):
        self.eng = eng
        self.fn = fn
        self.deps = []
        self.signal = False
        self.sem = None
        self.val = 0
        self.is_dma = is_dma


class Sched:
    def __init__(self, nc):
        self.nc = nc
        self.ops = {e: [] for e in ENGS}
        self.last_w = {}
        self.readers = {}
        self.dma_rr = {e: 0 for e in ENGS}
        self.dma_last = {}

    def _add(self, eng, fn, reads, writes, is_dma):
        op = _Op(eng, fn, is_dma)
        deps = []
        for r in reads:
            w = self.last_w.get(r)
            if w is not None:
                deps.append(w)
            if r.startswith("bk") or r.startswith("xb"):
                for rd in self.readers.get(r, ()):
                    if rd.eng != eng:
                        deps.append(rd)
        for r in writes:
            w = self.last_w.get(r)
            if w is not None and (w.eng != eng or w.is_dma or is_dma or (STRICT and eng != "pe")):
                deps.append(w)
            for rd in self.readers.get(r, ()):
                if rd.eng != eng or rd.is_dma or is_dma or STRICT:
                    deps.append(rd)
        if is_dma:
            slot = self.dma_rr[eng] % N_DMA_SEMS
            self.dma_rr[eng] += 1
            prev = self.dma_last.get((eng, slot))
            if prev is not None:
                deps.append(prev)
            self.dma_last[(eng, slot)] = op
            op.sem = (eng, slot)
            op.signal = True
        seen = set()
        for d in deps:
            if id(d) in seen or d is op:
                continue
            seen.add(id(d))
            d.signal = True
            op.deps.append(d)
        for r in reads:
            lst = self.readers.setdefault(r, [])
            if not is_dma:
                lst[:] = [o for o in lst if o.is_dma or o.eng != eng]
            lst.append(op)
        for r in writes:
            self.last_w[r] = op
            self.readers[r] = []
        self.ops[eng].append(op)
        return op

    def fence(self, keep=()):
        lasts = []
        for e in ENGS:
            if e == "sp":
                continue
            for op in reversed(self.ops[e]):
                if not op.is_dma:
                    lasts.append(op)
                    break
        lasts.extend(self.dma_last.values())
        for e in ENGS:
            f = _Op(e, lambda eng: eng.nop(), False)
            for d in lasts:
                if d.eng == e and not d.is_dma:
                    continue
                d.signal = True
                f.deps.append(d)
            self.ops[e].append(f)
        self.last_w.clear()
        self.readers.clear()

    def op(self, eng, fn, reads=(), writes=()):
        return self._add(eng, fn, tuple(reads), tuple(writes), False)

    def dma(self, eng, fn, reads=(), writes=()):
        return self._add(eng, fn, tuple(reads), tuple(writes), True)

    def emit(self, final_wait_ops=()):
        nc = self.nc

        with contextlib.ExitStack() as es:
            esem = {e: es.enter_context(nc.semaphore("s_" + e)) for e in ENGS if e != "sp"}
            dsem = {}
            for e in ENGS:
                if self.dma_rr[e] > 0:
                    for s in range(min(N_DMA_SEMS, self.dma_rr[e])):
                        dsem[(e, s)] = es.enter_context(nc.semaphore("d_%s_%d" % (e, s)))
            for e in ENGS:
                c = 0
                dc = {}
                for op in self.ops[e]:
                    if op.is_dma:
                        dc[op.sem] = dc.get(op.sem, 0) + 16
                        op.val = dc[op.sem]
                        op.sem = dsem[op.sem]
                    elif op.signal:
                        c += 1
                        op.val = c
                        op.sem = esem[e]
            self.sem_max = {e: max([op.val for op in self.ops[e] if not op.is_dma] + [0]) for e in ENGS}
            self.dma_max = max([op.val for e in ENGS for op in self.ops[e] if op.is_dma] + [0])
            self.n_ops = {e: len(self.ops[e]) for e in ENGS}
            finals = {}
            for op in final_wait_ops:
                finals.setdefault(op.eng, []).append(op)
            block = es.enter_context(nc.Block())

            def run(e, eng):
                waited = {}
                for op in self.ops[e]:
                    for d in op.deps:
                        k = id(d.sem)
                        if waited.get(k, 0) < d.val:
                            eng.wait_ge(d.sem, d.val)
                            waited[k] = d.val
                    inst = op.fn(eng)
                    if op.signal:
                        inst.then_inc(op.sem, 16 if op.is_dma else 1)
                for (qe, slot), op in self.dma_last.items():
                    if qe == e:
                        eng.wait_ge(op.sem, op.val)

            if self.ops["sp"] or finals.get("sp"):
                @block.sync
                def _(eng):
                    run("sp", eng)
            if self.ops["act"]:
                @block.scalar
                def _(eng):
                    run("act", eng)
            if self.ops["dve"]:
                @block.vector
                def _(eng):
                    run("dve", eng)
            if self.ops["pool"]:
                @block.gpsimd
                def _(eng):
                    run("pool", eng)
            if self.ops["pe"]:
                @block.tensor
                def _(eng):
                    run("pe", eng)


F32 = mybir.dt.float32
BF16 = mybir.dt.bfloat16
AF = mybir.ActivationFunctionType
ALU = mybir.AluOpType
AX = mybir.AxisListType

D = 2048
DC = 16
FF = 5632
FC = 44
MT = 1024
RH = 8
MH = 16
NEG = -30000.0
_DT = {F32: 4, BF16: 2}


class Arena:
    def __init__(self, nc, nbytes):
        self.t = nc.alloc_sbuf_tensor("arena", [128, nbytes // 2], BF16)
        self.nbytes = nbytes
        self.off = 0

    def alloc(self, shape, dtype):
        size = int(np.prod(shape[1:])) * _DT[dtype]
        off = (self.off + 31) // 32 * 32
        assert off + size <= self.nbytes, ("arena overflow", off, size, self.nbytes)
        v = self.t[0:shape[0], off // 2:(off + size) // 2]
        if dtype != BF16:
            v = v.bitcast(dtype)
        if len(shape) == 3:
            v = v.rearrange("p (a b) -> p a b", a=shape[1])
        elif len(shape) == 4:
            v = v.rearrange("p (a b c) -> p a b c", a=shape[1], b=shape[2])
        self.off = off + size
        return v


def emit_norm(S, AR, bank, xT, gcol, scr, ones_t, eps_t, ntile):
    xt = AR.alloc([128, DC, 512], F32)
    xo = AR.alloc([128, DC, 512], BF16)
    sq = [AR.alloc([128, 512], F32) for _ in range(2)]
    rs = AR.alloc([128, 512], F32)

    def tile(t):
        S.dma("sp", lambda e: e.dma_start(out=xt, in_=xT[:, t * 512:(t + 1) * 512].rearrange("(c p) t -> p c t", p=128)),
              writes=["n_xt"])
        for kc in range(DC):
            sb = kc % 2
            S.op("act", lambda e, kc=kc, sb=sb: e.activation(out=sq[sb], in_=xt[:, kc, :], func=AF.Square),
                 reads=["n_xt"], writes=["n_sq%d" % sb])
            S.op("pe", lambda e, kc=kc, sb=sb: e.matmul(bank[0][:, :], lhsT=ones_t, rhs=sq[sb], start=(kc == 0), stop=(kc == DC - 1)),
                 reads=["ones_t", "n_sq%d" % sb], writes=["bk0"])
        S.op("act", lambda e: e.activation(out=rs, in_=bank[0][:, :], func=AF.Sqrt, bias=eps_t[:, 0:1], scale=1.0 / D),
             reads=["bk0", "eps_t"], writes=["n_rs"])
        S.op("dve", lambda e: e.reciprocal(out=rs, in_=rs), reads=["n_rs"], writes=["n_rs"])
        for kc in range(DC):
            S.op("dve", lambda e, kc=kc: e.scalar_tensor_tensor(out=xo[:, kc, :], in0=xt[:, kc, :], scalar=gcol[:, kc:kc + 1], in1=rs,
                                                                  op0=ALU.mult, op1=ALU.mult),
                 reads=["n_xt", "n_rs", "gcol"], writes=["n_xo"])
        S.dma("sp", lambda e: e.dma_start(out=scr[:, t * 512:(t + 1) * 512].rearrange("(c p) t -> p c t", p=128), in_=xo),
              reads=["n_xo"], writes=["scr:%d" % t])
    for t in range(ntile):
        tile(t)


def emit_ret(nc, S, AR, bank, xbank, io, T):
    ntile = T // 512
    ones_t, identb, eps_t, geps_t = io["ones_t"], io["identb"], io["eps_t"], io["geps_t"]
    gcol = AR.alloc([128, DC], F32)
    zeta_t = AR.alloc([128, RH], F32)
    cd_t = AR.alloc([128, RH], F32)
    for (dst, src, nm) in [(gcol, io["gmix0"], "gcol"), (zeta_t, io["zeta"], "zeta_t"), (cd_t, io["cd"], "cd_t")]:
        S.dma("sp", lambda e, dst=dst, src=src: e.dma_start(out=dst, in_=src), writes=[nm])
    mark = AR.off
    emit_norm(S, AR, bank, io["xT"], gcol, io["scr"], ones_t, eps_t, ntile)
    AR.off = mark
    S.fence()
    wb = [AR.alloc([128, DC, 1536], BF16) for _ in range(2)]
    xn = [AR.alloc([128, DC, 512], BF16) for _ in range(2)]
    cs_t = [AR.alloc([128, 2, 512], F32) for _ in range(2)]
    hc = [dict(dec=AR.alloc([128, 128], F32), xi=AR.alloc([128, 512], F32), gn=AR.alloc([128, 512], F32)) for _ in range(2)]
    t1 = AR.alloc([128, 512], F32)
    t2 = AR.alloc([128, 512], F32)
    qr = AR.alloc([128, 2, 512], BF16)
    qx = AR.alloc([128, 2, 512], BF16)
    kr = AR.alloc([128, 2, 512], BF16)
    ktok = AR.alloc([128, 256], BF16)
    vb = AR.alloc([128, 512], BF16)
    vz = AR.alloc([128, 512], BF16)
    sgt = AR.alloc([128, 512], F32)
    stm = AR.alloc([128, 128], BF16)
    state = AR.alloc([128, 2, 512], F32)
    state_b = AR.alloc([128, 2, 512], BF16)
    bst = AR.alloc([128, 6], F32)
    mv = AR.alloc([128, 2], F32)
    rstd = AR.alloc([128, 1], F32)
    on = AR.alloc([128, 512], F32)
    yb = AR.alloc([128, 512], BF16)
    yTt = [AR.alloc([128, 4, 512], BF16) for _ in range(2)]
    w_in, cosd, sind, yT, scr = io["ret_w"], io["cosT"], io["sinT"], io["yT"], io["scr"]
    cnt = {"x": 0, "y": 0}

    def load_w(hd):
        b = hd % 2
        for (lo, hi) in [(0, 512), (512, 1024), (1024, 1536)]:
            S.dma("pool", lambda e, lo=lo, hi=hi: e.dma_start(
                out=wb[b][:, :, lo:hi], in_=w_in[hd, :, lo:hi].rearrange("(c p) n -> p c n", p=128)),
                writes=["wb%d:%d" % (b, lo)])
        S.dma("sp", lambda e: e.dma_start(out=hc[b]["dec"], in_=io["dec"][:, hd * 128:(hd + 1) * 128]), writes=["hc%d" % b])
        S.dma("sp", lambda e: e.dma_start(out=hc[b]["xi"], in_=io["xi"][:, hd * 512:(hd + 1) * 512]), writes=["hc%d" % b])
        S.dma("sp", lambda e: e.dma_start(out=hc[b]["gn"], in_=io["gnrep"][:, hd * 512:(hd + 1) * 512]), writes=["hc%d" % b])

    def load_xn(t):
        b = cnt["x"] % 2
        cnt["x"] += 1
        S.dma("sp", lambda e: e.dma_start(out=xn[b], in_=scr[:, t * 512:(t + 1) * 512].rearrange("(c p) t -> p c t", p=128)),
              reads=["scr:%d" % t], writes=["xn%d" % b])
        return b

    def rotary(dst, csb, nm):
        pa, pb = bank[0], bank[1]
        cs = cs_t[csb][:, 0, :]
        sn = cs_t[csb][:, 1, :]
        cn = "cs_t%d" % csb
        S.op("dve", lambda e: e.tensor_tensor(out=t1, in0=pa[:, :], in1=cs, op=ALU.mult), reads=["bk0", cn], writes=["t1"])
        S.op("dve", lambda e: e.tensor_tensor(out=t2, in0=pb[:, :], in1=sn, op=ALU.mult), reads=["bk1", cn], writes=["t2"])
        S.op("pool", lambda e: e.tensor_tensor(out=dst[:, 0, :], in0=t1, in1=t2, op=ALU.subtract), reads=["t1", "t2"], writes=[nm + "0"])
        S.op("dve", lambda e: e.tensor_tensor(out=t1, in0=pb[:, :], in1=cs, op=ALU.mult), reads=["bk1", cn], writes=["t1"])
        S.op("dve", lambda e: e.tensor_tensor(out=t2, in0=pa[:, :], in1=sn, op=ALU.mult), reads=["bk0", cn], writes=["t2"])
        S.op("pool", lambda e: e.tensor_tensor(out=dst[:, 1, :], in0=t1, in1=t2, op=ALU.add), reads=["t1", "t2"], writes=[nm + "1"])

    def head_tile(hd, t, xb):
        wbi = hd % 2
        w = wb[wbi]
        hcb = hc[wbi]
        hcn = "hc%d" % wbi
        wn = ["wb%d:%d" % (wbi, lo) for lo in (0, 512, 1024)]
        xnn = "xn%d" % xb
        x_ = xn[xb]
        csb = xb
        yti = cnt["y"] % 2
        cnt["y"] += 1
        S.dma("sp", lambda e: e.dma_start(out=cs_t[csb][:, 0, :], in_=cosd[:, t * 512:(t + 1) * 512]), writes=["cs_t%d" % csb])
        S.dma("sp", lambda e: e.dma_start(out=cs_t[csb][:, 1, :], in_=sind[:, t * 512:(t + 1) * 512]), writes=["cs_t%d" % csb])
        for (col, dst, nm) in [(0, qr, "qr"), (256, kr, "kr")]:
            for dch in range(2):
                for kc in range(DC):
                    S.op("pe", lambda e, dch=dch, kc=kc, col=col: e.matmul(
                        bank[dch][:, :], lhsT=w[:, kc, col + dch * 128:col + (dch + 1) * 128], rhs=x_[:, kc, :],
                        start=(kc == 0), stop=(kc == DC - 1)),
                        reads=[wn[0], xnn], writes=["bk%d" % dch])
            rotary(dst, csb, nm)
        for dch in range(2):
            S.op("dve", lambda e, dch=dch: e.tensor_tensor(out=qx[:, dch, :], in0=qr[:, dch, :], in1=hcb["xi"], op=ALU.mult),
                 reads=["qr%d" % dch, hcn], writes=["qx%d" % dch])

        def chunk(c):
            cs = slice(c * 128, (c + 1) * 128)
            first = (t == 0 and c == 0)
            for kc in range(DC):
                S.op("pe", lambda e, kc=kc: e.matmul(bank[2][:, :], lhsT=x_[:, kc, cs], rhs=w[:, kc, 512:1024],
                                                      start=(kc == 0), stop=(kc == DC - 1)),
                     reads=[wn[1], xnn], writes=["bk2"])
            S.op("act", lambda e: e.activation(out=vb, in_=bank[2][:, :], func=AF.Copy), reads=["bk2"], writes=["vb"])
            S.op("act", lambda e: e.activation(out=vz, in_=bank[2][:, :], func=AF.Copy, scale=zeta_t[:, hd:hd + 1]),
                 reads=["bk2", "zeta_t"], writes=["vz"])
            for kc in range(DC):
                S.op("pe", lambda e, kc=kc: e.matmul(bank[3][:, :], lhsT=x_[:, kc, cs], rhs=w[:, kc, 1024:1536],
                                                      start=(kc == 0), stop=(kc == DC - 1)),
                     reads=[wn[2], xnn], writes=["bk3"])
            S.op("act", lambda e: e.activation(out=sgt, in_=bank[3][:, :], func=AF.Silu), reads=["bk3"], writes=["sgt"])
            S.op("dve", lambda e: e.tensor_tensor(out=sgt, in0=sgt, in1=hcb["gn"], op=ALU.mult), reads=["sgt", hcn], writes=["sgt"])
            for dch in range(2):
                S.op("pe", lambda e, dch=dch: e.matmul(bank[0][:, 0:128], lhsT=kr[:, dch, cs], rhs=qr[:, dch, cs],
                                                        start=(dch == 0), stop=(dch == 1)),
                     reads=["kr%d" % dch, "qr%d" % dch], writes=["bk0"])
            S.op("dve", lambda e: e.tensor_tensor(out=stm, in0=bank[0][:, 0:128], in1=hcb["dec"], op=ALU.mult),
                 reads=["bk0", hcn], writes=["stm"])
            for dch in range(2):
                S.op("pe", lambda e, dch=dch: e.transpose(out=xbank[:, dch * 128:(dch + 1) * 128], in_=kr[:, dch, cs], identity=identb),
                     reads=["kr%d" % dch, "identb"], writes=["xb"])
            S.op("act", lambda e: e.activation(out=ktok, in_=xbank[:, 0:256], func=AF.Copy), reads=["xb"], writes=["ktok"])
            S.op("pe", lambda e: e.matmul(bank[4][:, :], lhsT=stm, rhs=vb, start=True, stop=first), reads=["stm", "vb"], writes=["bk4"])
            if not first:
                for dch in range(2):
                    S.op("pe", lambda e, dch=dch: e.matmul(bank[4][:, :], lhsT=qx[:, dch, cs], rhs=state_b[:, dch, :],
                                                            start=False, stop=(dch == 1)),
                         reads=["qx%d" % dch, "state_b%d" % dch], writes=["bk4"])
            for dch in range(2):
                S.op("pe", lambda e, dch=dch: e.matmul(bank[5 + dch][:, :], lhsT=ktok[:, dch * 128:(dch + 1) * 128], rhs=vz,
                                                        start=True, stop=True),
                     reads=["ktok", "vz"], writes=["bk%d" % (5 + dch)])
                if first:
                    S.op("dve", lambda e, dch=dch: e.tensor_copy(out=state[:, dch, :], in_=bank[5 + dch][:, :]),
                         reads=["bk%d" % (5 + dch)], writes=["state%d" % dch])
                else:
                    S.op("dve", lambda e, dch=dch: e.scalar_tensor_tensor(
                        out=state[:, dch, :], in0=state[:, dch, :], scalar=cd_t[:, hd:hd + 1], in1=bank[5 + dch][:, :],
                        op0=ALU.mult, op1=ALU.add),
                        reads=["state%d" % dch, "bk%d" % (5 + dch), "cd_t"], writes=["state%d" % dch])
                S.op("act", lambda e, dch=dch: e.activation(out=state_b[:, dch, :], in_=state[:, dch, :], func=AF.Copy),
                     reads=["state%d" % dch], writes=["state_b%d" % dch])
            S.op("dve", lambda e: e.bn_stats(out=bst, in_=bank[4][:, :]), reads=["bk4"], writes=["bst"])
            S.op("dve", lambda e: e.bn_aggr(out=mv, in_=bst), reads=["bst"], writes=["mv"])
            S.op("act", lambda e: e.activation(out=rstd, in_=mv[:, 1:2], func=AF.Sqrt, bias=geps_t[:, 0:1], scale=1.0),
                 reads=["mv", "geps_t"], writes=["rstd"])
            S.op("dve", lambda e: e.reciprocal(out=rstd, in_=rstd), reads=["rstd"], writes=["rstd"])
            S.op("dve", lambda e: e.tensor_scalar(out=on, in0=bank[4][:, :], scalar1=mv[:, 0:1], scalar2=rstd[:, 0:1],
                                                  op0=ALU.subtract, op1=ALU.mult),
                 reads=["bk4", "mv", "rstd"], writes=["on"])
            S.op("dve", lambda e: e.tensor_tensor(out=yb, in0=on, in1=sgt, op=ALU.mult), reads=["on", "sgt"], writes=["yb"])
            for i in range(4):
                S.op("pe", lambda e, i=i: e.transpose(out=xbank[:, 256 + i * 128:256 + (i + 1) * 128], in_=yb[:, i * 128:(i + 1) * 128],
                                                      identity=identb),
                     reads=["yb", "identb"], writes=["xb"])
            S.op("act", lambda e: e.activation(out=yTt[yti][:, :, cs], in_=xbank[:, 256:768].rearrange("p (i t) -> p i t", i=4),
                                               func=AF.Copy),
                 reads=["xb"], writes=["yTt%d" % yti])
        for c in range(4):
            chunk(c)
        S.dma("sp", lambda e: e.dma_start(
            out=yT[hd * 512:(hd + 1) * 512, 2 + t * 512:2 + (t + 1) * 512].rearrange("(i p) t -> p i t", p=128), in_=yTt[yti]),
            reads=["yTt%d" % yti])

    load_w(0)
    for hd in range(RH):
        if hd + 1 < RH:
            load_w(hd + 1)
        xb = load_xn(0)
        for t in range(ntile):
            nxb = load_xn(t + 1) if t + 1 < ntile else None
            head_tile(hd, t, xb)
            xb = nxb


def emit_ffn(nc, S, AR, bank, io, T, VK, final, tag, variants=((None, 0),), zero_first_halo=True):
    VKC = VK // 128
    nmacro = T // MT
    yT, hT, w_out, w_up, w_down, hp_d, dst = io["yT"], io["hT"], io["w_out"], io["w_up"], io["w_down"], io["hp_d"], io["dst"]
    ones_t, eps_t = io["ones_t"], io["eps_t"]
    big = AR.alloc([128, FC * MT], BF16)
    aT = big.rearrange("p (c t) -> p c t", c=FC)
    yt = big[:, 0:VKC * 514].rearrange("p (c t) -> p c t", c=VKC)
    hp = big[:, 32 * 514:32 * 514 + DC * 514 * 2].bitcast(F32).rearrange("p (c t) -> p c t", c=DC)
    hn = AR.alloc([128, DC, 1026], BF16)
    wbuf = [AR.alloc([128, FC * 128], BF16) for _ in range(2)]
    wub = [AR.alloc([128, 2, DC, 128], BF16) for _ in range(2)]
    urow = [AR.alloc([128, 1026], F32) for _ in range(2)]
    acc = [AR.alloc([128, 1024], F32) for _ in range(2)]
    sg = AR.alloc([128, 1024], F32)
    hin = [AR.alloc([128, 514], F32) for _ in range(2)]
    sq = [AR.alloc([128, 514], F32) for _ in range(2)]
    rstd = AR.alloc([128, 514], F32)
    obuf = [AR.alloc([128, 512], F32) for _ in range(2)]
    gn_t = AR.alloc([128, DC], F32)
    gf_t = AR.alloc([128, DC], F32)
    cw_t = AR.alloc([128, 2 * FC * 3], F32)
    cb_t = AR.alloc([128, 2 * FC], F32)
    for (d_, s_, nm) in [(gn_t, io["gn"], "gn_t"), (gf_t, io["gf"], "gf_t"), (cw_t, io["cw"], "cw_t"), (cb_t, io["cb"], "cb_t")]:
        S.dma("sp", lambda e, d_=d_, s_=s_: e.dma_start(out=d_, in_=s_), writes=[nm])
    AT_ALL = ["aT%d" % j for j in range(FC)]
    HP_ALL = ["hp%d" % k for k in range(DC)]
    wslot = [0]

    def rms_finish(ps_list):
        for (p, lo, hi, bi) in ps_list:
            S.op("act", lambda e, p=p, lo=lo, hi=hi: e.activation(out=rstd[:, lo:hi], in_=p, func=AF.Sqrt, bias=eps_t[:, 0:1], scale=1.0 / D),
                 reads=["bk%d" % bi, "eps_t"], writes=["rstd:%d" % lo])
            S.op("dve", lambda e, lo=lo, hi=hi: e.reciprocal(out=rstd[:, lo:hi], in_=rstd[:, lo:hi]),
                 reads=["rstd:%d" % lo], writes=["rstd:%d" % lo])

    def macro(m):
        c0 = m * MT

        def sub(s):
            lo, hi = (0, 514) if s == 0 else (514, 1026)
            W = hi - lo
            ntiles = [(0, 2, bank[1], 1), (2, 514, bank[0], 0)] if s == 0 else [(0, 512, bank[0], 0)]
            stiles = [(0, 2, bank[3], 3), (2, 514, bank[2], 2)] if s == 0 else [(0, 512, bank[2], 2)]
            skip = 2 if (m == 0 and s == 0 and zero_first_halo) else 0
            for (cond, base) in variants:
                S.dma("sp", lambda e, cond=cond, base=base: e.dma_start(
                    out=yt[:, :, skip:W], in_=yT[:, base + c0 + lo + skip:base + c0 + hi].rearrange("(c p) t -> p c t", p=128),
                    **({} if cond is None else {"cond": cond()})),
                    writes=["yt"] + AT_ALL)
            if skip:
                S.op("pool", lambda e: e.memset(yt[:, :, 0:2], 0.0), writes=["yt"] + AT_ALL)
            for dc in range(DC):
                wb = wslot[0] % 2
                wslot[0] += 1
                wv = wbuf[wb][:, 0:VKC * 128].rearrange("p (c n) -> p c n", c=VKC)
                S.dma("pool", lambda e, wv=wv, dc=dc: e.dma_start(
                    out=wv, in_=w_out[:, dc * 128:(dc + 1) * 128].rearrange("(c p) n -> p c n", p=128)), writes=["wd%d" % wb])
                hb = dc % 2
                for (cond, base) in variants:
                    S.dma("sp", lambda e, hb=hb, dc=dc, cond=cond, base=base: e.dma_start(
                        out=hin[hb][:, skip:W], in_=hT[dc * 128:(dc + 1) * 128, base + c0 + lo + skip:base + c0 + hi],
                        **({} if cond is None else {"cond": cond()})), writes=["hin%d" % hb])
                if skip:
                    S.op("pool", lambda e, hb=hb: e.memset(hin[hb][:, 0:2], 0.0), writes=["hin%d" % hb])
                for (a, b, pb, bi) in ntiles:
                    for vc in range(VKC):
                        S.op("pe", lambda e, a=a, b=b, pb=pb, vc=vc, wv=wv: e.matmul(
                            pb[:, 0:b - a], lhsT=wv[:, vc, :], rhs=yt[:, vc, a:b], start=(vc == 0), stop=(vc == VKC - 1)),
                            reads=["wd%d" % wb, "yt"], writes=["bk%d" % bi])
                    S.op("dve", lambda e, a=a, b=b, pb=pb, dc=dc, hb=hb: e.tensor_tensor(
                        out=hp[:, dc, a:b], in0=pb[:, 0:b - a], in1=hin[hb][:, a:b], op=ALU.add),
                        reads=["bk%d" % bi, "hin%d" % hb], writes=["hp%d" % dc] + AT_ALL)
                S.op("act", lambda e, dc=dc, hb=hb: e.activation(out=sq[hb][:, 0:W], in_=hp[:, dc, 0:W], func=AF.Square),
                     reads=["hp%d" % dc], writes=["sq%d" % hb])
                for (a, b, pb, bi) in stiles:
                    S.op("pe", lambda e, a=a, b=b, pb=pb, dc=dc, hb=hb: e.matmul(
                        pb[:, 0:b - a], lhsT=ones_t, rhs=sq[hb][:, a:b], start=(dc == 0), stop=(dc == DC - 1)),
                        reads=["ones_t", "sq%d" % hb], writes=["bk%d" % bi])
            rms_finish([(pb[:, 0:b - a], a, b, bi) for (a, b, pb, bi) in stiles])
            for kc in range(DC):
                S.op("dve", lambda e, kc=kc: e.scalar_tensor_tensor(
                    out=hn[:, kc, lo:hi], in0=hp[:, kc, 0:W], scalar=gn_t[:, kc:kc + 1], in1=rstd[:, 0:W], op0=ALU.mult, op1=ALU.mult),
                    reads=["hp%d" % kc, "gn_t"] + ["rstd:%d" % a for (a, b, pb, bi) in stiles], writes=["hn%d:%d" % (kc, s)])
            S.dma("sp", lambda e: e.dma_start(
                out=hp_d[:, m * 1026 + lo:m * 1026 + hi].rearrange("(c p) t -> p c t", p=128), in_=hp[:, :, 0:W]),
                reads=HP_ALL, writes=["hpd:%d:%d" % (m, s)])
        for s in range(2):
            sub(s)

        def load_wu(j):
            ub = j % 2
            for gv in range(2):
                col = gv * FF + j * 128
                S.dma("pool", lambda e, gv=gv, col=col: e.dma_start(
                    out=wub[ub][:, gv, :, :], in_=w_up[:, col:col + 128].rearrange("(c p) n -> p c n", p=128)),
                    writes=["wu%d:%d" % (ub, gv)])
        load_wu(0)
        HN_ALL = ["hn%d:%d" % (k, s) for k in range(DC) for s in range(2)]

        def upj(j):
            ub = j % 2
            for gv in range(2):
                main = (bank[4], bank[5]) if gv == 0 else (bank[6], bank[7])
                hcol = 2 * gv
                tiles = [(0, 2, bank[1][:, hcol:hcol + 2], "bk1"), (2, 514, main[0][:, :], "bk%d" % (4 + 2 * gv)),
                         (514, 1026, main[1][:, :], "bk%d" % (5 + 2 * gv))]
                for (a, b, pap, pn) in tiles:
                    for kc in range(DC):
                        S.op("pe", lambda e, a=a, b=b, pap=pap, kc=kc, gv=gv: e.matmul(
                            pap, lhsT=wub[ub][:, gv, kc, :], rhs=hn[:, kc, a:b], start=(kc == 0), stop=(kc == DC - 1)),
                            reads=["wu%d:%d" % (ub, gv)] + HN_ALL, writes=[pn])
                ur = urow[gv]
                for (a, b, pap, pn) in tiles:
                    S.op("act", lambda e, a=a, b=b, pap=pap, ur=ur: e.activation(out=ur[:, a:b], in_=pap, func=AF.Copy),
                         reads=[pn], writes=["urow%d" % gv])
                ch = gv * FC + j
                ac = acc[gv]
                S.op("dve", lambda e, ur=ur, ac=ac, ch=ch: e.tensor_scalar(
                    out=ac, in0=ur[:, 2:1026], scalar1=cw_t[:, ch * 3 + 2:ch * 3 + 3], scalar2=cb_t[:, ch:ch + 1],
                    op0=ALU.mult, op1=ALU.add), reads=["urow%d" % gv, "cw_t", "cb_t"], writes=["acc%d" % gv])
                S.op("dve", lambda e, ur=ur, ac=ac, ch=ch: e.scalar_tensor_tensor(
                    out=ac, in0=ur[:, 1:1025], scalar=cw_t[:, ch * 3 + 1:ch * 3 + 2], in1=ac, op0=ALU.mult, op1=ALU.add),
                    reads=["urow%d" % gv, "cw_t", "acc%d" % gv], writes=["acc%d" % gv])
                S.op("dve", lambda e, ur=ur, ac=ac, ch=ch: e.scalar_tensor_tensor(
                    out=ac, in0=ur[:, 0:1024], scalar=cw_t[:, ch * 3:ch * 3 + 1], in1=ac, op0=ALU.mult, op1=ALU.add),
                    reads=["urow%d" % gv, "cw_t", "acc%d" % gv], writes=["acc%d" % gv])
                if gv == 0:
                    S.op("act", lambda e, ac=ac: e.activation(out=sg, in_=ac, func=AF.Silu), reads=["acc0"], writes=["sg"])
            S.op("dve", lambda e: e.tensor_tensor(out=aT[:, j, :], in0=sg, in1=acc[1], op=ALU.mult),
                 reads=["sg", "acc1"], writes=["aT%d" % j, "yt"] + HP_ALL)
        for j in range(FC):
            if j + 1 < FC:
                load_wu(j + 1)
            upj(j)

        def load_wd(dc):
            wb = wslot[0] % 2
            wslot[0] += 1
            wv = wbuf[wb].rearrange("p (c n) -> p c n", c=FC)
            S.dma("pool", lambda e: e.dma_start(out=wv, in_=w_down[:, dc * 128:(dc + 1) * 128].rearrange("(c p) n -> p c n", p=128)),
                  writes=["wd%d" % wb])
            return wb, wv

        def down(dc, wb, wv):
            for t in range(2):
                bi = [0, 2, 3, 4][(2 * dc + t) % 4]
                pb = bank[bi]
                ob = (2 * dc + t) % 2
                S.dma("sp", lambda e, t=t, ob=ob: e.dma_start(
                    out=hin[ob][:, 0:512], in_=hp_d[dc * 128:(dc + 1) * 128, m * 1026 + 2 + t * 512:m * 1026 + 2 + (t + 1) * 512]),
                    reads=["hpd:%d:0" % m, "hpd:%d:1" % m], writes=["hin%d" % ob])
                for fc in range(FC):
                    S.op("pe", lambda e, pb=pb, fc=fc, t=t: e.matmul(
                        pb[:, :], lhsT=wv[:, fc, :], rhs=aT[:, fc, t * 512:(t + 1) * 512], start=(fc == 0), stop=(fc == FC - 1)),
                        reads=["wd%d" % wb, "aT%d" % fc], writes=["bk%d" % bi])
                S.op("dve", lambda e, pb=pb, ob=ob: e.tensor_tensor(out=obuf[ob], in0=pb[:, :], in1=hin[ob][:, 0:512], op=ALU.add),
                     reads=["bk%d" % bi, "hin%d" % ob], writes=["obuf%d" % ob])
                if final:
                    S.op("act", lambda e, ob=ob: e.activation(out=sq[ob][:, 0:512], in_=obuf[ob], func=AF.Square),
                         reads=["obuf%d" % ob], writes=["sq%d" % ob])
                    S.op("pe", lambda e, ob=ob, t=t: e.matmul(bank[5 + t][:, :], lhsT=ones_t, rhs=sq[ob][:, 0:512],
                                                              start=(dc == 0), stop=(dc == DC - 1)),
                         reads=["ones_t", "sq%d" % ob], writes=["bk%d" % (5 + t)])
                S.dma("sp", lambda e, t=t, ob=ob: e.dma_start(out=dst(dc, c0 + t * 512, 512), in_=obuf[ob]),
                      reads=["obuf%d" % ob], writes=["%s_o:%d:%d:%d" % (tag, m, dc, t)])
        nxt = load_wd(0)
        for dc in range(DC):
            wb, wv = nxt
            if dc + 1 < DC:
                nxt = load_wd(dc + 1)
            down(dc, wb, wv)
        if final:
            for t in range(2):
                S.op("act", lambda e, t=t: e.activation(out=rstd[:, 0:512], in_=bank[5 + t][:, :], func=AF.Sqrt, bias=eps_t[:, 0:1],
                                                        scale=1.0 / D), reads=["bk%d" % (5 + t), "eps_t"], writes=["rstd:f"])
                S.op("dve", lambda e: e.reciprocal(out=rstd[:, 0:512], in_=rstd[:, 0:512]), reads=["rstd:f"], writes=["rstd:f"])
                for dc in range(DC):
                    ob = dc % 2
                    S.dma("sp", lambda e, dc=dc, t=t, ob=ob: e.dma_start(out=hin[ob][:, 0:512], in_=dst(dc, c0 + t * 512, 512)),
                          reads=["%s_o:%d:%d:%d" % (tag, m, dc, t)], writes=["hin%d" % ob])
                    S.op("dve", lambda e, dc=dc, ob=ob: e.scalar_tensor_tensor(
                        out=obuf[ob], in0=hin[ob][:, 0:512], scalar=gf_t[:, dc:dc + 1], in1=rstd[:, 0:512], op0=ALU.mult, op1=ALU.mult),
                        reads=["hin%d" % ob, "gf_t", "rstd:f"], writes=["obuf%d" % ob])
                    S.dma("sp", lambda e, dc=dc, t=t, ob=ob: e.dma_start(out=dst(dc, c0 + t * 512, 512), in_=obuf[ob]),
                          reads=["obuf%d" % ob], writes=["%s_o:%d:%d:%d" % (tag, m, dc, t)])
    for m in range(nmacro):
        macro(m)


def emit_moba(nc, S, AR, bank, io, T):
    nblk = T // 256
    ntile = T // 512
    ones_t, identb, identf, eps_t = io["ones_t"], io["identb"], io["identf"], io["eps_t"]
    gcol = AR.alloc([128, DC], F32)
    t31 = AR.alloc([128, MH], F32)
    emat = AR.alloc([16, 16 * 128], BF16)
    for (d_, s_, nm) in [(gcol, io["gmix1"], "gcol"), (t31, io["t31"], "t31"), (emat, io["emat"], "emat")]:
        S.dma("sp", lambda e, d_=d_, s_=s_: e.dma_start(out=d_, in_=s_), writes=[nm])
    mark = AR.off
    emit_norm(S, AR, bank, io["xT"], gcol, io["scr"], ones_t, eps_t, ntile)
    AR.off = mark
    S.fence()
    wb = [AR.alloc([128, DC, 384], BF16) for _ in range(2)]
    xn = [AR.alloc([128, DC, 512], BF16) for _ in range(2)]
    qT = AR.alloc([128, T], BF16)
    kT = AR.alloc([128, T], BF16)
    vaug = AR.alloc([128, T // 128, 129], BF16)
    kms = AR.alloc([128, 16], F32)
    kmb = AR.alloc([128, 16], BF16)
    gpad = AR.alloc([128, 16], F32)
    m8 = AR.alloc([128, 8], F32)
    mbq = AR.alloc([128, 16], F32)
    mbT = AR.alloc([16, T], BF16)
    bt = [AR.alloc([128, 1024], F32) for _ in range(2)]
    tmp = [AR.alloc([128, 256], F32) for _ in range(2)]
    pT = [AR.alloc([128, 256], BF16) for _ in range(3)]
    rinv = AR.alloc([128, 1], F32)
    ob = AR.alloc([128, 128], BF16)
    oTt = [AR.alloc([128, 256], BF16) for _ in range(2)]
    cnt = {"x": 0, "p": 0, "s": 0, "o": 0, "t": 0}
    scale = 1.0 / math.sqrt(128.0)
    w_qkv, btd, scr, oT = io["moba_w"], io["bt"], io["scr"], io["oT"]
    ptb = bank[1][:, 0:128].bitcast(BF16)
    S.op("dve", lambda e: e.memset(vaug, 1.0), writes=["vaug_ones"])
    S.op("dve", lambda e: e.memset(kms, 0.0), writes=["kms"])

    def load_w(hd):
        b = hd % 2
        S.dma("pool", lambda e: e.dma_start(out=wb[b], in_=w_qkv[hd].rearrange("(c p) n -> p c n", p=128)), writes=["wb%d" % b])
        S.dma("sp", lambda e: e.dma_start(out=bt[b], in_=btd[hd]), writes=["bt%d" % b])

    def proj_tile(hd, t):
        b = cnt["x"] % 2
        cnt["x"] += 1
        wbi = hd % 2
        w = wb[wbi]
        S.dma("sp", lambda e: e.dma_start(out=xn[b], in_=scr[:, t * 512:(t + 1) * 512].rearrange("(c p) t -> p c t", p=128)),
              reads=["scr:%d" % t], writes=["xn%d" % b])
        ts = slice(t * 512, (t + 1) * 512)
        for kc in range(DC):
            S.op("pe", lambda e, kc=kc: e.matmul(bank[0][:, :], lhsT=w[:, kc, 0:128], rhs=xn[b][:, kc, :], start=(kc == 0), stop=(kc == DC - 1)),
                 reads=["wb%d" % wbi, "xn%d" % b], writes=["bk0"])
        S.op("act", lambda e: e.activation(out=qT[:, ts], in_=bank[0][:, :], func=AF.Copy, scale=scale), reads=["bk0"], writes=["qT:%d" % t])
        for kc in range(DC):
            S.op("pe", lambda e, kc=kc: e.matmul(bank[1][:, :], lhsT=w[:, kc, 128:256], rhs=xn[b][:, kc, :], start=(kc == 0), stop=(kc == DC - 1)),
                 reads=["wb%d" % wbi, "xn%d" % b], writes=["bk1"])
        S.op("act", lambda e: e.activation(out=kT[:, ts], in_=bank[1][:, :], func=AF.Copy), reads=["bk1"], writes=["kT:%d" % t])
        for g2 in range(2):
            S.op("dve", lambda e, g2=g2: e.tensor_reduce(out=kms[:, 2 * t + g2:2 * t + g2 + 1], in_=bank[1][:, g2 * 256:(g2 + 1) * 256],
                                                         axis=AX.X, op=ALU.add),
                 reads=["bk1", "kT:%d" % t], writes=["kms"])
        for c in range(4):
            for kc in range(DC):
                S.op("pe", lambda e, kc=kc, c=c: e.matmul(bank[2][:, c * 128:(c + 1) * 128], lhsT=xn[b][:, kc, c * 128:(c + 1) * 128],
                                                           rhs=w[:, kc, 256:384], start=(kc == 0), stop=(kc == DC - 1)),
                     reads=["wb%d" % wbi, "xn%d" % b], writes=["bk2"])
        for c in range(4):
            S.op("act", lambda e, c=c: e.activation(out=vaug[:, 4 * t + c, 0:128], in_=bank[2][:, c * 128:(c + 1) * 128], func=AF.Copy),
                 reads=["bk2", "vaug_ones"], writes=["v:%d" % t])

    def gate_tile(hd, qt):
        qb = qt // 2
        qs = slice(qt * 128, (qt + 1) * 128)
        S.op("pe", lambda e: e.matmul(bank[3][:, 0:16], lhsT=qT[:, qs], rhs=kmb, start=True, stop=True),
             reads=["qT:%d" % (qt // 4), "kmb"], writes=["bk3"])
        S.op("dve", lambda e: e.memset(gpad, -1e30), writes=["gpad"])
        if qb > 0:
            S.op("dve", lambda e: e.tensor_copy(out=gpad[:, 0:qb], in_=bank[3][:, 0:qb]), reads=["bk3", "gpad"], writes=["gpad"])
        S.op("dve", lambda e: e.max(out=m8, in_=gpad), reads=["gpad"], writes=["m8"])
        S.op("dve", lambda e: e.tensor_scalar(out=mbq, in0=gpad, scalar1=m8[:, 2:3], scalar2=1.0, op0=ALU.is_ge, op1=ALU.subtract),
             reads=["gpad", "m8"], writes=["mbq"])
        S.op("act", lambda e: e.activation(out=mbq, in_=mbq, func=AF.Copy, scale=-NEG), reads=["mbq"], writes=["mbq"])
        S.op("pe", lambda e: e.transpose(out=bank[0][0:16, 0:128], in_=mbq, identity=identf), reads=["mbq", "identf"], writes=["bk0"])
        S.op("act", lambda e: e.activation(out=mbT[:, qs], in_=bank[0][0:16, 0:128], func=AF.Copy), reads=["bk0"], writes=["mbT:%d" % (qt // 2)])

    def attn_block(hd, qb):
        qsl = slice(qb * 256, (qb + 1) * 256)
        btb = bt[hd % 2]
        nch = 2 * (qb + 1)
        obk = [bank[6], bank[7]]
        oti = cnt["o"] % 2
        cnt["o"] += 1

        def kchunk(n, kc2, i):
            kch = n * 2 + kc2
            sb = cnt["s"] % 2
            cnt["s"] += 1
            sbank = bank[4 + sb]
            sname = "bk%d" % (4 + sb)
            past = n < qb
            S.op("pe", lambda e: e.matmul(sbank[:, 0:256], lhsT=kT[:, kch * 128:(kch + 1) * 128], rhs=qT[:, qsl], start=True, stop=not past),
                 reads=["kT:%d" % (kch // 4), "qT:%d" % (qb // 2)], writes=[sname])
            if past:
                S.op("pe", lambda e: e.matmul(sbank[:, 0:256], lhsT=emat[:, n * 128:(n + 1) * 128], rhs=mbT[:, qsl], start=False, stop=True),
                     reads=["emat", "mbT:%d" % qb], writes=[sname])
            pb = cnt["p"] % 3
            cnt["p"] += 1
            if n >= qb - 1:
                kind = 0 if n == qb else 1
                tb = cnt["t"] % 2
                cnt["t"] += 1
                off = (kind * 2 + kc2) * 256
                S.op("dve", lambda e: e.tensor_tensor(out=tmp[tb], in0=sbank[:, 0:256], in1=btb[:, off:off + 256], op=ALU.add),
                     reads=[sname, "bt%d" % (hd % 2)], writes=["tmp%d" % tb])
                S.op("act", lambda e: e.activation(out=pT[pb], in_=tmp[tb], func=AF.Exp), reads=["tmp%d" % tb], writes=["pT%d" % pb])
            else:
                S.op("act", lambda e: e.activation(out=pT[pb], in_=sbank[:, 0:256], func=AF.Exp, bias=t31[:, hd:hd + 1], scale=1.0),
                     reads=[sname, "t31"], writes=["pT%d" % pb])
            for qt in range(2):
                S.op("pe", lambda e, qt=qt: e.matmul(obk[qt][:, 0:129], lhsT=pT[pb][:, qt * 128:(qt + 1) * 128], rhs=vaug[:, kch, :],
                                                     start=(i == 0), stop=(i == nch - 1)),
                     reads=["pT%d" % pb, "v:%d" % (kch // 4)], writes=["bk%d" % (6 + qt)])
        i = 0
        for n in range(qb + 1):
            for kc2 in range(2):
                kchunk(n, kc2, i)
                i += 1
        for qt in range(2):
            S.op("dve", lambda e, qt=qt: e.reciprocal(out=rinv, in_=obk[qt][:, 128:129]), reads=["bk%d" % (6 + qt)], writes=["rinv"])
            S.op("act", lambda e, qt=qt: e.activation(out=ob, in_=obk[qt][:, 0:128], func=AF.Copy, scale=rinv[:, 0:1]),
                 reads=["bk%d" % (6 + qt), "rinv"], writes=["ob"])
            S.op("pe", lambda e, qt=qt: e.transpose(out=ptb[:, qt * 128:(qt + 1) * 128], in_=ob, identity=identb),
                 reads=["ob", "identb"], writes=["bk1"])
            S.op("act", lambda e, qt=qt: e.activation(out=oTt[oti][:, qt * 128:(qt + 1) * 128], in_=ptb[:, qt * 128:(qt + 1) * 128], func=AF.Copy),
                 reads=["bk1"], writes=["oTt%d" % oti])
        S.dma("sp", lambda e: e.dma_start(out=oT[hd * 128:(hd + 1) * 128, 2 + qb * 256:2 + (qb + 1) * 256], in_=oTt[oti]),
              reads=["oTt%d" % oti])

    load_w(0)
    for hd in range(MH):
        if hd + 1 < MH:
            load_w(hd + 1)
        for t in range(ntile):
            proj_tile(hd, t)
        S.op("act", lambda e: e.activation(out=kmb, in_=kms, func=AF.Copy), reads=["kms"], writes=["kmb"])
        for qt in range(T // 128):
            gate_tile(hd, qt)
        for qb in range(nblk):
            attn_block(hd, qb)


def build_fused(T=4096, phases=(1, 2, 3, 4), dedup=True):
    nc = bass.Bass("TRN2", target_bir_lowering=False)
    nmacro = T // MT

    def ext(name, shape, dt=F32):
        return nc.dram_tensor(name, list(shape), dt, kind="ExternalInput").ap()
    io = dict(
        xT=ext("xT", [D, 2 + T]), ret_w=ext("ret_w", [RH, D, 1536]), gmix0=ext("gmix0", [128, DC]), gmix1=ext("gmix1", [128, DC]),
        cosT=ext("cosT", [128, T]), sinT=ext("sinT", [128, T]), dec=ext("dec", [128, RH * 128]), xi=ext("xi", [128, RH * 512]),
        zeta=ext("zeta", [128, RH]), cd=ext("cd", [128, RH]), gnrep=ext("gnrep", [128, RH * 512]),
        ret_w_out=ext("ret_w_out", [4096, D]), moba_w_out=ext("moba_w_out", [2048, D]),
        moba_w=ext("moba_w", [MH, D, 384]), bt=ext("bt", [MH, 128, 1024]), t31=ext("t31", [128, MH]), emat=ext("emat", [16, 2048], BF16),
        gf=ext("gf", [128, DC]))
    for l in range(2):
        io["w_up%d" % l] = ext("w_up%d" % l, [D, 2 * FF])
        io["w_down%d" % l] = ext("w_down%d" % l, [FF, D])
        io["gn%d" % l] = ext("gn%d" % l, [128, DC])
        io["cw%d" % l] = ext("cw%d" % l, [128, 2 * FC * 3])
        io["cb%d" % l] = ext("cb%d" % l, [128, 2 * FC])
    onesd, identbd, identfd = ext("ones", [128, 128]), ext("identb", [128, 128], BF16), ext("identf", [128, 128])
    TO = T // 2 if dedup else T
    out = nc.dram_tensor("oT", [D, TO], F32, kind="ExternalOutput").ap()
    scr = nc.dram_tensor("xn_scr", [D, T], BF16).ap()
    yT_s = nc.dram_tensor("yT_s", [4096, 2 + T], BF16).ap()
    h1T_s = nc.dram_tensor("h1T_s", [D, 2 + T], F32).ap()
    o1T_s = nc.dram_tensor("o1T_s", [2048, 2 + T], BF16).ap()
    hp_d = nc.dram_tensor("hp_s", [D, nmacro * 1026], F32).ap()
    o1h_s = nc.dram_tensor("o1h_s", [2048, 2 + T // 2], BF16).ap()
    h1h_s = nc.dram_tensor("h1h_s", [D, 2 + T // 2], F32).ap()
    half = nc.partition_id() % 2
    snaps = {}

    def snap_conds(e):
        snaps[0] = e.snap(half == 0)
        snaps[1] = e.snap(half == 1)
        return e.nop()

    S = Sched(nc)
    bank = [nc.alloc_psum_tensor("bank%d" % i, [128, 512], F32) for i in range(8)]
    xbank = bank[7][:, :].bitcast(BF16)
    AR = Arena(nc, 206 * 1024)
    ones_t = AR.alloc([128, 128], F32)
    identb = AR.alloc([128, 128], BF16)
    identf = AR.alloc([128, 128], F32)
    eps_t = AR.alloc([128, 1], F32)
    geps_t = AR.alloc([128, 1], F32)
    for (d_, s_, nm) in [(ones_t, onesd, "ones_t"), (identb, identbd, "identb"), (identf, identfd, "identf")]:
        S.dma("sp", lambda e, d_=d_, s_=s_: e.dma_start(out=d_, in_=s_), writes=[nm])
    S.op("dve", lambda e: e.memset(eps_t, 1e-6), writes=["eps_t"])
    S.op("dve", lambda e: e.memset(geps_t, 1e-5), writes=["geps_t"])
    io.update(ones_t=ones_t, identb=identb, identf=identf, eps_t=eps_t, geps_t=geps_t, scr=scr)
    base = AR.off
    KEEP = ["ones_t", "identb", "identf", "eps_t", "geps_t"]

    if 1 in phases:
        ioa = dict(io)
        ioa.update(xT=io["xT"][:, 2:2 + T], yT=yT_s)
        emit_ret(nc, S, AR, [bank[i] for i in range(7)] + [None], xbank, ioa, T)
        S.fence(keep=KEEP)
    AR.off = base
    if 2 in phases:
        iob = dict(yT=yT_s, hT=io["xT"], w_out=io["ret_w_out"], w_up=io["w_up0"], w_down=io["w_down0"], hp_d=hp_d,
                   dst=lambda dc, col, n: h1T_s[dc * 128:(dc + 1) * 128, 2 + col:2 + col + n],
                   gn=io["gn0"], gf=io["gf"], cw=io["cw0"], cb=io["cb0"], ones_t=ones_t, eps_t=eps_t)
        emit_ffn(nc, S, AR, bank, iob, T, 4096, False, "l0")
        S.fence(keep=KEEP)
    AR.off = base
    if 3 in phases:
        ioc = dict(io)
        ioc.update(xT=h1T_s[:, 2:2 + T], oT=o1T_s)
        emit_moba(nc, S, AR, bank, ioc, T)
        S.fence(keep=KEEP)
    AR.off = base
    if 4 in phases:
        iod = dict(yT=o1T_s, hT=h1T_s, w_out=io["moba_w_out"], w_up=io["w_up1"], w_down=io["w_down1"], hp_d=hp_d,
                   dst=lambda dc, col, n: out[dc * 128:(dc + 1) * 128, col:col + n],
                   gn=io["gn1"], gf=io["gf"], cw=io["cw1"], cb=io["cb1"], ones_t=ones_t, eps_t=eps_t)
        if dedup:
            zt = AR.alloc([128, DC, 2], F32)
            S.op("dve", lambda e: e.memset(zt, 0.0), writes=["zt"])
            S.dma("sp", lambda e: e.dma_start(out=h1T_s[:, 0:2].rearrange("(c p) t -> p c t", p=128), in_=zt), reads=["zt"])
            S.dma("sp", lambda e: e.dma_start(out=o1T_s[:, 0:2].rearrange("(c p) t -> p c t", p=128), in_=zt.bitcast(BF16)[:, :, 0:2]),
                  reads=["zt"])
            S.fence()
            S.op("sp", snap_conds)
            TH = T // 2
            for key, base in ((0, 0), (1, TH)):
                S.dma("sp", lambda e, key=key, base=base: e.dma_start(out=o1h_s, in_=o1T_s[:, base:base + 2 + TH], cond=snaps[key]),
                      writes=["o1h"])
                S.dma("sp", lambda e, key=key, base=base: e.dma_start(out=h1h_s, in_=h1T_s[:, base:base + 2 + TH], cond=snaps[key]),
                      writes=["h1h"])
            S.fence()
            iod.update(yT=o1h_s, hT=h1h_s)
            emit_ffn(nc, S, AR, bank, iod, TH, 2048, True, "l1", zero_first_halo=False)
        else:
            emit_ffn(nc, S, AR, bank, iod, T, 2048, True, "l1")
    S.emit()
    nc._sched_stats = (S.sem_max, S.dma_max, S.n_ops)
    return nc


def _lay_vec(v, nch):
    return np.ascontiguousarray(np.asarray(v, np.float32).reshape(nch, 128).T)


def t5_bucket_np(rel):
    n = np.maximum(rel, 0)
    max_exact = 16
    nf = np.maximum(n, max_exact).astype(np.float32)
    large = max_exact + (np.log(nf / max_exact) / math.log(128 / max_exact) * (32 - max_exact)).astype(np.int32)
    large = np.minimum(large, 31)
    return np.where(n < max_exact, n, large)


def fused_shared_inputs(T, mix_norm, ret_w_in, ret_gn, ret_w_out, moba_w_qkv, moba_w_out, rel_bias,
                        ffn_norm, ffn_w_up, ffn_conv_w, ffn_conv_b, ffn_w_down, final_norm):
    f32 = np.float32
    w_in = np.asarray(ret_w_in[0], f32)
    gn = np.asarray(ret_gn[0], f32)
    ret_w = np.ascontiguousarray(np.stack([np.concatenate(
        [w_in[:, h * 256:(h + 1) * 256], w_in[:, 2048 + h * 256:2048 + (h + 1) * 256],
         w_in[:, 4096 + h * 512:4096 + (h + 1) * 512], w_in[:, 8192 + h * 512:8192 + (h + 1) * 512]], axis=1) for h in range(RH)]))
    wq = np.asarray(moba_w_qkv[0], f32)
    moba_w = np.ascontiguousarray(np.stack([np.concatenate(
        [wq[:, h * 128:(h + 1) * 128], wq[:, 2048 + h * 128:2048 + (h + 1) * 128], wq[:, 4096 + h * 128:4096 + (h + 1) * 128]], axis=1)
        for h in range(MH)]))
    inv = 10000.0 ** (-np.arange(128, dtype=f32) / 128)
    ang = (np.arange(T, dtype=f32)[None, :] * inv[:, None]).astype(f32)
    dec = np.zeros((128, RH * 128), f32); xi = np.zeros((128, RH * 512), f32)
    zeta = np.zeros((128, RH), f32); cd = np.zeros((128, RH), f32)
    idx = np.arange(128, dtype=np.float64)
    for h in range(RH):
        lg = np.log1p(-2.0 ** (-5.0 - h))
        diff = idx[None, :] - idx[:, None]
        dec[:, h * 128:(h + 1) * 128] = np.where(diff >= 0, np.exp(lg * np.maximum(diff, 0.0)), 0.0) / 16.0
        xi[:, h * 512:(h + 1) * 512] = np.tile(np.exp(lg * (idx + 1.0)), 4)[None, :]
        zeta[:, h] = np.exp(lg * (127.0 - idx)) / 16.0
        cd[:, h] = np.exp(lg * 128.0)
    gnrep = np.ascontiguousarray(np.broadcast_to(gn[None, :], (128, RH * 512)))
    rb = np.asarray(rel_bias, f32)
    bt = np.zeros((MH, 128, 1024), f32); t31 = np.zeros((128, MH), f32)
    key = np.arange(256)[:, None]; q = np.arange(256)[None, :]
    b_own_idx = t5_bucket_np(q - key); b_adj_idx = t5_bucket_np(q + 256 - key)
    for h in range(MH):
        b_own = np.where(q - key >= 0, rb[b_own_idx, h], f32(NEG)).astype(f32)
        b_adj = rb[b_adj_idx, h].astype(f32)
        for kind, bm in ((0, b_own), (1, b_adj)):
            for kc2 in range(2):
                bt[h, :, (kind * 2 + kc2) * 256:(kind * 2 + kc2 + 1) * 256] = bm[kc2 * 128:(kc2 + 1) * 128, :]
        t31[:, h] = rb[31, h]
    emat = np.zeros((16, 2048), f32)
    for n in range(16):
        emat[n, n * 128:(n + 1) * 128] = 1.0
    d = dict(ret_w=ret_w, gmix0=_lay_vec(mix_norm[0], 16), gmix1=_lay_vec(mix_norm[1], 16),
             cosT=np.cos(ang).astype(f32), sinT=np.sin(ang).astype(f32), dec=dec, xi=xi, zeta=zeta, cd=cd, gnrep=gnrep,
             ret_w_out=np.ascontiguousarray(ret_w_out[0], dtype=f32), moba_w_out=np.ascontiguousarray(moba_w_out[0], dtype=f32),
             moba_w=moba_w, bt=bt, t31=t31, emat=emat.astype(ml_dtypes.bfloat16), gf=_lay_vec(final_norm, 16),
             ones=np.ones((128, 128), f32), identb=np.eye(128, dtype=f32).astype(ml_dtypes.bfloat16), identf=np.eye(128, dtype=f32))
    for l in range(2):
        cw = np.asarray(ffn_conv_w[l], f32)
        d["w_up%d" % l] = np.ascontiguousarray(ffn_w_up[l], dtype=f32)
        d["w_down%d" % l] = np.ascontiguousarray(ffn_w_down[l], dtype=f32)
        d["gn%d" % l] = _lay_vec(ffn_norm[l], 16)
        d["cw%d" % l] = np.ascontiguousarray(cw.T.reshape(88, 128, 3).transpose(1, 0, 2).reshape(128, 88 * 3))
        d["cb%d" % l] = _lay_vec(ffn_conv_b[l], 88)
    return d


def x_input(xb, T):
    xT = np.zeros((2048, 2 + T), np.float32)
    xT[:, 2:] = np.asarray(xb[:T], np.float32).T
    return xT

SEQ = 4096


def kernel(x, mix_norm, ret_w_in, ret_gn, ret_w_out, moba_w_qkv, moba_w_out, rel_bias,
           ffn_norm, ffn_w_up, ffn_conv_w, ffn_conv_b, ffn_w_down, final_norm):
    x = np.asarray(x, np.float32)
    shared = fused_shared_inputs(SEQ, mix_norm, ret_w_in, ret_gn, ret_w_out, moba_w_qkv, moba_w_out, rel_bias,
                                 ffn_norm, ffn_w_up, ffn_conv_w, ffn_conv_b, ffn_w_down, final_norm)
    nc = build_fused(T=SEQ)
    xin = [x_input(x[b], SEQ) for b in range(4)]
    in_maps = [dict(xT=xin[c // 2], **shared) for c in range(8)]
    res = run_bass_kernel_spmd(nc, in_maps, core_ids=list(range(8)))
    out = np.empty((4, SEQ, 2048), np.float32)
    for c in range(8):
        b, half = c // 2, c % 2
        out[b, half * (SEQ // 2):(half + 1) * (SEQ // 2)] = res.results[c]["oT"].T
    return out
```

```python
import math
import os
import contextlib
import numpy as np
import ml_dtypes
import concourse.bass as bass
import concourse.mybir as mybir
from concourse.bass_utils import run_bass_kernel_spmd


ENGS = ("pe", "act", "dve", "pool", "sp")
N_DMA_SEMS = 12
STRICT = True


class _Op:
    __slots__ = ("eng", "fn", "deps", "signal", "sem", "val", "is_dma", "idx")

    def __init__(self, eng, fn, is_dma):
        self.eng = eng
        self.fn = fn
        self.deps = []
        self.signal = False
        self.sem = None
        self.val = 0
        self.is_dma = is_dma


class Sched:
    def __init__(self, nc):
        self.nc = nc
        self.ops = {e: [] for e in ENGS}
        self.last_w = {}
        self.readers = {}
        self.dma_rr = {e: 0 for e in ENGS}
        self.dma_last = {}

    def _add(self, eng, fn, reads, writes, is_dma):
        op = _Op(eng, fn, is_dma)
        deps = []
        for r in reads:
            w = self.last_w.get(r)
            if w is not None:
                deps.append(w)
            if r.startswith("bk") or r.startswith("xb"):
                for rd in self.readers.get(r, ()):
                    if rd.eng != eng:
                        deps.append(rd)
        for r in writes:
            w = self.last_w.get(r)
            if w is not None and (w.eng != eng or w.is_dma or is_dma or (STRICT and eng != "pe")):
                deps.append(w)
            for rd in self.readers.get(r, ()):
                if rd.eng != eng or rd.is_dma or is_dma or STRICT:
                    deps.append(rd)
        if is_dma:
            slot = self.dma_rr[eng] % N_DMA_SEMS
            self.dma_rr[eng] += 1
            prev = self.dma_last.get((eng, slot))
            if prev is not None:
                deps.append(prev)
            self.dma_last[(eng, slot)] = op
            op.sem = (eng, slot)
            op.signal = True
        seen = set()
        for d in deps:
            if id(d) in seen or d is op:
                continue
            seen.add(id(d))
            d.signal = True
            op.deps.append(d)
        for r in reads:
            lst = self.readers.setdefault(r, [])
            if not is_dma:
                lst[:] = [o for o in lst if o.is_dma or o.eng != eng]
            lst.append(op)
        for r in writes:
            self.last_w[r] = op
            self.readers[r] = []
        self.ops[eng].append(op)
        return op

    def fence(self, keep=()):
        lasts = []
        for e in ENGS:
            if e == "sp":
                continue
            for op in reversed(self.ops[e]):
                if not op.is_dma:
                    lasts.append(op)
                    break
        lasts.extend(self.dma_last.values())
        for e in ENGS:
            f = _Op(e, lambda eng: eng.nop(), False)
            for d in lasts:
                if d.eng == e and not d.is_dma:
                    continue
                d.signal = True
                f.deps.append(d)
            self.ops[e].append(f)
        self.last_w.clear()
        self.readers.clear()

    def op(self, eng, fn, reads=(), writes=()):
        return self._add(eng, fn, tuple(reads), tuple(writes), False)

    def dma(self, eng, fn, reads=(), writes=()):
        return self._add(eng, fn, tuple(reads), tuple(writes), True)

    def emit(self, final_wait_ops=()):
        nc = self.nc

        with contextlib.ExitStack() as es:
            esem = {e: es.enter_context(nc.semaphore("s_" + e)) for e in ENGS if e != "sp"}
            dsem = {}
            for e in ENGS:
                if self.dma_rr[e] > 0:
                    for s in range(min(N_DMA_SEMS, self.dma_rr[e])):
                        dsem[(e, s)] = es.enter_context(nc.semaphore("d_%s_%d" % (e, s)))
            for e in ENGS:
                c = 0
                dc = {}
                for op in self.ops[e]:
                    if op.is_dma:
                        dc[op.sem] = dc.get(op.sem, 0) + 16
                        op.val = dc[op.sem]
                        op.sem = dsem[op.sem]
                    elif op.signal:
                        c += 1
                        op.val = c
                        op.sem = esem[e]
            self.sem_max = {e: max([op.val for op in self.ops[e] if not op.is_dma] + [0]) for e in ENGS}
            self.dma_max = max([op.val for e in ENGS for op in self.ops[e] if op.is_dma] + [0])
            self.n_ops = {e: len(self.ops[e]) for e in ENGS}
            finals = {}
            for op in final_wait_ops:
                finals.setdefault(op.eng, []).append(op)
            block = es.enter_context(nc.Block())

            def run(e, eng):
                waited = {}
                for op in self.ops[e]:
                    for d in op.deps:
                        k = id(d.sem)
                        if waited.get(k, 0) < d.val:
                            eng.wait_ge(d.sem, d.val)
                            waited[k] = d.val
                    inst = op.fn(eng)
                    if op.signal:
                        inst.then_inc(op.sem, 16 if op.is_dma else 1)
                for (qe, slot), op in self.dma_last.items():
                    if qe == e:
                        eng.wait_ge(op.sem, op.val)

            if self.ops["sp"] or finals.get("sp"):
                @block.sync
                def _(eng):
                    run("sp", eng)
            if self.ops["act"]:
                @block.scalar
                def _(eng):
                    run("act", eng)
            if self.ops["dve"]:
                @block.vector
                def _(eng):
                    run("dve", eng)
            if self.ops["pool"]:
                @block.gpsimd
                def _(eng):
                    run("pool", eng)
            if self.ops["pe"]:
                @block.tensor
                def _(eng):
                    run("pe", eng)


F32 = mybir.dt.float32
BF16 = mybir.dt.bfloat16
AF = mybir.ActivationFunctionType
ALU = mybir.AluOpType
AX = mybir.AxisListType

D = 2048
DC = 16
FF = 5632
FC = 44
MT = 1024
RH = 8
MH = 16
NEG = -30000.0
_DT = {F32: 4, BF16: 2}


class Arena:
    def __init__(self, nc, nbytes):
        self.t = nc.alloc_sbuf_tensor("arena", [128, nbytes // 2], BF16)
        self.nbytes = nbytes
        self.off = 0

    def alloc(self, shape, dtype):
        size = int(np.prod(shape[1:])) * _DT[dtype]
        off = (self.off + 31) // 32 * 32
        assert off + size <= self.nbytes, ("arena overflow", off, size, self.nbytes)
        v = self.t[0:shape[0], off // 2:(off + size) // 2]
        if dtype != BF16:
            v = v.bitcast(dtype)
        if len(shape) == 3:
            v = v.rearrange("p (a b) -> p a b", a=shape[1])
        elif len(shape) == 4:
            v = v.rearrange("p (a b c) -> p a b c", a=shape[1], b=shape[2])
        self.off = off + size
        return v


def emit_norm(S, AR, bank, xT, gcol, scr, ones_t, eps_t, ntile):
    xt = AR.alloc([128, DC, 512], F32)
    xo = AR.alloc([128, DC, 512], BF16)
    sq = [AR.alloc([128, 512], F32) for _ in range(2)]
    rs = AR.alloc([128, 512], F32)

    def tile(t):
        S.dma("sp", lambda e: e.dma_start(out=xt, in_=xT[:, t * 512:(t + 1) * 512].rearrange("(c p) t -> p c t", p=128)),
              writes=["n_xt"])
        for kc in range(DC):
            sb = kc % 2
            S.op("act", lambda e, kc=kc, sb=sb: e.activation(out=sq[sb], in_=xt[:, kc, :], func=AF.Square),
                 reads=["n_xt"], writes=["n_sq%d" % sb])
            S.op("pe", lambda e, kc=kc, sb=sb: e.matmul(bank[0][:, :], lhsT=ones_t, rhs=sq[sb], start=(kc == 0), stop=(kc == DC - 1)),
                 reads=["ones_t", "n_sq%d" % sb], writes=["bk0"])
        S.op("act", lambda e: e.activation(out=rs, in_=bank[0][:, :], func=AF.Sqrt, bias=eps_t[:, 0:1], scale=1.0 / D),
             reads=["bk0", "eps_t"], writes=["n_rs"])
        S.op("dve", lambda e: e.reciprocal(out=rs, in_=rs), reads=["n_rs"], writes=["n_rs"])
        for kc in range(DC):
            S.op("dve", lambda e, kc=kc: e.scalar_tensor_tensor(out=xo[:, kc, :], in0=xt[:, kc, :], scalar=gcol[:, kc:kc + 1], in1=rs,
                                                                  op0=ALU.mult, op1=ALU.mult),
                 reads=["n_xt", "n_rs", "gcol"], writes=["n_xo"])
        S.dma("sp", lambda e: e.dma_start(out=scr[:, t * 512:(t + 1) * 512].rearrange("(c p) t -> p c t", p=128), in_=xo),
              reads=["n_xo"], writes=["scr:%d" % t])
    for t in range(ntile):
        tile(t)


def emit_ret(nc, S, AR, bank, xbank, io, T):
    ntile = T // 512
    ones_t, identb, eps_t, geps_t = io["ones_t"], io["identb"], io["eps_t"], io["geps_t"]
    gcol = AR.alloc([128, DC], F32)
    zeta_t = AR.alloc([128, RH], F32)
    cd_t = AR.alloc([128, RH], F32)
    for (dst, src, nm) in [(gcol, io["gmix0"], "gcol"), (zeta_t, io["zeta"], "zeta_t"), (cd_t, io["cd"], "cd_t")]:
        S.dma("sp", lambda e, dst=dst, src=src: e.dma_start(out=dst, in_=src), writes=[nm])
    mark = AR.off
    emit_norm(S, AR, bank, io["xT"], gcol, io["scr"], ones_t, eps_t, ntile)
    AR.off = mark
    S.fence()
    wb = [AR.alloc([128, DC, 1536], BF16) for _ in range(2)]
    xn = [AR.alloc([128, DC, 512], BF16) for _ in range(2)]
    cs_t = [AR.alloc([128, 2, 512], F32) for _ in range(2)]
    hc = [dict(dec=AR.alloc([128, 128], F32), xi=AR.alloc([128, 512], F32), gn=AR.alloc([128, 512], F32)) for _ in range(2)]
    t1 = AR.alloc([128, 512], F32)
    t2 = AR.alloc([128, 512], F32)
    qr = AR.alloc([128, 2, 512], BF16)
    qx = AR.alloc([128, 2, 512], BF16)
    kr = AR.alloc([128, 2, 512], BF16)
    ktok = AR.alloc([128, 256], BF16)
    vb = AR.alloc([128, 512], BF16)
    vz = AR.alloc([128, 512], BF16)
    sgt = AR.alloc([128, 512], F32)
    stm = AR.alloc([128, 128], BF16)
    state = AR.alloc([128, 2, 512], F32)
    state_b = AR.alloc([128, 2, 512], BF16)
    bst = AR.alloc([128, 6], F32)
    mv = AR.alloc([128, 2], F32)
    rstd = AR.alloc([128, 1], F32)
    on = AR.alloc([128, 512], F32)
    yb = AR.alloc([128, 512], BF16)
    yTt = [AR.alloc([128, 4, 512], BF16) for _ in range(2)]
    w_in, cosd, sind, yT, scr = io["ret_w"], io["cosT"], io["sinT"], io["yT"], io["scr"]
    cnt = {"x": 0, "y": 0}

    def load_w(hd):
        b = hd % 2
        for (lo, hi) in [(0, 512), (512, 1024), (1024, 1536)]:
            S.dma("pool", lambda e, lo=lo, hi=hi: e.dma_start(
                out=wb[b][:, :, lo:hi], in_=w_in[hd, :, lo:hi].rearrange("(c p) n -> p c n", p=128)),
                writes=["wb%d:%d" % (b, lo)])
        S.dma("sp", lambda e: e.dma_start(out=hc[b]["dec"], in_=io["dec"][:, hd * 128:(hd + 1) * 128]), writes=["hc%d" % b])
        S.dma("sp", lambda e: e.dma_start(out=hc[b]["xi"], in_=io["xi"][:, hd * 512:(hd + 1) * 512]), writes=["hc%d" % b])
        S.dma("sp", lambda e: e.dma_start(out=hc[b]["gn"], in_=io["gnrep"][:, hd * 512:(hd + 1) * 512]), writes=["hc%d" % b])

    def load_xn(t):
        b = cnt["x"] % 2
        cnt["x"] += 1
        S.dma("sp", lambda e: e.dma_start(out=xn[b], in_=scr[:, t * 512:(t + 1) * 512].rearrange("(c p) t -> p c t", p=128)),
              reads=["scr:%d" % t], writes=["xn%d" % b])
        return b

    def rotary(dst, csb, nm):
        pa, pb = bank[0], bank[1]
        cs = cs_t[csb][:, 0, :]
        sn = cs_t[csb][:, 1, :]
        cn = "cs_t%d" % csb
        S.op("dve", lambda e: e.tensor_tensor(out=t1, in0=pa[:, :], in1=cs, op=ALU.mult), reads=["bk0", cn], writes=["t1"])
        S.op("dve", lambda e: e.tensor_tensor(out=t2, in0=pb[:, :], in1=sn, op=ALU.mult), reads=["bk1", cn], writes=["t2"])
        S.op("pool", lambda e: e.tensor_tensor(out=dst[:, 0, :], in0=t1, in1=t2, op=ALU.subtract), reads=["t1", "t2"], writes=[nm + "0"])
        S.op("dve", lambda e: e.tensor_tensor(out=t1, in0=pb[:, :], in1=cs, op=ALU.mult), reads=["bk1", cn], writes=["t1"])
        S.op("dve", lambda e: e.tensor_tensor(out=t2, in0=pa[:, :], in1=sn, op=ALU.mult), reads=["bk0", cn], writes=["t2"])
        S.op("pool", lambda e: e.tensor_tensor(out=dst[:, 1, :], in0=t1, in1=t2, op=ALU.add), reads=["t1", "t2"], writes=[nm + "1"])

    def head_tile(hd, t, xb):
        wbi = hd % 2
        w = wb[wbi]
        hcb = hc[wbi]
        hcn = "hc%d" % wbi
        wn = ["wb%d:%d" % (wbi, lo) for lo in (0, 512, 1024)]
        xnn = "xn%d" % xb
        x_ = xn[xb]
        csb = xb
        yti = cnt["y"] % 2
        cnt["y"] += 1
        S.dma("sp", lambda e: e.dma_start(out=cs_t[csb][:, 0, :], in_=cosd[:, t * 512:(t + 1) * 512]), writes=["cs_t%d" % csb])
        S.dma("sp", lambda e: e.dma_start(out=cs_t[csb][:, 1, :], in_=sind[:, t * 512:(t + 1) * 512]), writes=["cs_t%d" % csb])
        for (col, dst, nm) in [(0, qr, "qr"), (256, kr, "kr")]:
            for dch in range(2):
                for kc in range(DC):
                    S.op("pe", lambda e, dch=dch, kc=kc, col=col: e.matmul(
                        bank[dch][:, :], lhsT=w[:, kc, col + dch * 128:col + (dch + 1) * 128], rhs=x_[:, kc, :],
                        start=(kc == 0), stop=(kc == DC - 1)),
                        reads=[wn[0], xnn], writes=["bk%d" % dch])
            rotary(dst, csb, nm)
        for dch in range(2):
            S.op("dve", lambda e, dch=dch: e.tensor_tensor(out=qx[:, dch, :], in0=qr[:, dch, :], in1=hcb["xi"], op=ALU.mult),
                 reads=["qr%d" % dch, hcn], writes=["qx%d" % dch])

        def chunk(c):
            cs = slice(c * 128, (c + 1) * 128)
            first = (t == 0 and c == 0)
            for kc in range(DC):
                S.op("pe", lambda e, kc=kc: e.matmul(bank[2][:, :], lhsT=x_[:, kc, cs], rhs=w[:, kc, 512:1024],
                                                      start=(kc == 0), stop=(kc == DC - 1)),
                     reads=[wn[1], xnn], writes=["bk2"])
            S.op("act", lambda e: e.activation(out=vb, in_=bank[2][:, :], func=AF.Copy), reads=["bk2"], writes=["vb"])
            S.op("act", lambda e: e.activation(out=vz, in_=bank[2][:, :], func=AF.Copy, scale=zeta_t[:, hd:hd + 1]),
                 reads=["bk2", "zeta_t"], writes=["vz"])
            for kc in range(DC):
                S.op("pe", lambda e, kc=kc: e.matmul(bank[3][:, :], lhsT=x_[:, kc, cs], rhs=w[:, kc, 1024:1536],
                                                      start=(kc == 0), stop=(kc == DC - 1)),
                     reads=[wn[2], xnn], writes=["bk3"])
            S.op("act", lambda e: e.activation(out=sgt, in_=bank[3][:, :], func=AF.Silu), reads=["bk3"], writes=["sgt"])
            S.op("dve", lambda e: e.tensor_tensor(out=sgt, in0=sgt, in1=hcb["gn"], op=ALU.mult), reads=["sgt", hcn], writes=["sgt"])
            for dch in range(2):
                S.op("pe", lambda e, dch=dch: e.matmul(bank[0][:, 0:128], lhsT=kr[:, dch, cs], rhs=qr[:, dch, cs],
                                                        start=(dch == 0), stop=(dch == 1)),
                     reads=["kr%d" % dch, "qr%d" % dch], writes=["bk0"])
            S.op("dve", lambda e: e.tensor_tensor(out=stm, in0=bank[0][:, 0:128], in1=hcb["dec"], op=ALU.mult),
                 reads=["bk0", hcn], writes=["stm"])
            for dch in range(2):
                S.op("pe", lambda e, dch=dch: e.transpose(out=xbank[:, dch * 128:(dch + 1) * 128], in_=kr[:, dch, cs], identity=identb),
                     reads=["kr%d" % dch, "identb"], writes=["xb"])
            S.op("act", lambda e: e.activation(out=ktok, in_=xbank[:, 0:256], func=AF.Copy), reads=["xb"], writes=["ktok"])
            S.op("pe", lambda e: e.matmul(bank[4][:, :], lhsT=stm, rhs=vb, start=True, stop=first), reads=["stm", "vb"], writes=["bk4"])
            if not first:
                for dch in range(2):
                    S.op("pe", lambda e, dch=dch: e.matmul(bank[4][:, :], lhsT=qx[:, dch, cs], rhs=state_b[:, dch, :],
                                                            start=False, stop=(dch == 1)),
                         reads=["qx%d" % dch, "state_b%d" % dch], writes=["bk4"])
            for dch in range(2):
                S.op("pe", lambda e, dch=dch: e.matmul(bank[5 + dch][:, :], lhsT=ktok[:, dch * 128:(dch + 1) * 128], rhs=vz,
                                                        start=True, stop=True),
                     reads=["ktok", "vz"], writes=["bk%d" % (5 + dch)])
                if first:
                    S.op("dve", lambda e, dch=dch: e.tensor_copy(out=state[:, dch, :], in_=bank[5 + dch][:, :]),
                         reads=["bk%d" % (5 + dch)], writes=["state%d" % dch])
                else:
                    S.op("dve", lambda e, dch=dch: e.scalar_tensor_tensor(
                        out=state[:, dch, :], in0=state[:, dch, :], scalar=cd_t[:, hd:hd + 1], in1=bank[5 + dch][:, :],
                        op0=ALU.mult, op1=ALU.add),
                        reads=["state%d" % dch, "bk%d" % (5 + dch), "cd_t"], writes=["state%d" % dch])
                S.op("act", lambda e, dch=dch: e.activation(out=state_b[:, dch, :], in_=state[:, dch, :], func=AF.Copy),
                     reads=["state%d" % dch], writes=["state_b%d" % dch])
            S.op("dve", lambda e: e.bn_stats(out=bst, in_=bank[4][:, :]), reads=["bk4"], writes=["bst"])
            S.op("dve", lambda e: e.bn_aggr(out=mv, in_=bst), reads=["bst"], writes=["mv"])
            S.op("act", lambda e: e.activation(out=rstd, in_=mv[:, 1:2], func=AF.Sqrt, bias=geps_t[:, 0:1], scale=1.0),
                 reads=["mv", "geps_t"], writes=["rstd"])
            S.op("dve", lambda e: e.reciprocal(out=rstd, in_=rstd), reads=["rstd"], writes=["rstd"])
            S.op("dve", lambda e: e.tensor_scalar(out=on, in0=bank[4][:, :], scalar1=mv[:, 0:1], scalar2=rstd[:, 0:1],
                                                  op0=ALU.subtract, op1=ALU.mult),
                 reads=["bk4", "mv", "rstd"], writes=["on"])
            S.op("dve", lambda e: e.tensor_tensor(out=yb, in0=on, in1=sgt, op=ALU.mult), reads=["on", "sgt"], writes=["yb"])
            for i in range(4):
                S.op("pe", lambda e, i=i: e.transpose(out=xbank[:, 256 + i * 128:256 + (i + 1) * 128], in_=yb[:, i * 128:(i + 1) * 128],
                                                      identity=identb),
                     reads=["yb", "identb"], writes=["xb"])
            S.op("act", lambda e: e.activation(out=yTt[yti][:, :, cs], in_=xbank[:, 256:768].rearrange("p (i t) -> p i t", i=4),
                                               func=AF.Copy),
                 reads=["xb"], writes=["yTt%d" % yti])
        for c in range(4):
            chunk(c)
        S.dma("sp", lambda e: e.dma_start(
            out=yT[hd * 512:(hd + 1) * 512, 2 + t * 512:2 + (t + 1) * 512].rearrange("(i p) t -> p i t", p=128), in_=yTt[yti]),
            reads=["yTt%d" % yti])

    load_w(0)
    for hd in range(RH):
        if hd + 1 < RH:
            load_w(hd + 1)
        xb = load_xn(0)
        for t in range(ntile):
            nxb = load_xn(t + 1) if t + 1 < ntile else None
            head_tile(hd, t, xb)
            xb = nxb


def emit_ffn(nc, S, AR, bank, io, T, VK, final, tag, variants=((None, 0),), zero_first_halo=True):
    VKC = VK // 128
    nmacro = T // MT
    yT, hT, w_out, w_up, w_down, hp_d, dst = io["yT"], io["hT"], io["w_out"], io["w_up"], io["w_down"], io["hp_d"], io["dst"]
    ones_t, eps_t = io["ones_t"], io["eps_t"]
    big = AR.alloc([128, FC * MT], BF16)
    aT = big.rearrange("p (c t) -> p c t", c=FC)
    yt = big[:, 0:VKC * 514].rearrange("p (c t) -> p c t", c=VKC)
    hp = big[:, 32 * 514:32 * 514 + DC * 514 * 2].bitcast(F32).rearrange("p (c t) -> p c t", c=DC)
    hn = AR.alloc([128, DC, 1026], BF16)
    wbuf = [AR.alloc([128, FC * 128], BF16) for _ in range(2)]
    wub = [AR.alloc([128, 2, DC, 128], BF16) for _ in range(2)]
    urow = [AR.alloc([128, 1026], F32) for _ in range(2)]
    acc = [AR.alloc([128, 1024], F32) for _ in range(2)]
    sg = AR.alloc([128, 1024], F32)
    hin = [AR.alloc([128, 514], F32) for _ in range(2)]
    sq = [AR.alloc([128, 514], F32) for _ in range(2)]
    rstd = AR.alloc([128, 514], F32)
    obuf = [AR.alloc([128, 512], F32) for _ in range(2)]
    gn_t = AR.alloc([128, DC], F32)
    gf_t = AR.alloc([128, DC], F32)
    cw_t = AR.alloc([128, 2 * FC * 3], F32)
    cb_t = AR.alloc([128, 2 * FC], F32)
    for (d_, s_, nm) in [(gn_t, io["gn"], "gn_t"), (gf_t, io["gf"], "gf_t"), (cw_t, io["cw"], "cw_t"), (cb_t, io["cb"], "cb_t")]:
        S.dma("sp", lambda e, d_=d_, s_=s_: e.dma_start(out=d_, in_=s_), writes=[nm])
    AT_ALL = ["aT%d" % j for j in range(FC)]
    HP_ALL = ["hp%d" % k for k in range(DC)]
    wslot = [0]

    def rms_finish(ps_list):
        for (p, lo, hi, bi) in ps_list:
            S.op("act", lambda e, p=p, lo=lo, hi=hi: e.activation(out=rstd[:, lo:hi], in_=p, func=AF.Sqrt, bias=eps_t[:, 0:1], scale=1.0 / D),
                 reads=["bk%d" % bi, "eps_t"], writes=["rstd:%d" % lo])
            S.op("dve", lambda e, lo=lo, hi=hi: e.reciprocal(out=rstd[:, lo:hi], in_=rstd[:, lo:hi]),
                 reads=["rstd:%d" % lo], writes=["rstd:%d" % lo])

    def macro(m):
        c0 = m * MT

        def sub(s):
            lo, hi = (0, 514) if s == 0 else (514, 1026)
            W = hi - lo
            ntiles = [(0, 2, bank[1], 1), (2, 514, bank[0], 0)] if s == 0 else [(0, 512, bank[0], 0)]
            stiles = [(0, 2, bank[3], 3), (2, 514, bank[2], 2)] if s == 0 else [(0, 512, bank[2], 2)]
            skip = 2 if (m == 0 and s == 0 and zero_first_halo) else 0
            for (cond, base) in variants:
                S.dma("sp", lambda e, cond=cond, base=base: e.dma_start(
                    out=yt[:, :, skip:W], in_=yT[:, base + c0 + lo + skip:base + c0 + hi].rearrange("(c p) t -> p c t", p=128),
                    **({} if cond is None else {"cond": cond()})),
                    writes=["yt"] + AT_ALL)
            if skip:
                S.op("pool", lambda e: e.memset(yt[:, :, 0:2], 0.0), writes=["yt"] + AT_ALL)
            for dc in range(DC):
                wb = wslot[0] % 2
                wslot[0] += 1
                wv = wbuf[wb][:, 0:VKC * 128].rearrange("p (c n) -> p c n", c=VKC)
                S.dma("pool", lambda e, wv=wv, dc=dc: e.dma_start(
                    out=wv, in_=w_out[:, dc * 128:(dc + 1) * 128].rearrange("(c p) n -> p c n", p=128)), writes=["wd%d" % wb])
                hb = dc % 2
                for (cond, base) in variants:
                    S.dma("sp", lambda e, hb=hb, dc=dc, cond=cond, base=base: e.dma_start(
                        out=hin[hb][:, skip:W], in_=hT[dc * 128:(dc + 1) * 128, base + c0 + lo + skip:base + c0 + hi],
                        **({} if cond is None else {"cond": cond()})), writes=["hin%d" % hb])
                if skip:
                    S.op("pool", lambda e, hb=hb: e.memset(hin[hb][:, 0:2], 0.0), writes=["hin%d" % hb])
                for (a, b, pb, bi) in ntiles:
                    for vc in range(VKC):
                        S.op("pe", lambda e, a=a, b=b, pb=pb, vc=vc, wv=wv: e.matmul(
                            pb[:, 0:b - a], lhsT=wv[:, vc, :], rhs=yt[:, vc, a:b], start=(vc == 0), stop=(vc == VKC - 1)),
                            reads=["wd%d" % wb, "yt"], writes=["bk%d" % bi])
                    S.op("dve", lambda e, a=a, b=b, pb=pb, dc=dc, hb=hb: e.tensor_tensor(
                        out=hp[:, dc, a:b], in0=pb[:, 0:b - a], in1=hin[hb][:, a:b], op=ALU.add),
                        reads=["bk%d" % bi, "hin%d" % hb], writes=["hp%d" % dc] + AT_ALL)
                S.op("act", lambda e, dc=dc, hb=hb: e.activation(out=sq[hb][:, 0:W], in_=hp[:, dc, 0:W], func=AF.Square),
                     reads=["hp%d" % dc], writes=["sq%d" % hb])
                for (a, b, pb, bi) in stiles:
                    S.op("pe", lambda e, a=a, b=b, pb=pb, dc=dc, hb=hb: e.matmul(
                        pb[:, 0:b - a], lhsT=ones_t, rhs=sq[hb][:, a:b], start=(dc == 0), stop=(dc == DC - 1)),
                        reads=["ones_t", "sq%d" % hb], writes=["bk%d" % bi])
            rms_finish([(pb[:, 0:b - a], a, b, bi) for (a, b, pb, bi) in stiles])
            for kc in range(DC):
                S.op("dve", lambda e, kc=kc: e.scalar_tensor_tensor(
                    out=hn[:, kc, lo:hi], in0=hp[:, kc, 0:W], scalar=gn_t[:, kc:kc + 1], in1=rstd[:, 0:W], op0=ALU.mult, op1=ALU.mult),
                    reads=["hp%d" % kc, "gn_t"] + ["rstd:%d" % a for (a, b, pb, bi) in stiles], writes=["hn%d:%d" % (kc, s)])
            S.dma("sp", lambda e: e.dma_start(
                out=hp_d[:, m * 1026 + lo:m * 1026 + hi].rearrange("(c p) t -> p c t", p=128), in_=hp[:, :, 0:W]),
                reads=HP_ALL, writes=["hpd:%d:%d" % (m, s)])
        for s in range(2):
            sub(s)

        def load_wu(j):
            ub = j % 2
            for gv in range(2):
                col = gv * FF + j * 128
                S.dma("pool", lambda e, gv=gv, col=col: e.dma_start(
                    out=wub[ub][:, gv, :, :], in_=w_up[:, col:col + 128].rearrange("(c p) n -> p c n", p=128)),
                    writes=["wu%d:%d" % (ub, gv)])
        load_wu(0)
        HN_ALL = ["hn%d:%d" % (k, s) for k in range(DC) for s in range(2)]

        def upj(j):
            ub = j % 2
            for gv in range(2):
                main = (bank[4], bank[5]) if gv == 0 else (bank[6], bank[7])
                hcol = 2 * gv
                tiles = [(0, 2, bank[1][:, hcol:hcol + 2], "bk1"), (2, 514, main[0][:, :], "bk%d" % (4 + 2 * gv)),
                         (514, 1026, main[1][:, :], "bk%d" % (5 + 2 * gv))]
                for (a, b, pap, pn) in tiles:
                    for kc in range(DC):
                        S.op("pe", lambda e, a=a, b=b, pap=pap, kc=kc, gv=gv: e.matmul(
                            pap, lhsT=wub[ub][:, gv, kc, :], rhs=hn[:, kc, a:b], start=(kc == 0), stop=(kc == DC - 1)),
                            reads=["wu%d:%d" % (ub, gv)] + HN_ALL, writes=[pn])
                ur = urow[gv]
                for (a, b, pap, pn) in tiles:
                    S.op("act", lambda e, a=a, b=b, pap=pap, ur=ur: e.activation(out=ur[:, a:b], in_=pap, func=AF.Copy),
                         reads=[pn], writes=["urow%d" % gv])
                ch = gv * FC + j
                ac = acc[gv]
                S.op("dve", lambda e, ur=ur, ac=ac, ch=ch: e.tensor_scalar(
                    out=ac, in0=ur[:, 2:1026], scalar1=cw_t[:, ch * 3 + 2:ch * 3 + 3], scalar2=cb_t[:, ch:ch + 1],
                    op0=ALU.mult, op1=ALU.add), reads=["urow%d" % gv, "cw_t", "cb_t"], writes=["acc%d" % gv])
                S.op("dve", lambda e, ur=ur, ac=ac, ch=ch: e.scalar_tensor_tensor(
                    out=ac, in0=ur[:, 1:1025], scalar=cw_t[:, ch * 3 + 1:ch * 3 + 2], in1=ac, op0=ALU.mult, op1=ALU.add),
                    reads=["urow%d" % gv, "cw_t", "acc%d" % gv], writes=["acc%d" % gv])
                S.op("dve", lambda e, ur=ur, ac=ac, ch=ch: e.scalar_tensor_tensor(
                    out=ac, in0=ur[:, 0:1024], scalar=cw_t[:, ch * 3:ch * 3 + 1], in1=ac, op0=ALU.mult, op1=ALU.add),
                    reads=["urow%d" % gv, "cw_t", "acc%d" % gv], writes=["acc%d" % gv])
                if gv == 0:
                    S.op("act", lambda e, ac=ac: e.activation(out=sg, in_=ac, func=AF.Silu), reads=["acc0"], writes=["sg"])
            S.op("dve", lambda e: e.tensor_tensor(out=aT[:, j, :], in0=sg, in1=acc[1], op=ALU.mult),
                 reads=["sg", "acc1"], writes=["aT%d" % j, "yt"] + HP_ALL)
        for j in range(FC):
            if j + 1 < FC:
                load_wu(j + 1)
            upj(j)

        def load_wd(dc):
            wb = wslot[0] % 2
            wslot[0] += 1
            wv = wbuf[wb].rearrange("p (c n) -> p c n", c=FC)
            S.dma("pool", lambda e: e.dma_start(out=wv, in_=w_down[:, dc * 128:(dc + 1) * 128].rearrange("(c p) n -> p c n", p=128)),
                  writes=["wd%d" % wb])
            return wb, wv

        def down(dc, wb, wv):
            for t in range(2):
                bi = [0, 2, 3, 4][(2 * dc + t) % 4]
                pb = bank[bi]
                ob = (2 * dc + t) % 2
                S.dma("sp", lambda e, t=t, ob=ob: e.dma_start(
                    out=hin[ob][:, 0:512], in_=hp_d[dc * 128:(dc + 1) * 128, m * 1026 + 2 + t * 512:m * 1026 + 2 + (t + 1) * 512]),
                    reads=["hpd:%d:0" % m, "hpd:%d:1" % m], writes=["hin%d" % ob])
                for fc in range(FC):
                    S.op("pe", lambda e, pb=pb, fc=fc, t=t: e.matmul(
                        pb[:, :], lhsT=wv[:, fc, :], rhs=aT[:, fc, t * 512:(t + 1) * 512], start=(fc == 0), stop=(fc == FC - 1)),
                        reads=["wd%d" % wb, "aT%d" % fc], writes=["bk%d" % bi])
                S.op("dve", lambda e, pb=pb, ob=ob: e.tensor_tensor(out=obuf[ob], in0=pb[:, :], in1=hin[ob][:, 0:512], op=ALU.add),
                     reads=["bk%d" % bi, "hin%d" % ob], writes=["obuf%d" % ob])
                if final:
                    S.op("act", lambda e, ob=ob: e.activation(out=sq[ob][:, 0:512], in_=obuf[ob], func=AF.Square),
                         reads=["obuf%d" % ob], writes=["sq%d" % ob])
                    S.op("pe", lambda e, ob=ob, t=t: e.matmul(bank[5 + t][:, :], lhsT=ones_t, rhs=sq[ob][:, 0:512],
                                                              start=(dc == 0), stop=(dc == DC - 1)),
                         reads=["ones_t", "sq%d" % ob], writes=["bk%d" % (5 + t)])
                S.dma("sp", lambda e, t=t, ob=ob: e.dma_start(out=dst(dc, c0 + t * 512, 512), in_=obuf[ob]),
                      reads=["obuf%d" % ob], writes=["%s_o:%d:%d:%d" % (tag, m, dc, t)])
        nxt = load_wd(0)
        for dc in range(DC):
            wb, wv = nxt
            if dc + 1 < DC:
                nxt = load_wd(dc + 1)
            down(dc, wb, wv)
        if final:
            for t in range(2):
                S.op("act", lambda e, t=t: e.activation(out=rstd[:, 0:512], in_=bank[5 + t][:, :], func=AF.Sqrt, bias=eps_t[:, 0:1],
                                                        scale=1.0 / D), reads=["bk%d" % (5 + t), "eps_t"], writes=["rstd:f"])
                S.op("dve", lambda e: e.reciprocal(out=rstd[:, 0:512], in_=rstd[:, 0:512]), reads=["rstd:f"], writes=["rstd:f"])
                for dc in range(DC):
                    ob = dc % 2
                    S.dma("sp", lambda e, dc=dc, t=t, ob=ob: e.dma_start(out=hin[ob][:, 0:512], in_=dst(dc, c0 + t * 512, 512)),
                          reads=["%s_o:%d:%d:%d" % (tag, m, dc, t)], writes=["hin%d" % ob])
                    S.op("dve", lambda e, dc=dc, ob=ob: e.scalar_tensor_tensor(
                        out=obuf[ob], in0=hin[ob][:, 0:512], scalar=gf_t[:, dc:dc + 1], in1=rstd[:, 0:512], op0=ALU.mult, op1=ALU.mult),
                        reads=["hin%d" % ob, "gf_t", "rstd:f"], writes=["obuf%d" % ob])
                    S.dma("sp", lambda e, dc=dc, t=t, ob=ob: e.dma_start(out=dst(dc, c0 + t * 512, 512), in_=obuf[ob]),
                          reads=["obuf%d" % ob], writes=["%s_o:%d:%d:%d" % (tag, m, dc, t)])
    for m in range(nmacro):
        macro(m)


def emit_moba(nc, S, AR, bank, io, T):
    nblk = T // 256
    ntile = T // 512
    ones_t, identb, identf, eps_t = io["ones_t"], io["identb"], io["identf"], io["eps_t"]
    gcol = AR.alloc([128, DC], F32)
    t31 = AR.alloc([128, MH], F32)
    emat = AR.alloc([16, 16 * 128], BF16)
    for (d_, s_, nm) in [(gcol, io["gmix1"], "gcol"), (t31, io["t31"], "t31"), (emat, io["emat"], "emat")]:
        S.dma("sp", lambda e, d_=d_, s_=s_: e.dma_start(out=d_, in_=s_), writes=[nm])
    mark = AR.off
    emit_norm(S, AR, bank, io["xT"], gcol, io["scr"], ones_t, eps_t, ntile)
    AR.off = mark
    S.fence()
    wb = [AR.alloc([128, DC, 384], BF16) for _ in range(2)]
    xn = [AR.alloc([128, DC, 512], BF16) for _ in range(2)]
    qT = AR.alloc([128, T], BF16)
    kT = AR.alloc([128, T], BF16)
    vaug = AR.alloc([128, T // 128, 129], BF16)
    kms = AR.alloc([128, 16], F32)
    kmb = AR.alloc([128, 16], BF16)
    gpad = AR.alloc([128, 16], F32)
    m8 = AR.alloc([128, 8], F32)
    mbq = AR.alloc([128, 16], F32)
    mbT = AR.alloc([16, T], BF16)
    bt = [AR.alloc([128, 1024], F32) for _ in range(2)]
    tmp = [AR.alloc([128, 256], F32) for _ in range(3)]
    pT = [AR.alloc([128, 256], BF16) for _ in range(4)]
    rinv = AR.alloc([128, 1], F32)
    ob = AR.alloc([128, 128], BF16)
    oTt = [AR.alloc([128, 256], BF16) for _ in range(2)]
    cnt = {"x": 0, "p": 0, "s": 0, "o": 0, "t": 0}
    scale = 1.0 / math.sqrt(128.0)
    w_qkv, btd, scr, oT = io["moba_w"], io["bt"], io["scr"], io["oT"]
    ptb = bank[1][:, 0:128].bitcast(BF16)
    S.op("dve", lambda e: e.memset(vaug, 1.0), writes=["vaug_ones"])
    S.op("dve", lambda e: e.memset(kms, 0.0), writes=["kms"])

    def load_w(hd):
        b = hd % 2
        S.dma("pool", lambda e: e.dma_start(out=wb[b], in_=w_qkv[hd].rearrange("(c p) n -> p c n", p=128)), writes=["wb%d" % b])
        S.dma("sp", lambda e: e.dma_start(out=bt[b], in_=btd[hd]), writes=["bt%d" % b])

    def proj_tile(hd, t):
        b = cnt["x"] % 2
        cnt["x"] += 1
        wbi = hd % 2
        w = wb[wbi]
        S.dma("sp", lambda e: e.dma_start(out=xn[b], in_=scr[:, t * 512:(t + 1) * 512].rearrange("(c p) t -> p c t", p=128)),
              reads=["scr:%d" % t], writes=["xn%d" % b])
        ts = slice(t * 512, (t + 1) * 512)
        for kc in range(DC):
            S.op("pe", lambda e, kc=kc: e.matmul(bank[0][:, :], lhsT=w[:, kc, 0:128], rhs=xn[b][:, kc, :], start=(kc == 0), stop=(kc == DC - 1)),
                 reads=["wb%d" % wbi, "xn%d" % b], writes=["bk0"])
        S.op("act", lambda e: e.activation(out=qT[:, ts], in_=bank[0][:, :], func=AF.Copy, scale=scale), reads=["bk0"], writes=["qT:%d" % t])
        for kc in range(DC):
            S.op("pe", lambda e, kc=kc: e.matmul(bank[1][:, :], lhsT=w[:, kc, 128:256], rhs=xn[b][:, kc, :], start=(kc == 0), stop=(kc == DC - 1)),
                 reads=["wb%d" % wbi, "xn%d" % b], writes=["bk1"])
        S.op("act", lambda e: e.activation(out=kT[:, ts], in_=bank[1][:, :], func=AF.Copy), reads=["bk1"], writes=["kT:%d" % t])
        for g2 in range(2):
            S.op("dve", lambda e, g2=g2: e.tensor_reduce(out=kms[:, 2 * t + g2:2 * t + g2 + 1], in_=bank[1][:, g2 * 256:(g2 + 1) * 256],
                                                         axis=AX.X, op=ALU.add),
                 reads=["bk1", "kT:%d" % t], writes=["kms"])
        for c in range(4):
            for kc in range(DC):
                S.op("pe", lambda e, kc=kc, c=c: e.matmul(bank[2][:, c * 128:(c + 1) * 128], lhsT=xn[b][:, kc, c * 128:(c + 1) * 128],
                                                           rhs=w[:, kc, 256:384], start=(kc == 0), stop=(kc == DC - 1)),
                     reads=["wb%d" % wbi, "xn%d" % b], writes=["bk2"])
        for c in range(4):
            S.op("act", lambda e, c=c: e.activation(out=vaug[:, 4 * t + c, 0:128], in_=bank[2][:, c * 128:(c + 1) * 128], func=AF.Copy),
                 reads=["bk2", "vaug_ones"], writes=["v:%d" % t])

    def gate_tile(hd, qt):
        qb = qt // 2
        qs = slice(qt * 128, (qt + 1) * 128)
        S.op("pe", lambda e: e.matmul(bank[3][:, 0:16], lhsT=qT[:, qs], rhs=kmb, start=True, stop=True),
             reads=["qT:%d" % (qt // 4), "kmb"], writes=["bk3"])
        S.op("dve", lambda e: e.memset(gpad, -1e30), writes=["gpad"])
        if qb > 0:
            S.op("dve", lambda e: e.tensor_copy(out=gpad[:, 0:qb], in_=bank[3][:, 0:qb]), reads=["bk3", "gpad"], writes=["gpad"])
        S.op("dve", lambda e: e.max(out=m8, in_=gpad), reads=["gpad"], writes=["m8"])
        S.op("dve", lambda e: e.tensor_scalar(out=mbq, in0=gpad, scalar1=m8[:, 2:3], scalar2=1.0, op0=ALU.is_ge, op1=ALU.subtract),
             reads=["gpad", "m8"], writes=["mbq"])
        S.op("act", lambda e: e.activation(out=mbq, in_=mbq, func=AF.Copy, scale=-NEG), reads=["mbq"], writes=["mbq"])
        S.op("pe", lambda e: e.transpose(out=bank[0][0:16, 0:128], in_=mbq, identity=identf), reads=["mbq", "identf"], writes=["bk0"])
        S.op("act", lambda e: e.activation(out=mbT[:, qs], in_=bank[0][0:16, 0:128], func=AF.Copy), reads=["bk0"], writes=["mbT:%d" % (qt // 2)])

    def attn_block(hd, qb):
        qsl = slice(qb * 256, (qb + 1) * 256)
        btb = bt[hd % 2]
        nch = 2 * (qb + 1)
        obk = [bank[6], bank[7]]
        oti = cnt["o"] % 2
        cnt["o"] += 1

        SB = [4, 5, 2, 3]

        def score(n, kc2):
            kch = n * 2 + kc2
            bi = SB[cnt["s"] % 4]
            cnt["s"] += 1
            sbank = bank[bi]
            sname = "bk%d" % bi
            past = n < qb
            S.op("pe", lambda e: e.matmul(sbank[:, 0:256], lhsT=kT[:, kch * 128:(kch + 1) * 128], rhs=qT[:, qsl], start=True, stop=not past),
                 reads=["kT:%d" % (kch // 4), "qT:%d" % (qb // 2)], writes=[sname])
            if past:
                S.op("pe", lambda e: e.matmul(sbank[:, 0:256], lhsT=emat[:, n * 128:(n + 1) * 128], rhs=mbT[:, qsl], start=False, stop=True),
                     reads=["emat", "mbT:%d" % qb], writes=[sname])
            pb = cnt["p"] % 4
            cnt["p"] += 1
            if n >= qb - 1:
                kind = 0 if n == qb else 1
                tb = cnt["t"] % 3
                cnt["t"] += 1
                off = (kind * 2 + kc2) * 256
                S.op("dve", lambda e: e.tensor_tensor(out=tmp[tb], in0=sbank[:, 0:256], in1=btb[:, off:off + 256], op=ALU.add),
                     reads=[sname, "bt%d" % (hd % 2)], writes=["tmp%d" % tb])
                S.op("act", lambda e: e.activation(out=pT[pb], in_=tmp[tb], func=AF.Exp), reads=["tmp%d" % tb], writes=["pT%d" % pb])
            else:
                S.op("act", lambda e: e.activation(out=pT[pb], in_=sbank[:, 0:256], func=AF.Exp, bias=t31[:, hd:hd + 1], scale=1.0),
                     reads=[sname, "t31"], writes=["pT%d" % pb])
            return kch, pb

        def pv(i, kch, pb):
            for qt in range(2):
                S.op("pe", lambda e, qt=qt: e.matmul(obk[qt][:, 0:129], lhsT=pT[pb][:, qt * 128:(qt + 1) * 128], rhs=vaug[:, kch, :],
                                                     start=(i == 0), stop=(i == nch - 1)),
                     reads=["pT%d" % pb, "v:%d" % (kch // 4)], writes=["bk%d" % (6 + qt)])
        chunks = [(n, kc2) for n in range(qb + 1) for kc2 in range(2)]
        LA = 3
        info = {}
        for i in range(min(LA, nch)):
            info[i] = score(*chunks[i])
        for i in range(nch):
            pv(i, *info[i])
            if i + LA < nch:
                info[i + LA] = score(*chunks[i + LA])
        for qt in range(2):
            S.op("dve", lambda e, qt=qt: e.reciprocal(out=rinv, in_=obk[qt][:, 128:129]), reads=["bk%d" % (6 + qt)], writes=["rinv"])
            S.op("act", lambda e, qt=qt: e.activation(out=ob, in_=obk[qt][:, 0:128], func=AF.Copy, scale=rinv[:, 0:1]),
                 reads=["bk%d" % (6 + qt), "rinv"], writes=["ob"])
            S.op("pe", lambda e, qt=qt: e.transpose(out=ptb[:, qt * 128:(qt + 1) * 128], in_=ob, identity=identb),
                 reads=["ob", "identb"], writes=["bk1"])
            S.op("act", lambda e, qt=qt: e.activation(out=oTt[oti][:, qt * 128:(qt + 1) * 128], in_=ptb[:, qt * 128:(qt + 1) * 128], func=AF.Copy),
                 reads=["bk1"], writes=["oTt%d" % oti])
        S.dma("sp", lambda e: e.dma_start(out=oT[hd * 128:(hd + 1) * 128, 2 + qb * 256:2 + (qb + 1) * 256], in_=oTt[oti]),
              reads=["oTt%d" % oti])

    load_w(0)
    for hd in range(MH):
        if hd + 1 < MH:
            load_w(hd + 1)
        for t in range(ntile):
            proj_tile(hd, t)
        S.op("act", lambda e: e.activation(out=kmb, in_=kms, func=AF.Copy), reads=["kms"], writes=["kmb"])
        for qt in range(T // 128):
            gate_tile(hd, qt)
        for qb in range(nblk):
            attn_block(hd, qb)


def build_fused(T=4096, phases=(1, 2, 3, 4), dedup=True):
    nc = bass.Bass("TRN2", target_bir_lowering=False)
    nmacro = T // MT

    def ext(name, shape, dt=F32):
        return nc.dram_tensor(name, list(shape), dt, kind="ExternalInput").ap()
    io = dict(
        xT=ext("xT", [D, 2 + T]), ret_w=ext("ret_w", [RH, D, 1536]), gmix0=ext("gmix0", [128, DC]), gmix1=ext("gmix1", [128, DC]),
        cosT=ext("cosT", [128, T]), sinT=ext("sinT", [128, T]), dec=ext("dec", [128, RH * 128]), xi=ext("xi", [128, RH * 512]),
        zeta=ext("zeta", [128, RH]), cd=ext("cd", [128, RH]), gnrep=ext("gnrep", [128, RH * 512]),
        ret_w_out=ext("ret_w_out", [4096, D]), moba_w_out=ext("moba_w_out", [2048, D]),
        moba_w=ext("moba_w", [MH, D, 384]), bt=ext("bt", [MH, 128, 1024]), t31=ext("t31", [128, MH]), emat=ext("emat", [16, 2048], BF16),
        gf=ext("gf", [128, DC]))
    for l in range(2):
        io["w_up%d" % l] = ext("w_up%d" % l, [D, 2 * FF])
        io["w_down%d" % l] = ext("w_down%d" % l, [FF, D])
        io["gn%d" % l] = ext("gn%d" % l, [128, DC])
        io["cw%d" % l] = ext("cw%d" % l, [128, 2 * FC * 3])
        io["cb%d" % l] = ext("cb%d" % l, [128, 2 * FC])
    onesd, identbd, identfd = ext("ones", [128, 128]), ext("identb", [128, 128], BF16), ext("identf", [128, 128])
    TO = T // 2 if dedup else T
    out = nc.dram_tensor("oT", [D, TO], F32, kind="ExternalOutput").ap()
    scr = nc.dram_tensor("xn_scr", [D, T], BF16).ap()
    yT_s = nc.dram_tensor("yT_s", [4096, 2 + T], BF16).ap()
    h1T_s = nc.dram_tensor("h1T_s", [D, 2 + T], F32).ap()
    o1T_s = nc.dram_tensor("o1T_s", [2048, 2 + T], BF16).ap()
    hp_d = nc.dram_tensor("hp_s", [D, nmacro * 1026], F32).ap()
    o1h_s = nc.dram_tensor("o1h_s", [2048, 2 + T // 2], BF16).ap()
    h1h_s = nc.dram_tensor("h1h_s", [D, 2 + T // 2], F32).ap()
    half = nc.partition_id() % 2
    snaps = {}

    def snap_conds(e):
        snaps[0] = e.snap(half == 0)
        snaps[1] = e.snap(half == 1)
        return e.nop()

    S = Sched(nc)
    bank = [nc.alloc_psum_tensor("bank%d" % i, [128, 512], F32) for i in range(8)]
    xbank = bank[7][:, :].bitcast(BF16)
    AR = Arena(nc, 206 * 1024)
    ones_t = AR.alloc([128, 128], F32)
    identb = AR.alloc([128, 128], BF16)
    identf = AR.alloc([128, 128], F32)
    eps_t = AR.alloc([128, 1], F32)
    geps_t = AR.alloc([128, 1], F32)
    for (d_, s_, nm) in [(ones_t, onesd, "ones_t"), (identb, identbd, "identb"), (identf, identfd, "identf")]:
        S.dma("sp", lambda e, d_=d_, s_=s_: e.dma_start(out=d_, in_=s_), writes=[nm])
    S.op("dve", lambda e: e.memset(eps_t, 1e-6), writes=["eps_t"])
    S.op("dve", lambda e: e.memset(geps_t, 1e-5), writes=["geps_t"])
    io.update(ones_t=ones_t, identb=identb, identf=identf, eps_t=eps_t, geps_t=geps_t, scr=scr)
    base = AR.off
    KEEP = ["ones_t", "identb", "identf", "eps_t", "geps_t"]

    if 1 in phases:
        ioa = dict(io)
        ioa.update(xT=io["xT"][:, 2:2 + T], yT=yT_s)
        emit_ret(nc, S, AR, [bank[i] for i in range(7)] + [None], xbank, ioa, T)
        S.fence(keep=KEEP)
    AR.off = base
    if 2 in phases:
        iob = dict(yT=yT_s, hT=io["xT"], w_out=io["ret_w_out"], w_up=io["w_up0"], w_down=io["w_down0"], hp_d=hp_d,
                   dst=lambda dc, col, n: h1T_s[dc * 128:(dc + 1) * 128, 2 + col:2 + col + n],
                   gn=io["gn0"], gf=io["gf"], cw=io["cw0"], cb=io["cb0"], ones_t=ones_t, eps_t=eps_t)
        emit_ffn(nc, S, AR, bank, iob, T, 4096, False, "l0")
        S.fence(keep=KEEP)
    AR.off = base
    if 3 in phases:
        ioc = dict(io)
        ioc.update(xT=h1T_s[:, 2:2 + T], oT=o1T_s)
        emit_moba(nc, S, AR, bank, ioc, T)
        S.fence(keep=KEEP)
    AR.off = base
    if 4 in phases:
        iod = dict(yT=o1T_s, hT=h1T_s, w_out=io["moba_w_out"], w_up=io["w_up1"], w_down=io["w_down1"], hp_d=hp_d,
                   dst=lambda dc, col, n: out[dc * 128:(dc + 1) * 128, col:col + n],
                   gn=io["gn1"], gf=io["gf"], cw=io["cw1"], cb=io["cb1"], ones_t=ones_t, eps_t=eps_t)
        if dedup:
            zt = AR.alloc([128, DC, 2], F32)
            S.op("dve", lambda e: e.memset(zt, 0.0), writes=["zt"])
            S.dma("sp", lambda e: e.dma_start(out=h1T_s[:, 0:2].rearrange("(c p) t -> p c t", p=128), in_=zt), reads=["zt"])
            S.dma("sp", lambda e: e.dma_start(out=o1T_s[:, 0:2].rearrange("(c p) t -> p c t", p=128), in_=zt.bitcast(BF16)[:, :, 0:2]),
                  reads=["zt"])
            S.fence()
            S.op("sp", snap_conds)
            TH = T // 2
            for key, base in ((0, 0), (1, TH)):
                S.dma("sp", lambda e, key=key, base=base: e.dma_start(out=o1h_s, in_=o1T_s[:, base:base + 2 + TH], cond=snaps[key]),
                      writes=["o1h"])
                S.dma("sp", lambda e, key=key, base=base: e.dma_start(out=h1h_s, in_=h1T_s[:, base:base + 2 + TH], cond=snaps[key]),
                      writes=["h1h"])
            S.fence()
            iod.update(yT=o1h_s, hT=h1h_s)
            emit_ffn(nc, S, AR, bank, iod, TH, 2048, True, "l1", zero_first_halo=False)
        else:
            emit_ffn(nc, S, AR, bank, iod, T, 2048, True, "l1")
    S.emit()
    nc._sched_stats = (S.sem_max, S.dma_max, S.n_ops)
    return nc


def _lay_vec(v, nch):
    return np.ascontiguousarray(np.asarray(v, np.float32).reshape(nch, 128).T)


def t5_bucket_np(rel):
    n = np.maximum(rel, 0)
    max_exact = 16
    nf = np.maximum(n, max_exact).astype(np.float32)
    large = max_exact + (np.log(nf / max_exact) / math.log(128 / max_exact) * (32 - max_exact)).astype(np.int32)
    large = np.minimum(large, 31)
    return np.where(n < max_exact, n, large)


def fused_shared_inputs(T, mix_norm, ret_w_in, ret_gn, ret_w_out, moba_w_qkv, moba_w_out, rel_bias,
                        ffn_norm, ffn_w_up, ffn_conv_w, ffn_conv_b, ffn_w_down, final_norm):
    f32 = np.float32
    w_in = np.asarray(ret_w_in[0], f32)
    gn = np.asarray(ret_gn[0], f32)
    ret_w = np.ascontiguousarray(np.stack([np.concatenate(
        [w_in[:, h * 256:(h + 1) * 256], w_in[:, 2048 + h * 256:2048 + (h + 1) * 256],
         w_in[:, 4096 + h * 512:4096 + (h + 1) * 512], w_in[:, 8192 + h * 512:8192 + (h + 1) * 512]], axis=1) for h in range(RH)]))
    wq = np.asarray(moba_w_qkv[0], f32)
    moba_w = np.ascontiguousarray(np.stack([np.concatenate(
        [wq[:, h * 128:(h + 1) * 128], wq[:, 2048 + h * 128:2048 + (h + 1) * 128], wq[:, 4096 + h * 128:4096 + (h + 1) * 128]], axis=1)
        for h in range(MH)]))
    inv = 10000.0 ** (-np.arange(128, dtype=f32) / 128)
    ang = (np.arange(T, dtype=f32)[None, :] * inv[:, None]).astype(f32)
    dec = np.zeros((128, RH * 128), f32); xi = np.zeros((128, RH * 512), f32)
    zeta = np.zeros((128, RH), f32); cd = np.zeros((128, RH), f32)
    idx = np.arange(128, dtype=np.float64)
    for h in range(RH):
        lg = np.log1p(-2.0 ** (-5.0 - h))
        diff = idx[None, :] - idx[:, None]
        dec[:, h * 128:(h + 1) * 128] = np.where(diff >= 0, np.exp(lg * np.maximum(diff, 0.0)), 0.0) / 16.0
        xi[:, h * 512:(h + 1) * 512] = np.tile(np.exp(lg * (idx + 1.0)), 4)[None, :]
        zeta[:, h] = np.exp(lg * (127.0 - idx)) / 16.0
        cd[:, h] = np.exp(lg * 128.0)
    gnrep = np.ascontiguousarray(np.broadcast_to(gn[None, :], (128, RH * 512)))
    rb = np.asarray(rel_bias, f32)
    bt = np.zeros((MH, 128, 1024), f32); t31 = np.zeros((128, MH), f32)
    key = np.arange(256)[:, None]; q = np.arange(256)[None, :]
    b_own_idx = t5_bucket_np(q - key); b_adj_idx = t5_bucket_np(q + 256 - key)
    for h in range(MH):
        b_own = np.where(q - key >= 0, rb[b_own_idx, h], f32(NEG)).astype(f32)
        b_adj = rb[b_adj_idx, h].astype(f32)
        for kind, bm in ((0, b_own), (1, b_adj)):
            for kc2 in range(2):
                bt[h, :, (kind * 2 + kc2) * 256:(kind * 2 + kc2 + 1) * 256] = bm[kc2 * 128:(kc2 + 1) * 128, :]
        t31[:, h] = rb[31, h]
    emat = np.zeros((16, 2048), f32)
    for n in range(16):
        emat[n, n * 128:(n + 1) * 128] = 1.0
    d = dict(ret_w=ret_w, gmix0=_lay_vec(mix_norm[0], 16), gmix1=_lay_vec(mix_norm[1], 16),
             cosT=np.cos(ang).astype(f32), sinT=np.sin(ang).astype(f32), dec=dec, xi=xi, zeta=zeta, cd=cd, gnrep=gnrep,
             ret_w_out=np.ascontiguousarray(ret_w_out[0], dtype=f32), moba_w_out=np.ascontiguousarray(moba_w_out[0], dtype=f32),
             moba_w=moba_w, bt=bt, t31=t31, emat=emat.astype(ml_dtypes.bfloat16), gf=_lay_vec(final_norm, 16),
             ones=np.ones((128, 128), f32), identb=np.eye(128, dtype=f32).astype(ml_dtypes.bfloat16), identf=np.eye(128, dtype=f32))
    for l in range(2):
        cw = np.asarray(ffn_conv_w[l], f32)
        d["w_up%d" % l] = np.ascontiguousarray(ffn_w_up[l], dtype=f32)
        d["w_down%d" % l] = np.ascontiguousarray(ffn_w_down[l], dtype=f32)
        d["gn%d" % l] = _lay_vec(ffn_norm[l], 16)
        d["cw%d" % l] = np.ascontiguousarray(cw.T.reshape(88, 128, 3).transpose(1, 0, 2).reshape(128, 88 * 3))
        d["cb%d" % l] = _lay_vec(ffn_conv_b[l], 88)
    return d


def x_input(xb, T):
    xT = np.zeros((2048, 2 + T), np.float32)
    xT[:, 2:] = np.asarray(xb[:T], np.float32).T
    return xT

SEQ = 4096


def kernel(x, mix_norm, ret_w_in, ret_gn, ret_w_out, moba_w_qkv, moba_w_out, rel_bias,
           ffn_norm, ffn_w_up, ffn_conv_w, ffn_conv_b, ffn_w_down, final_norm):
    x = np.asarray(x, np.float32)
    shared = fused_shared_inputs(SEQ, mix_norm, ret_w_in, ret_gn, ret_w_out, moba_w_qkv, moba_w_out, rel_bias,
                                 ffn_norm, ffn_w_up, ffn_conv_w, ffn_conv_b, ffn_w_down, final_norm)
    nc = build_fused(T=SEQ)
    xin = [x_input(x[b], SEQ) for b in range(4)]
    in_maps = [dict(xT=xin[c // 2], **shared) for c in range(8)]
    res = run_bass_kernel_spmd(nc, in_maps, core_ids=list(range(8)))
    out = np.empty((4, SEQ, 2048), np.float32)
    for c in range(8):
        b, half = c // 2, c % 2
        out[b, half * (SEQ // 2):(half + 1) * (SEQ // 2)] = res.results[c]["oT"].T
    return out
```
